# Optimizing a Trainium2 kernel written in Bass

```python
import math
import jax, jax.numpy as jnp
from jax import lax
import numpy as np

D_MODEL = 1024
BATCH = 4
SEQ = 8192
DEPTH = 4
DEC_BATCH = 8
DEC_SEQ = 4096
PAST_LEN = 128

ROPE_THETA = 500000.0
LN_EPS = 1e-5
ALPHA = (2.0 * DEPTH) ** 0.25
BETA = (8.0 * DEPTH) ** -0.25

A_HD = 64
A_HEADS = D_MODEL // (2 * A_HD)
A_QK = 2 * A_HEADS * A_HD
A_V = A_HEADS * 2 * A_HD
A_QBLK = 128

B_HD = 64
B_HEADS = D_MODEL // B_HD
B_GROUPS = ((128, 1), (512, 4), (2048, 16))
B_W = B_HEADS * B_HD
B_BLK = 64

kernel_name = "hybrid_diffattn_dilated_deepnorm_encoder"


def layer_norm(x, g, b):
    xf = x.astype(jnp.float32)
    mu = jnp.mean(xf, axis=-1, keepdims=True)
    var = jnp.mean(jnp.square(xf - mu), axis=-1, keepdims=True)
    y = (xf - mu) * lax.rsqrt(var + LN_EPS) * g.astype(jnp.float32) + b.astype(jnp.float32)
    return y.astype(x.dtype)


def rope_partial(x, pos):
    hd = x.shape[-1]
    rd = hd // 4
    half = rd // 2
    inv = ROPE_THETA ** (-jnp.arange(half, dtype=jnp.float32) / half)
    ang = pos.astype(jnp.float32)[:, None] * inv[None, :]
    cos = jnp.cos(ang)[:, None, :]
    sin = jnp.sin(ang)[:, None, :]
    xf = x.astype(jnp.float32)
    x1, x2, rest = xf[..., :half], xf[..., half:rd], xf[..., rd:]
    out = jnp.concatenate([x1 * cos - x2 * sin, x2 * cos + x1 * sin, rest], axis=-1)
    return out.astype(x.dtype)


def diff_attention(q, k, v, lam):
    B, S, H, _, hd = q.shape
    nq = S // A_QBLK
    scale = 1.0 / math.sqrt(hd)
    qb = q.reshape(B, nq, A_QBLK, H, 2, hd).transpose(1, 0, 2, 3, 4, 5)

    def one_block(qblk):
        s = jnp.einsum('bqhcd,bkhcd->bhcqk', qblk, k).astype(jnp.float32) * scale
        p = jax.nn.softmax(s, axis=-1)
        a = p[:, :, 0] - lam * p[:, :, 1]
        return jnp.einsum('bhqk,bkhe->bqhe', a.astype(v.dtype), v)

    o = lax.map(one_block, qb)
    return o.transpose(1, 0, 2, 3, 4).reshape(B, S, H, v.shape[-1])


def mixer_a(x, w_in, lq1, lk1, lq2, lk2, subln_g, w_out, layer_idx):
    B, S, _ = x.shape
    h = x @ w_in
    q, k, v, g = jnp.split(h, [A_QK, 2 * A_QK, 2 * A_QK + A_V], axis=-1)
    pos = jnp.arange(S)
    q = rope_partial(q.reshape(B, S, 2 * A_HEADS, A_HD), pos).reshape(B, S, A_HEADS, 2, A_HD)
    k = rope_partial(k.reshape(B, S, 2 * A_HEADS, A_HD), pos).reshape(B, S, A_HEADS, 2, A_HD)
    v = v.reshape(B, S, A_HEADS, 2 * A_HD)
    lam_init = 0.8 - 0.6 * math.exp(-0.3 * layer_idx)
    lam = (jnp.exp(jnp.sum(lq1.astype(jnp.float32) * lk1.astype(jnp.float32)))
           - jnp.exp(jnp.sum(lq2.astype(jnp.float32) * lk2.astype(jnp.float32))) + lam_init)
    o = diff_attention(q, k, v, lam).astype(jnp.float32)
    o = o * lax.rsqrt(jnp.mean(jnp.square(o), axis=-1, keepdims=True) + LN_EPS)
    o = o * subln_g.astype(jnp.float32) * (1.0 - lam_init)
    y = o.reshape(B, S, A_V).astype(x.dtype) * jax.nn.silu(g)
    return y @ w_out


def dilated_attention(q, k, v, dil, radius):
    B, S, H, hd = q.shape
    L = S // dil
    nb = -(-L // B_BLK)
    Lp = nb * B_BLK
    scale = 1.0 / math.sqrt(hd)

    def to_res(t):
        return t.reshape(B, L, dil, H, hd).transpose(0, 2, 1, 3, 4)

    qr = jnp.pad(to_res(q), ((0, 0), (0, 0), (0, Lp - L), (0, 0), (0, 0)))
    qr = qr.reshape(B, dil, nb, B_BLK, H, hd)

    def windows(t):
        tp = jnp.pad(to_res(t), ((0, 0), (0, 0), (B_BLK, Lp - L + B_BLK), (0, 0), (0, 0)))
        tp = tp.reshape(B, dil, nb + 2, B_BLK, H, hd)
        return jnp.concatenate([tp[:, :, :-2], tp[:, :, 1:-1], tp[:, :, 2:]], axis=3)

    kw = windows(k)
    vw = windows(v)
    s = jnp.einsum('brnqhd,brnkhd->brnhqk', qr, kw).astype(jnp.float32) * scale
    qpos = jnp.arange(B_BLK)[:, None]
    t = jnp.arange(3 * B_BLK)[None, :]
    band = jnp.abs(t - B_BLK - qpos) <= radius
    kglob = jnp.arange(nb)[:, None] * B_BLK + jnp.arange(3 * B_BLK)[None, :] - B_BLK
    inrange = (kglob >= 0) & (kglob < L)
    mask = band[None, :, :] & inrange[:, None, :]
    s = jnp.where(mask[None, None, :, None, :, :], s, -1e30)
    lse = jax.nn.logsumexp(s, axis=-1)
    p = jnp.exp(s - lse[..., None])
    o = jnp.einsum('brnhqk,brnkhd->brnqhd', p.astype(v.dtype), vw)
    o = o.reshape(B, dil, Lp, H, hd)[:, :, :L].transpose(0, 2, 1, 3, 4).reshape(B, S, H, hd)
    lse = lse.transpose(0, 1, 2, 4, 3).reshape(B, dil, Lp, H)[:, :, :L]
    lse = lse.transpose(0, 2, 1, 3).reshape(B, S, H)
    return o, lse


def mixer_b(x, w_in, w_out):
    B, S, _ = x.shape
    h = x @ w_in
    parts = jnp.split(h, 3 * len(B_GROUPS) + 1, axis=-1)
    gate = parts[-1]
    pos = jnp.arange(S)
    outs, lses = [], []
    for gi, (win, dil) in enumerate(B_GROUPS):
        q, k, v = [p.reshape(B, S, B_HEADS, B_HD) for p in parts[3 * gi:3 * gi + 3]]
        q = rope_partial(q, pos)
        k = rope_partial(k, pos)
        o, lse = dilated_attention(q, k, v, dil, (win // 2) // dil)
        outs.append(o)
        lses.append(lse)
    wts = jax.nn.softmax(jnp.stack(lses, axis=0), axis=0)
    o = jnp.sum(wts[..., None] * jnp.stack(outs, axis=0).astype(jnp.float32), axis=0)
    y = o.reshape(B, S, B_W).astype(x.dtype) * jax.nn.silu(gate)
    return y @ w_out


def setup_inputs(seed: int = 0) -> dict:
    key = jax.random.key(seed)
    ks = iter(jax.random.split(key, 64))

    def nrm(shape, scale):
        return jax.random.normal(next(ks), shape, jnp.float32) * scale

    d = {}
    d["x_prompt"] = nrm((BATCH, SEQ, D_MODEL), 1.0)
    d["x_sample"] = nrm((DEC_BATCH, DEC_SEQ, D_MODEL), 1.0)
    for i in range(DEPTH):
        if i % 2 == 0:
            d[f"w_in_{i}"] = nrm((D_MODEL, 2 * A_QK + 2 * A_V), D_MODEL ** -0.5)
            d[f"lam_q1_{i}"] = nrm((A_HD,), 0.1)
            d[f"lam_k1_{i}"] = nrm((A_HD,), 0.1)
            d[f"lam_q2_{i}"] = nrm((A_HD,), 0.1)
            d[f"lam_k2_{i}"] = nrm((A_HD,), 0.1)
            d[f"subln_g_{i}"] = 1.0 + nrm((2 * A_HD,), 0.02)
            d[f"w_out_{i}"] = nrm((A_V, D_MODEL), BETA * A_V ** -0.5)
        else:
            d[f"w_in_{i}"] = nrm((D_MODEL, (3 * len(B_GROUPS) + 1) * B_W), D_MODEL ** -0.5)
            d[f"w_out_{i}"] = nrm((B_W, D_MODEL), BETA * B_W ** -0.5)
        d[f"ln_g_{i}"] = 1.0 + nrm((D_MODEL,), 0.02)
        d[f"ln_b_{i}"] = nrm((D_MODEL,), 0.02)
    return d


def reference(x_prompt, x_sample,
              w_in_0, lam_q1_0, lam_k1_0, lam_q2_0, lam_k2_0, subln_g_0, w_out_0, ln_g_0, ln_b_0,
              w_in_1, w_out_1, ln_g_1, ln_b_1,
              w_in_2, lam_q1_2, lam_k1_2, lam_q2_2, lam_k2_2, subln_g_2, w_out_2, ln_g_2, ln_b_2,
              w_in_3, w_out_3, ln_g_3, ln_b_3):
    layers = [
        ("A", (w_in_0, lam_q1_0, lam_k1_0, lam_q2_0, lam_k2_0, subln_g_0, w_out_0), ln_g_0, ln_b_0),
        ("B", (w_in_1, w_out_1), ln_g_1, ln_b_1),
        ("A", (w_in_2, lam_q1_2, lam_k1_2, lam_q2_2, lam_k2_2, subln_g_2, w_out_2), ln_g_2, ln_b_2),
        ("B", (w_in_3, w_out_3), ln_g_3, ln_b_3),
    ]

    def trunk(x):
        for i in range(DEPTH):
            kind, params, g, b = layers[i]
            if kind == "A":
                f = mixer_a(x, *params, layer_idx=i)
            else:
                f = mixer_b(x, *params)
            x = layer_norm(ALPHA * x + f, g, b)
        return x

    y_prompt = trunk(x_prompt)
    y_sample = trunk(x_sample)
    return (y_prompt, y_sample)
```

```python
import math
from contextlib import ExitStack

import numpy as np
import ml_dtypes
import concourse.bass as bass
import concourse.mybir as mybir
from concourse.bass_utils import run_bass_kernel_spmd

F32 = mybir.dt.float32
BF16 = mybir.dt.bfloat16
AF = mybir.ActivationFunctionType
ALU = mybir.AluOpType

T_TOK = 8192
D = 1024
NTB = T_TOK // 128
NSB = T_TOK // 512
DEPTH = 4
ALPHA = (2.0 * DEPTH) ** 0.25
EPS = 1e-5
THETA = 500000.0
NEG = -30000.0
GROUPS = ((128, 1), (512, 4), (2048, 16))

import os as _os
SMALL = bool(_os.environ.get("KSMALL"))
CFG = dict(p0_tbs=range(NTB), p1_sbs=range(NSB), p2_heads=range(8), p2_qts=range(16), p2_kbs=list(range(NTB)), p3_sbs=range(NSB))
if SMALL:
    CFG = dict(p0_tbs=range(8), p1_sbs=range(2), p2_heads=range(8), p2_qts=[0, 1], p2_kbs=list(range(8)), p3_sbs=range(2))
    if _os.environ.get("KDEBUG") == "b":
        CFG.update(p0_tbs=range(16), p3_sbs=range(4))


class Ev:
    __slots__ = ("sem", "val")

    def __init__(self, sem, val):
        self.sem = sem
        self.val = val


class Buf:
    def __init__(self, name):
        self.name = name
        self.w = None
        self.r = {}
        self.dsem = None


class Sem:
    def __init__(self, h, name):
        self.h = h
        self.name = name
        self.cnt = 0


class Eng:
    def __init__(self, trk, name, eng, relaxed=False):
        self.name = name
        self.eng = eng
        self.sem = trk.new_sem("e_" + name)
        self.waited = {}
        self.pending = []
        self.relaxed = relaxed

    def wait(self, ev):
        assert ev.val is not None, "waiting on unresolved event"
        if self.waited.get(ev.sem, 0) >= ev.val:
            return
        self.eng.wait_ge(ev.sem.h, ev.val)
        self.waited[ev.sem] = ev.val


class Tracker:
    def __init__(self, nc, es):
        self.nc = nc
        self.es = es
        self.sems = []
        self.dsem_pool = []
        self.engs = []

    def new_sem(self, name):
        s = Sem(self.es.enter_context(self.nc.semaphore(name)), name)
        self.sems.append(s)
        return s

    def add_eng(self, name, eng, relaxed=False):
        e = Eng(self, name, eng, relaxed)
        self.engs.append(e)
        return e

    def _deps(self, E, reads, writes):
        for b in reads:
            if b.w is not None:
                self._wait(E, b.w, raw=True)
        for b in writes:
            if b.w is not None:
                self._wait(E, b.w, raw=True)
            for ev in b.r.values():
                self._wait(E, ev, raw=False)

    def _wait(self, E, ev, raw):
        if ev.sem is E.sem:
            if E.relaxed or ev.val is None:
                return
        E.wait(ev)

    def _record(self, ev, reads, writes):
        for b in reads:
            b.r[ev.sem] = ev
        for b in writes:
            b.w = ev
            b.r = {}

    def op(self, E, fn, reads=(), writes=(), inc=True):
        self._deps(E, reads, writes)
        ins = fn()
        if inc:
            E.sem.cnt += 1
            ins.then_inc(E.sem.h, 1)
            ev = Ev(E.sem, E.sem.cnt)
            for (pev, pr, pw) in E.pending:
                pev.val = ev.val
            E.pending = []
            self._record(ev, reads, writes)
        else:
            ev = Ev(E.sem, None)
            E.pending.append((ev, reads, writes))
            self._record(ev, reads, writes)
        return ins

    def get_dsem(self, b):
        if b.dsem is None:
            if self.dsem_pool:
                b.dsem = self.dsem_pool.pop()
            else:
                b.dsem = self.new_sem("d%d" % len(self.sems))
        return b.dsem

    def dma(self, Q, out, in_, reads=(), writes=(), owner=None, **kw):
        self._deps(Q, reads, writes)
        ds = self.get_dsem(owner)
        ins = Q.eng.dma_start(out=out, in_=in_, **kw)
        ds.cnt += 16
        ins.then_inc(ds.h, 16)
        ev = Ev(ds, ds.cnt)
        self._record(ev, reads, writes)
        return ins

    def release(self, bufs):
        for b in bufs:
            if b.dsem is not None:
                self.dsem_pool.append(b.dsem)
                b.dsem = None

    def barrier(self):
        for E in self.engs:
            assert not E.pending, E.name
        for E in self.engs:
            for X in self.engs:
                if X is not E and X.sem.cnt > 0:
                    E.wait(Ev(X.sem, X.sem.cnt))
            for s in self.sems:
                if s.name.startswith("d") and s.cnt > 0:
                    E.wait(Ev(s, s.cnt))


class Prog:
    def __init__(self, NL=4):
        self.NL = NL
        nc = self.nc = bass.Bass("TRN2", target_bir_lowering=False)
        self.es = ExitStack()
        es = self.es
        dt = nc.dram_tensor
        self.x_in = dt("x", [T_TOK, D], F32, kind="ExternalInput").ap()
        self.y_out = dt("y", [T_TOK, D], F32, kind="ExternalOutput").ap()
        self.w_in = []
        self.w_out = []
        self.ln_g = []
        self.ln_b = []
        self.lamv = {}
        self.subg = {}
        self.used_layers = range(DEPTH)
        if _os.environ.get("KDEBUG") in ("p0", "p1a", "p2a", "p3a"):
            self.used_layers = range(1)
        elif _os.environ.get("KDEBUG") == "b":
            self.used_layers = range(2)
        for i in self.used_layers:
            ncol = 4096 if i % 2 == 0 else 10240
            self.w_in.append(dt(f"w_in_{i}", [D, ncol], F32, kind="ExternalInput").ap())
            self.w_out.append(dt(f"w_out_{i}", [D, D], F32, kind="ExternalInput").ap())
            self.ln_g.append(dt(f"ln_g_{i}", [D], F32, kind="ExternalInput").ap())
            self.ln_b.append(dt(f"ln_b_{i}", [D], F32, kind="ExternalInput").ap())
            if i % 2 == 0:
                self.lamv[i] = dt(f"lamv_{i}", [4, 64], F32, kind="ExternalInput").ap()
                self.subg[i] = dt(f"subln_g_{i}", [128], F32, kind="ExternalInput").ap()
        self.rope_d = dt("rope", [3, 128, NTB * 32], F32, kind="ExternalInput").ap()
        self.cm_d = dt("cm", [128, 1], F32, kind="ExternalInput").ap()
        self.mb_d = dt("mb", [128, 3 * 256], BF16, kind="ExternalInput").ap()
        self.ident_d = dt("ident", [128, 128], BF16, kind="ExternalInput").ap()
        self.xs = [dt(f"xs{i}", [T_TOK, D], F32).ap() for i in range(2)]
        self.xT = dt("xT", [D, T_TOK], BF16).ap()
        self.gT = dt("gT", [D, T_TOK], BF16).ap()
        self.qT = dt("qT", [3, 8, 128, T_TOK], BF16).ap()
        self.kT = dt("kT", [3, 8, 128, T_TOK], BF16).ap()
        self.vaA = dt("vaA", [8, 128, NTB * 129], BF16).ap()
        self.vaB = dt("vaB", [3, 8, 128, NTB * 130], BF16).ap()
        self.yT = dt("yT", [8, 128, T_TOK], BF16).ap()
        self.OB = dt("OB", [3, T_TOK, 16 * 65], F32).ap()

        trk = self.trk = Tracker(nc, es)
        self.PE = trk.add_eng("pe", nc.tensor, relaxed=True)
        rlx = bool(_os.environ.get("KRELAX"))
        self.ACT = trk.add_eng("act", nc.scalar, relaxed=rlx)
        self.DVE = trk.add_eng("dve", nc.vector, relaxed=rlx)
        self.POOL = trk.add_eng("pool", nc.gpsimd, relaxed=rlx)
        self.SP = trk.add_eng("sp", nc.sync)
        self.psall = es.enter_context(nc.psum_tensor("psall", [128, 4096], F32))
        self.ps = [self.psall[:, i * 512:(i + 1) * 512] for i in range(8)]
        self.ident = es.enter_context(nc.sbuf_tensor("ident_sb", [128, 128], BF16))
        self.cm = es.enter_context(nc.sbuf_tensor("cm_sb", [128, 1], F32))
        self.mb = es.enter_context(nc.sbuf_tensor("mb_sb", [128, 3 * 256], BF16))
        self.b_const = Buf("const")
        trk.dma(self.SP, self.ident[:], self.ident_d, writes=[self.b_const], owner=self.b_const)
        trk.dma(self.SP, self.cm[:], self.cm_d, writes=[self.b_const], owner=self.b_const)
        trk.dma(self.SP, self.mb[:], self.mb_d, writes=[self.b_const], owner=self.b_const)
        trk.barrier()

    def sb(self, st, name, shape, dtype):
        self._uid = getattr(self, "_uid", 0) + 1
        return st.enter_context(self.nc.sbuf_tensor(f"{name}_u{self._uid}", shape, dtype))

    def emit_xT_block(self, tb, src_ap, src_buf, xbf, b_xbf, tp, b_tp, xTst, b_xTst, cast_eng):
        trk, nc = self.trk, self.nc
        sbi, tb4 = divmod(tb, 4)
        k = tb % 2
        if cast_eng is self.ACT:
            trk.op(self.ACT, lambda: nc.scalar.copy(out=xbf[k][:], in_=src_ap), reads=[src_buf], writes=[b_xbf[k]])
        else:
            trk.op(cast_eng, lambda: cast_eng.eng.tensor_copy(out=xbf[k][:], in_=src_ap), reads=[src_buf], writes=[b_xbf[k]])
        tpv = tp[k]
        for c in range(8):
            trk.op(self.PE, lambda c=c: nc.tensor.transpose(out=tpv[:, c * 128:(c + 1) * 128], in_=xbf[k][:, c * 128:(c + 1) * 128], identity=self.ident[:]),
                   reads=[b_xbf[k], self.b_const], writes=[b_tp[k]], inc=(c == 7))
        s = sbi % 2
        trk.op(self.DVE, lambda: nc.vector.tensor_copy(out=xTst[s][:, :, tb4 * 128:(tb4 + 1) * 128],
                                                       in_=tpv.rearrange("p (c t) -> p c t", c=8)),
               reads=[b_tp[k]], writes=[b_xTst[s]])
        if tb4 == 3:
            trk.dma(self.SP, self.xT.rearrange("(c p) t -> p c t", p=128)[:, :, sbi * 512:(sbi + 1) * 512], xTst[s][:],
                    reads=[b_xTst[s]], owner=b_xTst[s])

    def tp_views(self, banks):
        return [self.ps[b][:].bitcast(BF16) for b in banks]

    def phase_p0(self):
        trk, nc = self.trk, self.nc
        with ExitStack() as st:
            xin = [self.sb(st, f"p0x{i}", [128, D], F32) for i in range(2)]
            b_xin = [Buf(f"xin{i}") for i in range(2)]
            xbf = [self.sb(st, f"p0xb{i}", [128, D], BF16) for i in range(2)]
            b_xbf = [Buf(f"xbf{i}") for i in range(2)]
            xTst = [self.sb(st, f"p0st{i}", [128, 8, 512], BF16) for i in range(2)]
            b_xTst = [Buf(f"xTst{i}") for i in range(2)]
            tp = self.tp_views([6, 7])
            b_tp = [Buf("tp0"), Buf("tp1")]
            for tb in CFG['p0_tbs']:
                k = tb % 2
                trk.dma(self.SP, xin[k][:], self.x_in[tb * 128:(tb + 1) * 128, :], writes=[b_xin[k]], owner=b_xin[k])
                self.emit_xT_block(tb, xin[k][:], b_xin[k], xbf, b_xbf, tp, b_tp, xTst, b_xTst, self.POOL)
            trk.barrier()
            trk.release(b_xin + b_xTst)

    def load_w_bf16(self, dst, w_ap, col0, ncols, b_w, st, tag=""):
        trk, nc = self.trk, self.nc
        step = 1024
        if not hasattr(st, "_wstg_t"):
            st._wstg_t = [self.sb(st, f"wstg{i}", [128, step], F32) for i in range(2)]
            st._wstg_b = [Buf(f"wstg{i}") for i in range(2)]
        stg = st._wstg_t
        b_stg = st._wstg_b
        i = 0
        for c in range(8):
            for c0 in range(0, ncols, step):
                n = min(step, ncols - c0)
                k = i % 2
                i += 1
                trk.dma(self.SP, stg[k][:, 0:n], w_ap[c * 128:(c + 1) * 128, col0 + c0:col0 + c0 + n],
                        writes=[b_stg[k]], owner=b_stg[k])
                trk.op(self.POOL, lambda k=k, n=n, c=c, c0=c0: nc.gpsimd.tensor_copy(out=dst[:, c, c0:c0 + n], in_=stg[k][:, 0:n]),
                       reads=[b_stg[k]], writes=[b_w])
        self._wstg = b_stg

    def emit_rope(self, U, b_U, qkb, b_qkb, rtab, b_rt, rb, tmpA, tmpB, b_tmp):
        trk, nc = self.trk, self.nc
        if "r" in _os.environ.get("KSKIP", ""):
            return
        Uv = U.rearrange("p (h e) -> p h e", e=64)
        x16 = Uv[:, :, 0:16]
        cc = rtab[:, rb * 32:rb * 32 + 16]
        ns = rtab[:, rb * 32 + 16:rb * 32 + 24]
        ps_ = rtab[:, rb * 32 + 24:rb * 32 + 32]
        ccb = bass.AP(cc.tensor, cc.offset, [list(cc.ap[0]), [0, 16], [1, 16]])
        nsb = bass.AP(ns.tensor, ns.offset, [list(ns.ap[0]), [0, 16], [1, 8]])
        psb = bass.AP(ps_.tensor, ps_.offset, [list(ps_.ap[0]), [0, 16], [1, 8]])
        tA = tmpA[:].rearrange("p (h e) -> p h e", e=16)
        tB = tmpB[:].rearrange("p (h e) -> p h e", e=16)
        mode = _os.environ.get("KROPE", "1")
        if mode == "1":
            trk.op(self.DVE, lambda: nc.vector.tensor_copy(out=tA, in_=x16), writes=[b_U, b_tmp[0]])
            trk.op(self.DVE, lambda: nc.vector.tensor_tensor(out=tB[:, :, 0:8], in0=tA[:, :, 8:16], in1=nsb, op=ALU.mult), reads=[b_tmp[0], b_rt], writes=[b_tmp[1]])
            trk.op(self.DVE, lambda: nc.vector.tensor_tensor(out=tB[:, :, 8:16], in0=tA[:, :, 0:8], in1=psb, op=ALU.mult), reads=[b_tmp[0], b_rt], writes=[b_tmp[1]])
            trk.op(self.DVE, lambda: nc.vector.tensor_tensor(out=tA, in0=tA, in1=ccb, op=ALU.mult), reads=[b_tmp[0], b_rt], writes=[b_tmp[0]])
        elif mode == "3":
            trk.op(self.DVE, lambda: nc.vector.tensor_copy(out=tA, in_=x16), writes=[b_U, b_tmp[0]])
            return
        elif mode == "4":
            trk.op(self.DVE, lambda: nc.vector.memset(tmpA[:], 0.0), writes=[b_tmp[0]])
            trk.op(self.DVE, lambda: nc.vector.memset(tmpB[:], 0.0), writes=[b_tmp[1]])
        else:
            trk.op(self.DVE, lambda: nc.vector.tensor_copy(out=tA, in_=x16), writes=[b_U, b_tmp[0]])
            trk.op(self.DVE, lambda: nc.vector.tensor_copy(out=tB, in_=x16), writes=[b_U, b_tmp[1]])
        qv = qkb.rearrange("p (h e) -> p h e", e=64)[:, :, 0:16]
        trk.op(self.DVE, lambda: nc.vector.tensor_tensor(out=qv, in0=tA, in1=tB, op=ALU.add), reads=[b_tmp[0], b_tmp[1]], writes=[b_qkb])

    def phase_p1a(self, l):
        trk, nc = self.trk, self.nc
        PE, ACT, DVE, POOL, SP = self.PE, self.ACT, self.DVE, self.POOL, self.SP
        with ExitStack() as st:
            wbf = self.sb(st, "wbf", [128, 8, 4096], BF16)
            b_w = Buf("w")
            self.load_w_bf16(wbf, self.w_in[l], 0, 4096, b_w, st)
            rtab = self.sb(st, "rtab", [128, NTB * 32], F32)
            b_rt = Buf("rt")
            trk.dma(SP, rtab[:], self.rope_d[0], writes=[b_rt], owner=b_rt)
            xTs = [self.sb(st, f"xTs{i}", [128, 8, 512], BF16) for i in range(2)]
            b_xTs = [Buf(f"xTs{i}") for i in range(2)]
            gst = [self.sb(st, f"gst{i}", [128, 8, 512], BF16) for i in range(2)]
            b_gst = [Buf(f"gst{i}") for i in range(2)]
            qst = [self.sb(st, f"qst{i}", [128, 8, 512], BF16) for i in range(2)]
            b_qst = [Buf(f"qst{i}") for i in range(2)]
            kst = [self.sb(st, f"kst{i}", [128, 8, 512], BF16) for i in range(2)]
            b_kst = [Buf(f"kst{i}") for i in range(2)]
            vst = [self.sb(st, f"vst{i}", [128, 8, 4 * 129], BF16) for i in range(2)]
            b_vst = [Buf(f"vst{i}") for i in range(2)]
            qkb = [self.sb(st, f"qkb{i}", [128, 1024], BF16) for i in range(2)]
            b_qkb = [Buf(f"qkb{i}") for i in range(2)]
            tmpA = self.sb(st, "rtA", [128, 256], F32)
            tmpB = self.sb(st, "rtB", [128, 256], F32)
            b_tmp = [Buf("rtA"), Buf("rtB")]
            for i in range(2):
                trk.op(POOL, lambda i=i: nc.gpsimd.memset(vst[i][:], 1.0), writes=[b_vst[i]])
            Ub = [(0, 1), (2, 3)]
            b_U = [Buf("U0"), Buf("U1")]
            b_G = [Buf("G0"), Buf("G1")]
            tp = self.tp_views([6, 7])
            b_tp = [Buf("tp0"), Buf("tp1")]
            ucnt = 0
            qkcnt = 0
            gcnt = 0
            pend_tp = []
            def load_x(sbi):
                trk.dma(SP, xTs[sbi % 2][:], self.xT.rearrange("(c p) t -> p c t", p=128)[:, :, sbi * 512:(sbi + 1) * 512],
                        writes=[b_xTs[sbi % 2]], owner=b_xTs[sbi % 2])
            p1sbs = list(CFG['p1_sbs'])
            load_x(p1sbs[0])
            for si_, sbi in enumerate(p1sbs):
                s = sbi % 2
                if si_ + 1 < len(p1sbs):
                    load_x(p1sbs[si_ + 1])
                for cb in range(8):
                    g = gcnt % 2
                    gcnt += 1
                    G = self.ps[4 + g]
                    for c in range(8):
                        trk.op(PE, lambda c=c: nc.tensor.matmul(G[:], lhsT=wbf[:, c, 3072 + cb * 128:3072 + (cb + 1) * 128],
                                                                rhs=xTs[s][:, c, :], start=(c == 0), stop=(c == 7)),
                               reads=[b_w, b_xTs[s]], writes=[b_G[g]], inc=(c == 7))
                    trk.op(ACT, lambda: nc.scalar.activation(out=gst[s][:, cb, :], in_=G[:], func=AF.Silu),
                           reads=[b_G[g]], writes=[b_gst[s]])
                trk.dma(SP, self.gT.rearrange("(c p) t -> p c t", p=128)[:, :, sbi * 512:(sbi + 1) * 512], gst[s][:],
                        reads=[b_gst[s]], owner=b_gst[s])
                for tb4 in range(4):
                    tb = sbi * 4 + tb4
                    for ui in range(3):
                        u = ucnt % 2
                        ucnt += 1
                        b0, b1 = Ub[u]
                        for n in range(2):
                            for c in range(8):
                                trk.op(PE, lambda n=n, c=c: nc.tensor.matmul(
                                    self.ps[(b0, b1)[n]][:], lhsT=xTs[s][:, c, tb4 * 128:(tb4 + 1) * 128],
                                    rhs=wbf[:, c, ui * 1024 + n * 512: ui * 1024 + (n + 1) * 512], start=(c == 0), stop=(c == 7)),
                                    reads=[b_w, b_xTs[s]], writes=[b_U[u]], inc=(n == 1 and c == 7))
                        Uap = self.pair_ap(b0)
                        while pend_tp:
                            pend_tp.pop(0)()
                        if ui < 2:
                            k = qkcnt % 2
                            qkcnt += 1
                            trk.op(ACT, lambda: nc.scalar.copy(out=qkb[k][:], in_=Uap), writes=[b_U[u], b_qkb[k]])
                            self.emit_rope(Uap, b_U[u], qkb[k][:], b_qkb[k], rtab, b_rt, tb, tmpA, tmpB, b_tmp)
                            def _tp(k=k, ui=ui, s=s, tb4=tb4):
                                tpv = tp[k]
                                for c in range(8):
                                    trk.op(PE, lambda c=c: nc.tensor.transpose(out=tpv[:, c * 128:(c + 1) * 128], in_=qkb[k][:, c * 128:(c + 1) * 128], identity=self.ident[:]),
                                           reads=[b_qkb[k], self.b_const], writes=[b_tp[k]], inc=(c == 7))
                                stg, b_stg = (qst, b_qst) if ui == 0 else (kst, b_kst)
                                trk.op(DVE, lambda: nc.vector.tensor_copy(out=stg[s][:, :, tb4 * 128:(tb4 + 1) * 128],
                                                                          in_=tpv.rearrange("p (c t) -> p c t", c=8)),
                                       reads=[b_tp[k]], writes=[b_stg[s]])
                            pend_tp.append(_tp)
                        else:
                            vv = vst[s][:].rearrange("p h (t e) -> p h t e", e=129)[:, :, tb4, 0:128]
                            trk.op(ACT, lambda: nc.scalar.copy(out=vv, in_=Uap.rearrange("p (h e) -> p h e", e=128)),
                                   reads=[b_U[u]], writes=[b_vst[s]])
                while pend_tp:
                    pend_tp.pop(0)()
                sl = slice(sbi * 512, (sbi + 1) * 512)
                trk.dma(SP, self.qT[0].rearrange("h p t -> p h t")[:, :, sl], qst[s][:], reads=[b_qst[s]], owner=b_qst[s])
                trk.dma(SP, self.kT[0].rearrange("h p t -> p h t")[:, :, sl], kst[s][:], reads=[b_kst[s]], owner=b_kst[s])
                trk.dma(SP, self.vaA.rearrange("h p x -> p h x")[:, :, sbi * 516:(sbi + 1) * 516], vst[s][:], reads=[b_vst[s]], owner=b_vst[s])
            trk.barrier()
            trk.release(self._wstg + [b_rt] + b_xTs + b_gst + b_qst + b_kst + b_vst)

    def pair_ap(self, b0):
        return self.ps_all_ap(b0, 2)

    def phase_p2a(self, l):
        trk, nc = self.trk, self.nc
        PE, ACT, DVE, POOL, SP = self.PE, self.ACT, self.DVE, self.POOL, self.SP
        lam_init = 0.8 - 0.6 * math.exp(-0.3 * l)
        NK = NTB
        with ExitStack() as st:
            kTh = [self.sb(st, f"kTh{i}", [128, T_TOK], BF16) for i in range(2)]
            qTh = [self.sb(st, f"qTh{i}", [128, T_TOK], BF16) for i in range(2)]
            vah = [self.sb(st, f"vah{i}", [128, NK * 129], BF16) for i in range(2)]
            vax = [self.sb(st, f"vax{i}", [128, NK * 129], BF16) for i in range(2)]
            b_kTh = [Buf(f"kTh{i}") for i in range(2)]
            b_qTh = [Buf(f"qTh{i}") for i in range(2)]
            b_vah = [Buf(f"vah{i}") for i in range(2)]
            b_vax = [Buf(f"vax{i}") for i in range(2)]
            ptt = [self.sb(st, f"ptt{b}", [128, 1024], BF16) for b in range(3)]
            pt = [[ptt[b][:, c * 512:(c + 1) * 512] for b in range(3)] for c in range(2)]
            b_pt = [Buf(f"pt{b}") for b in range(3)]
            osb = self.sb(st, "osb", [128, 8 * 129], F32)
            b_osb = Buf("osb")
            dt_ = self.sb(st, "dtmp", [128, 512], F32)
            b_dt = Buf("dtmp")
            junk = self.sb(st, "junk", [128, 128], F32)
            b_junk = Buf("junk")
            sm = self.sb(st, "sm", [128, 32], F32)
            b_rl, b_ssq, b_ln, b_rstd = Buf("rl"), Buf("ssq"), Buf("ln"), Buf("rstd")
            ybf = self.sb(st, "ybf", [128, 512], BF16)
            b_ybf = Buf("ybf")
            yst = [self.sb(st, f"yst{i}", [128, 512], BF16) for i in range(2)]
            b_yst = [Buf(f"yst{i}") for i in range(2)]
            lamt = self.sb(st, "lamt", [128, 4 * 64], F32)
            lams = self.sb(st, "lams", [128, 8], F32)
            b_lam = Buf("lam")
            b_lams = Buf("lams")
            epsb = self.sb(st, "epsb", [128, 1], F32)
            b_eps = Buf("eps")
            trk.op(POOL, lambda: nc.gpsimd.memset(epsb[:], EPS), writes=[b_eps])
            trk.dma(SP, lamt[:], self.lamv[l].rearrange("a b -> (a b)").partition_broadcast(128), writes=[b_lam], owner=b_lam)
            lt = lamt[:].rearrange("p (a b) -> p a b", b=64)
            trk.op(DVE, lambda: nc.vector.tensor_tensor(out=lt[:, 0, :], in0=lt[:, 0, :], in1=lt[:, 1, :], op=ALU.mult), reads=[b_lam], writes=[b_lam])
            trk.op(DVE, lambda: nc.vector.tensor_tensor(out=lt[:, 2, :], in0=lt[:, 2, :], in1=lt[:, 3, :], op=ALU.mult), reads=[b_lam], writes=[b_lam])
            trk.op(DVE, lambda: nc.vector.tensor_reduce(out=lams[:, 0:1], in_=lt[:, 0, :], axis=mybir.AxisListType.X, op=ALU.add), reads=[b_lam], writes=[b_lams])
            trk.op(DVE, lambda: nc.vector.tensor_reduce(out=lams[:, 1:2], in_=lt[:, 2, :], axis=mybir.AxisListType.X, op=ALU.add), reads=[b_lam], writes=[b_lams])
            trk.op(ACT, lambda: nc.scalar.activation(out=lams[:, 2:4], in_=lams[:, 0:2], func=AF.Exp), reads=[b_lams], writes=[b_lams])
            trk.op(DVE, lambda: nc.vector.scalar_tensor_tensor(out=lams[:, 4:5], in0=lams[:, 3:4], scalar=-lam_init, in1=lams[:, 2:3],
                                                               op0=ALU.add, op1=ALU.subtract), reads=[b_lams], writes=[b_lams])
            nlam = lams[:, 4:5]

            S_b = [[0, 2], [1, 3]]
            b_S = [Buf("S0"), Buf("S1")]
            O_b = [4, 5, 6]
            b_O = Buf("O")
            tpv = self.ps[7][:].bitcast(BF16)
            b_tpy = Buf("tpy")

            def acc_ap(c, j):
                a = c * 4 + j
                return self.ps[O_b[a // 3]][:, (a % 3) * 129:(a % 3) * 129 + 129]

            def load_head(h):
                s = h % 2
                TL = 1024 if SMALL else T_TOK
                VL = TL // 128 * 129
                trk.dma(SP, kTh[s][:, :TL], self.kT[0, h][:, :TL], writes=[b_kTh[s]], owner=b_kTh[s])
                trk.dma(SP, vah[s][:, :VL], self.vaA[h][:, :VL], writes=[b_vah[s]], owner=b_vah[s])
                trk.dma(SP, qTh[s][:, :TL], self.qT[0, h][:, :TL], writes=[b_qTh[s]], owner=b_qTh[s])
                trk.op(POOL, lambda: nc.gpsimd.tensor_scalar(out=vax[s][:, :VL], in0=vah[s][:, :VL], scalar1=self.cm[:, 0:1], scalar2=1.0, op0=ALU.mult, op1=ALU.mult),
                       reads=[b_vah[s], self.b_const], writes=[b_vax[s]])

            tcount = [0]
            ycount = [0]

            def qk(h, qt, kb):
                s = h % 2
                t = tcount[0]
                sl = t % 2
                for c in range(2):
                    trk.op(PE, lambda c=c: nc.tensor.matmul(self.ps[S_b[c][sl]][:], lhsT=kTh[s][c * 64:(c + 1) * 64, kb * 128:(kb + 1) * 128],
                                                            rhs=qTh[s][c * 64:(c + 1) * 64, qt * 512:(qt + 1) * 512], start=True, stop=True),
                           reads=[b_kTh[s], b_qTh[s]], writes=[b_S[sl]], inc=(c == 1))
                p3 = t % 3
                trk.op(ACT, lambda: nc.scalar.activation(out=ptt[p3][:], in_=self.psall[:, 2 * sl * 512:(2 * sl + 2) * 512], func=AF.Exp, scale=0.125),
                       reads=[b_S[sl]], writes=[b_pt[p3]])
                tcount[0] += 1
                return p3

            def av(h, qt, kb, p3):
                s = h % 2
                qhalf = (qt * 512) // 4096
                khalf = (kb * 128) // 4096
                vsrc, b_vsrc = (vah[s], b_vah[s]) if qhalf == khalf else (vax[s], b_vax[s])
                for c in range(2):
                    for j in range(4):
                        a = c * 4 + j
                        trk.op(PE, lambda c=c, j=j, a=a: nc.tensor.matmul(acc_ap(c, j), lhsT=pt[c][p3][:, j * 128:(j + 1) * 128],
                                                                          rhs=vsrc[:, kb * 129:(kb + 1) * 129],
                                                                          start=(kb == KBS[0] and a % 3 == 0), stop=(kb == KBS[-1]), skip_group_check=True),
                               reads=[b_pt[p3], b_vsrc], writes=[b_O], inc=(a == 7))

            def epilogue_stages(h, qt):
                def s0():
                    for b in range(3):
                        n = 387 if b < 2 else 258
                        trk.op(DVE, lambda b=b, n=n: nc.vector.tensor_copy(out=osb[:, b * 387:b * 387 + n], in_=self.ps[O_b[b]][:, 0:n]),
                               reads=[b_O], writes=[b_osb])
                ov = osb[:].rearrange("p (a e) -> p a e", e=129)

                def s1():
                    trk.op(DVE, lambda: nc.vector.reciprocal(out=sm[:, 0:8], in_=ov[:, :, 128]), reads=[b_osb], writes=[b_rl])
                    trk.op(DVE, lambda: nc.vector.tensor_scalar(out=sm[:, 8:12], in0=sm[:, 4:8], scalar1=nlam, scalar2=None, op0=ALU.mult),
                           reads=[b_rl, b_lams], writes=[b_rl])
                    for j in range(4):
                        dj = dt_[:, j * 128:(j + 1) * 128]
                        trk.op(DVE, lambda j=j, dj=dj: nc.vector.tensor_scalar(out=dj, in0=ov[:, j, 0:128], scalar1=sm[:, j:j + 1], scalar2=None, op0=ALU.mult),
                               reads=[b_osb, b_rl], writes=[b_dt])
                        trk.op(DVE, lambda j=j, dj=dj: nc.vector.scalar_tensor_tensor(out=dj, in0=ov[:, 4 + j, 0:128], scalar=sm[:, 8 + j:9 + j], in1=dj,
                                                                                      op0=ALU.mult, op1=ALU.add),
                               reads=[b_osb, b_rl, b_dt], writes=[b_dt])
                        trk.op(DVE, lambda j=j, dj=dj: nc.vector.scalar_tensor_tensor(out=junk[:], in0=dj, scalar=1.0, in1=dj, op0=ALU.mult, op1=ALU.mult,
                                                                                      accum_out=sm[:, 12 + j:13 + j]),
                               reads=[b_dt], writes=[b_junk, b_ssq])

                def s2():
                    trk.op(ACT, lambda: nc.scalar.activation(out=sm[:, 16:20], in_=sm[:, 12:16], func=AF.Ln, scale=1.0 / 128.0, bias=epsb[:, 0:1]),
                           reads=[b_ssq, b_eps], writes=[b_ln])
                    trk.op(ACT, lambda: nc.scalar.activation(out=sm[:, 20:24], in_=sm[:, 16:20], func=AF.Exp, scale=-0.5),
                           reads=[b_ln], writes=[b_rstd])

                def s3():
                    for j in range(4):
                        trk.op(DVE, lambda j=j: nc.vector.tensor_scalar(out=ybf[:, j * 128:(j + 1) * 128], in0=dt_[:, j * 128:(j + 1) * 128],
                                                                        scalar1=sm[:, 20 + j:21 + j], scalar2=None, op0=ALU.mult),
                               reads=[b_dt, b_rstd], writes=[b_ybf])
                    for j in range(4):
                        trk.op(PE, lambda j=j: nc.tensor.transpose(out=tpv[:, j * 128:(j + 1) * 128], in_=ybf[:, j * 128:(j + 1) * 128], identity=self.ident[:]),
                               reads=[b_ybf, self.b_const], writes=[b_tpy], inc=(j == 3))

                def s4():
                    y = ycount[0] % 2
                    ycount[0] += 1
                    trk.op(DVE, lambda: nc.vector.tensor_copy(out=yst[y][:], in_=tpv[:, 0:512]), reads=[b_tpy], writes=[b_yst[y]])
                    trk.dma(SP, self.yT[h][:, qt * 512:(qt + 1) * 512], yst[y][:], reads=[b_yst[y]], owner=b_yst[y])
                return [s0, s1, s2, s3, s4]

            heads = list(CFG['p2_heads'])
            KBS = CFG['p2_kbs']
            steps = [(hi, h, qt, kb) for hi, h in enumerate(heads) for qt in CFG['p2_qts'] for kb in KBS]
            n = len(steps)
            load_head(heads[0])
            if len(heads) > 1:
                load_head(heads[1])
            pend = []
            p3s = {}
            sched = (1, 2, 3, 4, 5) if SMALL else (1, 3, 8, 12, 16)

            def do_qk(i):
                hi, h, qt, kb = steps[i]
                p3s[i] = qk(h, qt, kb)

            def do_av(i):
                hi, h, qt, kb = steps[i]
                av(h, qt, kb, p3s.pop(i))
                if (i + 1 == n or steps[i + 1][0] != hi) and hi + 2 < len(heads):
                    load_head(heads[hi + 2])
                ki = KBS.index(kb)
                if kb == KBS[-1]:
                    assert not pend
                    pend.extend(epilogue_stages(h, qt))
                    pend.pop(0)()
                elif pend and ki in sched:
                    pend.pop(0)()

            do_qk(0)
            for i in range(n):
                if i + 1 < n:
                    do_qk(i + 1)
                if i >= 1:
                    do_av(i - 1)
            do_av(n - 1)
            for f in pend:
                f()
            trk.barrier()
            trk.release(b_kTh + b_qTh + b_vah + b_yst + [b_lam])

    def phase_p3(self, l, x_src, x_dst, kindB):
        trk, nc = self.trk, self.nc
        PE, ACT, DVE, POOL, SP = self.PE, self.ACT, self.DVE, self.POOL, self.SP
        lam_init = 0.8 - 0.6 * math.exp(-0.3 * l)
        last = (l == self.NL - 1)
        with ExitStack() as st:
            wob = self.sb(st, "wob", [128, 8, 1024], BF16)
            b_w = Buf("wo")
            self.load_w_bf16(wob, self.w_out[l], 0, 1024, b_w, st)
            gb = self.sb(st, "lng", [128, D], F32)
            bb = self.sb(st, "lnb", [128, D], F32)
            b_gb = Buf("gb")
            b_bb = Buf("bb")
            trk.dma(SP, gb[:], self.ln_g[l].partition_broadcast(128), writes=[b_gb], owner=b_gb)
            trk.dma(SP, bb[:], self.ln_b[l].partition_broadcast(128), writes=[b_bb], owner=b_bb)
            epsb = self.sb(st, "epsb3", [128, 1], F32)
            b_eps = Buf("eps")
            trk.op(POOL, lambda: nc.gpsimd.memset(epsb[:], EPS), writes=[b_eps])
            if not kindB:
                sg = self.sb(st, "sg", [128, 1], F32)
                b_sg = Buf("sg")
                trk.dma(SP, sg[:], self.subg[l].rearrange("(p o) -> p o", o=1), writes=[b_sg], owner=b_sg)
                trk.op(DVE, lambda: nc.vector.tensor_scalar(out=sg[:], in0=sg[:], scalar1=(1.0 - lam_init), scalar2=None, op0=ALU.mult), reads=[b_sg], writes=[b_sg])
                trk.op(DVE, lambda: nc.vector.tensor_scalar(out=wob[:].rearrange("p h n -> p (h n)"), in0=wob[:].rearrange("p h n -> p (h n)"),
                                                            scalar1=sg[:, 0:1], scalar2=None, op0=ALU.mult), reads=[b_w, b_sg], writes=[b_w])
            gTs = [self.sb(st, f"gTs{i}", [128, 8, 512], BF16) for i in range(2)]
            b_gTs = [Buf(f"gTs{i}") for i in range(2)]
            ypT = [self.sb(st, f"ypT{i}", [128, 8, 512], BF16) for i in range(2)]
            b_ypT = [Buf(f"ypT{i}") for i in range(2)]
            if not kindB:
                yTs = [self.sb(st, f"yTs{i}", [128, 8, 512], BF16) for i in range(2)]
                b_yTs = [Buf(f"yTs{i}") for i in range(2)]
            else:
                Og = [[self.sb(st, f"Og{g}{i}", [128, 16 * 65], F32) for i in range(2)] for g in range(3)]
                b_Og = [[Buf(f"Og{g}{i}") for i in range(2)] for g in range(3)]
                rlb = self.sb(st, "rlb", [128, 16], F32)
                b_rlb = Buf("rlb")
                obf = [self.sb(st, f"obf{i}", [128, D], BF16) for i in range(2)]
                b_obf = [Buf(f"obf{i}") for i in range(2)]
            xin = [self.sb(st, f"xin{i}", [128, D], F32) for i in range(2)]
            b_xin = [Buf(f"xin{i}") for i in range(2)]
            z = [self.sb(st, f"z{i}", [128, D], F32) for i in range(2)]
            b_z = [Buf(f"z{i}") for i in range(2)]
            xo = [self.sb(st, f"xo{i}", [128, D], F32) for i in range(2)]
            b_xo = [Buf(f"xo{i}") for i in range(2)]
            stats = self.sb(st, "stats", [128, 16], F32)
            b_stats = Buf("stats")
            mv = self.sb(st, "mv", [128, 8], F32)
            b_mv = Buf("mv")
            xbf = [self.sb(st, f"xbf{i}", [128, D], BF16) for i in range(2)]
            b_xbf = [Buf(f"xbf{i}") for i in range(2)]
            xTst = [self.sb(st, f"xTst{i}", [128, 8, 512], BF16) for i in range(2)]
            b_xTst = [Buf(f"xTst{i}") for i in range(2)]
            b_F = [Buf("F0"), Buf("F1")]
            tp = self.tp_views([4, 5])
            b_tp = [Buf("tp0"), Buf("tp1")]
            tpy = self.tp_views([6, 7])
            b_tpy = [Buf("tpy0"), Buf("tpy1")]
            gTv = self.gT.rearrange("(c p) t -> p c t", p=128)
            pend_x = []
            def issue_sb(sbi):
                s = sbi % 2
                sl = slice(sbi * 512, (sbi + 1) * 512)
                trk.dma(SP, gTs[s][:], gTv[:, :, sl], writes=[b_gTs[s]], owner=b_gTs[s])
                if not kindB:
                    trk.dma(SP, yTs[s][:], self.yT.rearrange("h p t -> p h t")[:, :, sl], writes=[b_yTs[s]], owner=b_yTs[s])

            def issue_tb(tb):
                k = tb % 2
                trk.dma(SP, xin[k][:], x_src[tb * 128:(tb + 1) * 128, :], writes=[b_xin[k]], owner=b_xin[k])
                if kindB:
                    for g, (win, dil) in enumerate(GROUPS):
                        src = self.OB[g].rearrange("(r m) e -> m r e", r=dil)[tb * 128 // dil: tb * 128 // dil + 128 // dil, :, :]
                        trk.dma(SP, Og[g][k][:], src, writes=[b_Og[g][k]], owner=b_Og[g][k])

            sbs = list(CFG['p3_sbs'])
            tbs = [(si, sbi, tb4) for si, sbi in enumerate(sbs) for tb4 in range(4)]
            ntb = len(tbs)

            def front(i):
                si, sbi, tb4 = tbs[i]
                s = sbi % 2
                tb = sbi * 4 + tb4
                k = tb % 2
                if tb4 == 0:
                    if si + 1 < len(sbs):
                        issue_sb(sbs[si + 1])
                    if not kindB:
                        trk.op(POOL, lambda: nc.gpsimd.tensor_tensor(out=ypT[s][:], in0=yTs[s][:], in1=gTs[s][:], op=ALU.mult),
                               reads=[b_yTs[s], b_gTs[s]], writes=[b_ypT[s]])
                if kindB:
                    trk.op(POOL, lambda: nc.gpsimd.tensor_tensor(out=Og[0][k][:], in0=Og[0][k][:], in1=Og[1][k][:], op=ALU.add),
                           reads=[b_Og[0][k], b_Og[1][k]], writes=[b_Og[0][k]])
                    trk.op(POOL, lambda: nc.gpsimd.tensor_tensor(out=Og[0][k][:], in0=Og[0][k][:], in1=Og[2][k][:], op=ALU.add),
                           reads=[b_Og[0][k], b_Og[2][k]], writes=[b_Og[0][k]])
                    Uv = Og[0][k][:].rearrange("p (h e) -> p h e", e=65)
                    trk.op(DVE, lambda: nc.vector.reciprocal(out=rlb[:], in_=Uv[:, :, 64]), reads=[b_Og[0][k]], writes=[b_rlb])
                    rl3 = bass.AP(rlb[:].tensor, rlb[:].offset, [list(rlb[:].ap[0]), [1, 16], [0, 64]])
                    trk.op(DVE, lambda: nc.vector.tensor_tensor(out=obf[k][:].rearrange("p (h e) -> p h e", e=64), in0=Uv[:, :, 0:64], in1=rl3, op=ALU.mult),
                           reads=[b_Og[0][k], b_rlb], writes=[b_obf[k]])
                    for c in range(8):
                        trk.op(PE, lambda c=c: nc.tensor.transpose(out=tpy[k][:, c * 128:(c + 1) * 128], in_=obf[k][:, c * 128:(c + 1) * 128], identity=self.ident[:]),
                               reads=[b_obf[k], self.b_const], writes=[b_tpy[k]], inc=(c == 7))
                    trk.op(DVE, lambda: nc.vector.tensor_tensor(out=ypT[s][:, :, tb4 * 128:(tb4 + 1) * 128], in0=tpy[k].rearrange("p (c t) -> p c t", c=8),
                                                                in1=gTs[s][:, :, tb4 * 128:(tb4 + 1) * 128], op=ALU.mult),
                           reads=[b_tpy[k], b_gTs[s]], writes=[b_ypT[s]])
                f = k
                for n in range(2):
                    for h in range(8):
                        trk.op(PE, lambda n=n, h=h: nc.tensor.matmul(self.ps[2 * f + n][:], lhsT=ypT[s][:, h, tb4 * 128:(tb4 + 1) * 128],
                                                                     rhs=wob[:, h, n * 512:(n + 1) * 512], start=(h == 0), stop=(h == 7)),
                               reads=[b_ypT[s], b_w], writes=[b_F[f]], inc=(n == 1 and h == 7))

            def back(i):
                si, sbi, tb4 = tbs[i]
                tb = sbi * 4 + tb4
                k = tb % 2
                f = k
                while pend_x:
                    pend_x.pop(0)()
                Fap = self.ps_all_ap(2 * f, 2)
                trk.op(DVE, lambda: nc.vector.scalar_tensor_tensor(out=z[k][:], in0=xin[k][:], scalar=ALPHA, in1=Fap, op0=ALU.mult, op1=ALU.add),
                       reads=[b_xin[k], b_F[f]], writes=[b_z[k]])
                for n in range(2):
                    trk.op(DVE, lambda n=n: nc.vector.bn_stats(out=stats[:, n * 6:(n + 1) * 6], in_=z[k][:, n * 512:(n + 1) * 512]),
                           reads=[b_z[k]], writes=[b_stats])
                trk.op(DVE, lambda: nc.vector.bn_aggr(out=mv[:, 0:2], in_=stats[:, 0:12]), reads=[b_stats], writes=[b_mv])
                trk.op(ACT, lambda: nc.scalar.activation(out=mv[:, 2:3], in_=mv[:, 1:2], func=AF.Ln, bias=epsb[:, 0:1]), reads=[b_mv, b_eps], writes=[b_mv])
                trk.op(ACT, lambda: nc.scalar.activation(out=mv[:, 3:4], in_=mv[:, 2:3], func=AF.Exp, scale=-0.5), reads=[b_mv], writes=[b_mv])
                trk.op(DVE, lambda: nc.vector.tensor_scalar(out=z[k][:], in0=z[k][:], scalar1=mv[:, 0:1], scalar2=mv[:, 3:4], op0=ALU.subtract, op1=ALU.mult),
                       reads=[b_z[k], b_mv], writes=[b_z[k]])
                trk.op(DVE, lambda: nc.vector.tensor_tensor(out=z[k][:], in0=z[k][:], in1=gb[:], op=ALU.mult), reads=[b_z[k], b_gb], writes=[b_z[k]])
                trk.op(POOL, lambda: nc.gpsimd.tensor_tensor(out=xo[k][:], in0=z[k][:], in1=bb[:], op=ALU.add), reads=[b_z[k], b_bb], writes=[b_xo[k]])
                trk.dma(SP, x_dst[tb * 128:(tb + 1) * 128, :], xo[k][:], reads=[b_xo[k]], owner=b_xo[k])
                if not last:
                    pend_x.append(lambda tb=tb, k=k: self.emit_xT_block(tb, xo[k][:], b_xo[k], xbf, b_xbf, tp, b_tp, xTst, b_xTst, self.ACT))
                if i + 2 < ntb:
                    issue_tb(tbs[i + 2][1] * 4 + tbs[i + 2][2])

            issue_sb(sbs[0])
            issue_tb(tbs[0][1] * 4 + tbs[0][2])
            if ntb > 1:
                issue_tb(tbs[1][1] * 4 + tbs[1][2])
            front(0)
            for i in range(ntb):
                if i + 1 < ntb:
                    front(i + 1)
                back(i)
            while pend_x:
                pend_x.pop(0)()
            trk.barrier()
            rel = self._wstg + [b_gb, b_bb] + b_gTs + b_xin + b_xo + b_xTst
            if kindB:
                rel += [b for g in b_Og for b in g]
            else:
                rel += b_yTs + [b_sg]
            trk.release(rel)

    def phase_p1b(self, l):
        trk, nc = self.trk, self.nc
        PE, ACT, DVE, POOL, SP = self.PE, self.ACT, self.DVE, self.POOL, self.SP
        with ExitStack() as st:
            wbf = self.sb(st, "wbfB", [128, 8, 3072], BF16)
            b_w = Buf("w")
            wg = self.sb(st, "wgB", [128, 8, 1024], BF16)
            b_wg = Buf("wg")
            rtab = self.sb(st, "rtabB", [128, NTB * 32], F32)
            b_rt = Buf("rt")
            xTw = [self.sb(st, f"xTw{i}", [128, 8, 2048], BF16) for i in range(2)]
            b_xTw = [Buf(f"xTw{i}") for i in range(2)]
            gst = self.sb(st, "gstB", [128, 8, 512], BF16)
            b_gst = Buf("gst")
            qst = self.sb(st, "qstB", [128, 8, 512], BF16)
            b_qst = Buf("qst")
            kst = self.sb(st, "kstB", [128, 8, 512], BF16)
            b_kst = Buf("kst")
            vst = self.sb(st, "vstB", [128, 8, 4, 130], BF16)
            b_vst = Buf("vst")
            qkb = [self.sb(st, f"qkbB{i}", [128, 1024], BF16) for i in range(2)]
            b_qkb = [Buf(f"qkb{i}") for i in range(2)]
            tmpA = self.sb(st, "rtAB", [128, 256], F32)
            tmpB = self.sb(st, "rtBB", [128, 256], F32)
            b_tmp = [Buf("rtA"), Buf("rtB")]
            trk.op(POOL, lambda: nc.gpsimd.memset(vst[:], 1.0), writes=[b_vst])
            Ub = [(0, 1), (2, 3)]
            b_U = [Buf("U0"), Buf("U1")]
            b_G = [Buf("G0"), Buf("G1")]
            tp = self.tp_views([6, 7])
            b_tp = [Buf("tp0"), Buf("tp1")]
            ucnt = 0
            qkcnt = 0
            gcnt = 0
            wcnt = 0
            pend_tp = []
            self.load_w_bf16(wg, self.w_in[l], 9216, 1024, b_wg, st, tag="g")
            wst_all = list(self._wstg)
            xTv = self.xT.rearrange("(c p) t -> p c t", p=128)
            gTv = self.gT.rearrange("(c p) t -> p c t", p=128)
            windows = range(1) if SMALL else range(4)
            for g, (win, dil) in enumerate(GROUPS):
                L = T_TOK // dil
                nkt = L // 128
                self.load_w_bf16(wbf, self.w_in[l], g * 3072, 3072, b_w, st, tag=f"w{g}")
                wst_all += list(self._wstg)
                trk.dma(SP, rtab[:], self.rope_d[g], writes=[b_rt], owner=b_rt)
                for w in windows:
                    xw = wcnt % 2
                    wcnt += 1
                    if wcnt == 1:
                        trk.dma(SP, xTw[xw][:], xTv[:, :, w * 2048:(w + 1) * 2048], writes=[b_xTw[xw]], owner=b_xTw[xw])
                    wl = list(windows)
                    nxt_w = wl[wl.index(w) + 1] if wl.index(w) + 1 < len(wl) else (wl[0] if g < 2 else None)
                    if nxt_w is not None:
                        nx = wcnt % 2
                        trk.dma(SP, xTw[nx][:], xTv[:, :, nxt_w * 2048:(nxt_w + 1) * 2048], writes=[b_xTw[nx]], owner=b_xTw[nx])
                    if g == 0:
                        for sb4 in range(4):
                            sbi = w * 4 + sb4
                            for cb in range(8):
                                gi = gcnt % 2
                                gcnt += 1
                                G = self.ps[4 + gi]
                                for c in range(8):
                                    trk.op(PE, lambda c=c, cb=cb, G=G: nc.tensor.matmul(G, lhsT=wg[:, c, cb * 128:(cb + 1) * 128],
                                                                                        rhs=xTw[xw][:, c, sb4 * 512:(sb4 + 1) * 512], start=(c == 0), stop=(c == 7)),
                                           reads=[b_wg, b_xTw[xw]], writes=[b_G[gi]], inc=(c == 7))
                                trk.op(ACT, lambda cb=cb, G=G: nc.scalar.activation(out=gst[:, cb, :], in_=G, func=AF.Silu),
                                       reads=[b_G[gi]], writes=[b_gst])
                            trk.dma(SP, gTv[:, :, sbi * 512:(sbi + 1) * 512], gst[:], reads=[b_gst], owner=b_gst)
                    nj = 16 // dil
                    for r in range(dil):
                        for j in range(nj):
                            rb = r * nkt + (w * 2048 // dil) // 128 + j
                            slot = j % 4
                            run_end = (slot == 3) or (j == nj - 1)
                            tok0 = r + dil * 128 * j
                            for ui in range(3):
                                u = ucnt % 2
                                ucnt += 1
                                b0, b1 = Ub[u]
                                for n in range(2):
                                    for c in range(8):
                                        lw = xTw[xw][:, c, tok0:tok0 + (127 * dil + 1):dil] if dil > 1 else xTw[xw][:, c, tok0:tok0 + 128]
                                        trk.op(PE, lambda n=n, c=c, lw=lw, b0=b0, b1=b1, ui=ui: nc.tensor.matmul(
                                            self.ps[(b0, b1)[n]], lhsT=lw,
                                            rhs=wbf[:, c, ui * 1024 + n * 512: ui * 1024 + (n + 1) * 512], start=(c == 0), stop=(c == 7)),
                                            reads=[b_w, b_xTw[xw]], writes=[b_U[u]], inc=(n == 1 and c == 7))
                                Uap = self.pair_ap(b0)
                                while pend_tp:
                                    pend_tp.pop(0)()
                                if ui < 2:
                                    k = qkcnt % 2
                                    qkcnt += 1
                                    trk.op(ACT, lambda k=k, Uap=Uap: nc.scalar.copy(out=qkb[k][:], in_=Uap), writes=[b_U[u], b_qkb[k]])
                                    self.emit_rope(Uap, b_U[u], qkb[k][:], b_qkb[k], rtab, b_rt, rb, tmpA, tmpB, b_tmp)
                                    def _tp(k=k, ui=ui, slot=slot):
                                        tpv = tp[k]
                                        for c in range(8):
                                            trk.op(PE, lambda c=c: nc.tensor.transpose(out=tpv[:, c * 128:(c + 1) * 128], in_=qkb[k][:, c * 128:(c + 1) * 128], identity=self.ident[:]),
                                                   reads=[b_qkb[k], self.b_const], writes=[b_tp[k]], inc=(c == 7))
                                        stg, b_stg = (qst, b_qst) if ui == 0 else (kst, b_kst)
                                        trk.op(DVE, lambda: nc.vector.tensor_copy(out=stg[:, :, slot * 128:(slot + 1) * 128],
                                                                                  in_=tpv.rearrange("p (c t) -> p c t", c=8)),
                                               reads=[b_tp[k]], writes=[b_stg])
                                    pend_tp.append(_tp)
                                else:
                                    vv = vst[:, :, slot, :].rearrange("p h (a e) -> p h a e", e=65)[:, :, :, 0:64]
                                    trk.op(ACT, lambda vv=vv, Uap=Uap: nc.scalar.copy(out=vv, in_=Uap.rearrange("p (h a e) -> p h a e", a=2, e=64)),
                                           reads=[b_U[u]], writes=[b_vst])
                            if run_end:
                                while pend_tp:
                                    pend_tp.pop(0)()
                                nb = slot + 1
                                rb0 = rb - slot
                                trk.dma(SP, self.qT[g].rearrange("h p t -> p h t")[:, :, rb0 * 128:(rb0 + nb) * 128], qst[:, :, 0:nb * 128], reads=[b_qst], owner=b_qst)
                                trk.dma(SP, self.kT[g].rearrange("h p t -> p h t")[:, :, rb0 * 128:(rb0 + nb) * 128], kst[:, :, 0:nb * 128], reads=[b_kst], owner=b_kst)
                                trk.dma(SP, self.vaB[g].rearrange("h p (b x) -> p h b x", x=130)[:, :, rb0:rb0 + nb, :], vst[:, :, 0:nb, :], reads=[b_vst], owner=b_vst)
            trk.barrier()
            trk.release(wst_all + [b_rt, b_gst, b_qst, b_kst, b_vst] + b_xTw)

    def phase_p2b(self, l):
        trk, nc = self.trk, self.nc
        PE, ACT, DVE, POOL, SP = self.PE, self.ACT, self.DVE, self.POOL, self.SP
        with ExitStack() as st:
            kTh = [self.sb(st, f"kThB{i}", [128, T_TOK], BF16) for i in range(2)]
            qTh = [self.sb(st, f"qThB{i}", [128, T_TOK], BF16) for i in range(2)]
            vah = [self.sb(st, f"vahB{i}", [128, NTB, 130], BF16) for i in range(2)]
            b_kTh = [Buf(f"kTh{i}") for i in range(2)]
            b_qTh = [Buf(f"qTh{i}") for i in range(2)]
            b_vah = [Buf(f"vah{i}") for i in range(2)]
            pt = [self.sb(st, f"ptB{i}", [128, 2, 256], BF16) for i in range(3)]
            b_pt = [Buf(f"pt{i}") for i in range(3)]
            ost = [self.sb(st, f"ostB{i}", [128, 130], F32) for i in range(4)]
            b_ost = [Buf(f"ost{i}") for i in range(4)]
            b_S = [Buf("S0"), Buf("S1")]
            b_O = [Buf("O0"), Buf("O1"), Buf("O2")]
            O_b = [4, 5, 6]
            tcount = [0]
            ocount = [0]
            lcount = [0]
            TL = 2048 if SMALL else T_TOK

            def load(g, hp):
                s = lcount[0] % 2
                lcount[0] += 1
                win, dil = GROUPS[g]
                Lf = T_TOK // dil
                Ll = TL // dil
                pieces = [(r * Lf, r * Lf + Ll) for r in range(dil)] if SMALL else [(0, T_TOK)]
                for (a, b) in pieces:
                    trk.dma(SP, kTh[s][:, a:b], self.kT[g, hp][:, a:b], writes=[b_kTh[s]], owner=b_kTh[s])
                    trk.dma(SP, qTh[s][:, a:b], self.qT[g, hp][:, a:b], writes=[b_qTh[s]], owner=b_qTh[s])
                    trk.dma(SP, vah[s][:, a // 128:b // 128, :], self.vaB[g, hp].rearrange("p (b x) -> p b x", x=130)[:, a // 128:b // 128, :],
                            writes=[b_vah[s]], owner=b_vah[s])
                return s

            def tile_geom(g, r, j):
                win, dil = GROUPS[g]
                Lf = T_TOK // dil
                nktf = Lf // 128
                nkt = (TL // dil) // 128
                c0 = 0 if j > 0 else 64
                c1 = 256 if j < nkt - 1 else 192
                Lh = Lf // 2
                var = 0
                if j == Lh // 128 - 1:
                    var = 1
                elif j == Lh // 128:
                    var = 2
                return Lf, nktf, nkt, c0, c1, var

            def qk(s, g, r, j):
                L, nktf, nkt, c0, c1, var = tile_geom(g, r, j)
                t = tcount[0]
                tcount[0] += 1
                sl = t % 2
                kt = r * nktf + j
                q0 = r * L + 128 * j - 64 + c0
                nco = c1 - c0
                for h2 in range(2):
                    Sb = self.ps[2 * sl + h2][:, 0:nco]
                    trk.op(PE, lambda h2=h2, Sb=Sb: nc.tensor.matmul(Sb, lhsT=kTh[s][h2 * 64:(h2 + 1) * 64, kt * 128:(kt + 1) * 128],
                                                                    rhs=qTh[s][h2 * 64:(h2 + 1) * 64, q0:q0 + nco], start=True, stop=True),
                           reads=[b_kTh[s], b_qTh[s]], writes=[b_S[sl]], inc=(h2 == 1))
                p3 = t % 3
                Sv = self.psall[:, 2 * sl * 512:(2 * sl + 2) * 512].rearrange("p (b n) -> p b n", b=2)[:, :, 0:nco]
                trk.op(ACT, lambda: nc.scalar.activation(out=pt[p3][:, :, 0:nco], in_=Sv, func=AF.Exp, scale=0.125),
                       reads=[b_S[sl]], writes=[b_pt[p3]])
                mk = self.mb[:, var * 256 + c0:var * 256 + c1]
                mk3 = bass.AP(mk.tensor, mk.offset, [list(mk.ap[0]), [0, 2], [1, nco]])
                trk.op(DVE, lambda: nc.vector.tensor_tensor(out=pt[p3][:, :, 0:nco], in0=pt[p3][:, :, 0:nco], in1=mk3, op=ALU.mult),
                       reads=[self.b_const], writes=[b_pt[p3]])
                return p3

            def av(s, g, hp, r, j, p3):
                L, nktf, nkt, c0, c1, var = tile_geom(g, r, j)
                kt = r * nktf + j
                parts = []
                parts.append((j - 1, c0, 128, j == 0, True))
                parts.append((j, 128, c1, True, j == nkt - 1))
                for (ch, a0, a1, first, lastc) in parts:
                    rows = a1 - a0
                    ob = (ch + 1) % 3
                    for h2 in range(2):
                        trk.op(PE, lambda h2=h2, a0=a0, a1=a1, rows=rows, ob=ob, first=first: nc.tensor.matmul(
                            self.ps[O_b[ob]][0:rows, h2 * 65:(h2 + 1) * 65], lhsT=pt[p3][:, h2, a0 - c0:a1 - c0],
                            rhs=vah[s][:, kt, h2 * 65:(h2 + 1) * 65], start=(first and h2 == 0), stop=lastc, skip_group_check=True),
                            reads=[b_pt[p3], b_vah[s]], writes=[b_O[ob]], inc=(h2 == 1))
                    if lastc:
                        o = ocount[0] % 4
                        ocount[0] += 1
                        trk.op(DVE, lambda rows=rows, ob=ob, o=o: nc.vector.tensor_copy(out=ost[o][0:rows, :], in_=self.ps[O_b[ob]][0:rows, 0:130]),
                               reads=[b_O[ob]], writes=[b_ost[o]])
                        ridx0 = r * L + (128 * ch + 64 if ch >= 0 else 0)
                        if ch == nkt - 1:
                            ridx0 = r * L + nkt * 128 - 64
                        trk.dma(SP, self.OB[g][ridx0:ridx0 + rows, hp * 130:(hp + 1) * 130], ost[o][0:rows, :], reads=[b_ost[o]], owner=b_ost[o])

            todo = [(g, hp) for g in range(3) for hp in range(8)]
            steps = []
            for ti, (g, hp) in enumerate(todo):
                win, dil = GROUPS[g]
                nkt = (TL // dil) // 128
                for r in range(dil):
                    for j in range(nkt):
                        steps.append((ti, g, hp, r, j))
            n = len(steps)
            bufs = {0: load(*todo[0])}
            if len(todo) > 1:
                bufs[1] = load(*todo[1])
            p3s = {}

            def do_qk(i):
                ti, g, hp, r, j = steps[i]
                p3s[i] = qk(bufs[ti], g, r, j)

            def do_av(i):
                ti, g, hp, r, j = steps[i]
                av(bufs[ti], g, hp, r, j, p3s.pop(i))
                if (i + 1 == n or steps[i + 1][0] != ti) and ti + 2 < len(todo):
                    bufs[ti + 2] = load(*todo[ti + 2])

            do_qk(0)
            for i in range(n):
                if i + 1 < n:
                    do_qk(i + 1)
                if i >= 1:
                    do_av(i - 1)
            do_av(n - 1)
            trk.barrier()
            trk.release(b_kTh + b_qTh + b_vah + b_ost)

    def ps_all_ap(self, b0, n):
        return self.psall[:, b0 * 512:(b0 + n) * 512]

    def dump(self, src_ap):
        b = Buf("dump")
        yv = self.y_out.rearrange("(a b) c -> a (b c)", b=8)
        for i in range(8):
            for j in range(4):
                self.trk.dma(self.POOL, yv[i * 128:(i + 1) * 128, j * 2048:(j + 1) * 2048], src_ap[i * 128:(i + 1) * 128, j * 2048:(j + 1) * 2048], owner=b)
        self.trk.barrier()

    def build(self):
        import os
        dbg = os.environ.get("KDEBUG", "")
        self.phase_p0()
        if dbg == "p0":
            self.es.close()
            return self.nc
        if dbg == "p1a":
            self.phase_p1a(0)
            self.es.close()
            return self.nc
        if dbg == "p2a":
            self.phase_p1a(0)
            self.phase_p2a(0)
            if _os.environ.get("KTWICE"):
                self.phase_p2a(0)
            self.es.close()
            return self.nc
        if dbg == "b":
            sub = _os.environ.get("KBSUB", "123")
            if "1" in sub:
                self.phase_p1b(1)
            if "2" in sub:
                self.phase_p2b(1)
            if "3" in sub:
                self.phase_p3(1, self.x_in, self.y_out, kindB=True)
            self.es.close()
            return self.nc
        if dbg == "p3a":
            self.phase_p1a(0)
            self.phase_p2a(0)
            self.phase_p3(0, self.x_in, self.y_out, kindB=False)
            self.es.close()
            return self.nc
        cur = self.x_in
        for l in range(self.NL):
            dst = self.y_out if l == self.NL - 1 else self.xs[l % 2]
            if l % 2 == 0:
                self.phase_p1a(l)
                self.phase_p2a(l)
                self.phase_p3(l, cur, dst, kindB=False)
            else:
                self.phase_p1b(l)
                self.phase_p2b(l)
                self.phase_p3(l, cur, dst, kindB=True)
            cur = dst
        self.es.close()
        return self.nc


def rope_table(pos):
    half = 8
    inv = (np.float32(THETA) ** (-np.arange(half, dtype=np.float32) / np.float32(half))).astype(np.float32)
    ang = pos.astype(np.float32)[:, None] * inv[None, :]
    c = np.cos(ang).astype(np.float32)
    s = np.sin(ang).astype(np.float32)
    return np.concatenate([c, c, -s, s], axis=1).astype(np.float32)


def host_consts(is_sample):
    t = np.arange(T_TOK)
    pos = (t % 4096) if is_sample else t
    ropes = []
    for (win, dil) in GROUPS:
        L = T_TOK // dil
        ridx = np.arange(T_TOK)
        r, m = ridx // L, ridx % L
        tok = m * dil + r
        tab = rope_table(pos[tok])
        ropes.append(tab.reshape(NTB, 128, 32).transpose(1, 0, 2).reshape(128, NTB * 32))
    rope = np.stack(ropes).astype(np.float32)
    cm = np.full((128, 1), 0.0 if is_sample else 1.0, np.float32)
    kk = np.arange(128)[:, None]
    cc = np.arange(256)[None, :]
    band = (cc >= kk) & (cc <= kk + 128)
    base = np.where(band, 1.0, 0.0).astype(np.float32)
    hi = base.copy()
    lo = base.copy()
    if is_sample:
        hi[:, 192:] = 0.0
        lo[:, :64] = 0.0
    mb = np.concatenate([base, hi, lo], axis=1).astype(ml_dtypes.bfloat16)
    ident = np.eye(128, dtype=np.float32).astype(ml_dtypes.bfloat16)
    return {"rope": rope, "cm": cm, "mb": mb, "ident": ident}


_PROG_CACHE = {}


def make_in_maps(inputs):
    xp = np.ascontiguousarray(inputs["x_prompt"], dtype=np.float32)
    xs = np.ascontiguousarray(inputs["x_sample"], dtype=np.float32)
    shared = {}
    for i in range(DEPTH):
        shared[f"w_in_{i}"] = np.ascontiguousarray(inputs[f"w_in_{i}"], dtype=np.float32)
        shared[f"w_out_{i}"] = np.ascontiguousarray(inputs[f"w_out_{i}"], dtype=np.float32)
        shared[f"ln_g_{i}"] = np.ascontiguousarray(inputs[f"ln_g_{i}"], dtype=np.float32)
        shared[f"ln_b_{i}"] = np.ascontiguousarray(inputs[f"ln_b_{i}"], dtype=np.float32)
        if i % 2 == 0:
            shared[f"lamv_{i}"] = np.stack([inputs[f"lam_q1_{i}"], inputs[f"lam_k1_{i}"], inputs[f"lam_q2_{i}"], inputs[f"lam_k2_{i}"]]).astype(np.float32)
            shared[f"subln_g_{i}"] = np.ascontiguousarray(inputs[f"subln_g_{i}"], dtype=np.float32)
    cp = host_consts(False)
    cs = host_consts(True)
    in_maps = []
    for c in range(8):
        m = dict(shared)
        if c < 4:
            m["x"] = xp[c]
            m.update(cp)
        else:
            j = c - 4
            m["x"] = np.ascontiguousarray(xs[2 * j:2 * j + 2].reshape(T_TOK, D))
            m.update(cs)
        in_maps.append(m)
    return in_maps


def filter_maps(in_maps):
    if _os.environ.get("KDEBUG"):
        keep = lambda k: not (len(k) > 2 and k[-2] == "_" and k[-1].isdigit() and int(k[-1]) >= {"b": 2}.get(_os.environ["KDEBUG"], 1))
        in_maps = [{k: v for k, v in m.items() if keep(k)} for m in in_maps]
    return in_maps


def run(inputs, NL=4):
    if NL not in _PROG_CACHE:
        _PROG_CACHE[NL] = Prog(NL).build()
    nc = _PROG_CACHE[NL]
    in_maps = filter_maps(make_in_maps(inputs))
    res = run_bass_kernel_spmd(nc, in_maps, core_ids=list(range(8)))
    outs = [r["y"] for r in res.results]
    y_prompt = np.stack(outs[:4]).astype(np.float32)
    y_sample = np.concatenate([o.reshape(2, 4096, D) for o in outs[4:]], axis=0).astype(np.float32)
    return y_prompt, y_sample


def kernel(**inputs):
    return run(inputs, NL=4)
```

```python
import math
from contextlib import ExitStack

import numpy as np
import ml_dtypes
import concourse.bass as bass
import concourse.mybir as mybir
from concourse.bass_utils import run_bass_kernel_spmd

F32 = mybir.dt.float32
BF16 = mybir.dt.bfloat16
AF = mybir.ActivationFunctionType
ALU = mybir.AluOpType

T_TOK = 8192
D = 1024
NTB = T_TOK // 128
NSB = T_TOK // 512
DEPTH = 4
ALPHA = (2.0 * DEPTH) ** 0.25
EPS = 1e-5
THETA = 500000.0
NEG = -30000.0
GROUPS = ((128, 1), (512, 4), (2048, 16))

import os as _os
SMALL = bool(_os.environ.get("KSMALL"))
CFG = dict(p0_tbs=range(NTB), p1_sbs=range(NSB), p2_heads=range(8), p2_qts=range(16), p2_kbs=list(range(NTB)), p3_sbs=range(NSB))
if SMALL:
    CFG = dict(p0_tbs=range(8), p1_sbs=range(2), p2_heads=range(8), p2_qts=[0, 1], p2_kbs=list(range(8)), p3_sbs=range(2))
    if _os.environ.get("KDEBUG") == "b":
        CFG.update(p0_tbs=range(16), p3_sbs=range(4))


class Ev:
    __slots__ = ("sem", "val")

    def __init__(self, sem, val):
        self.sem = sem
        self.val = val


class Buf:
    def __init__(self, name):
        self.name = name
        self.w = None
        self.r = {}
        self.dsem = None


class Sem:
    def __init__(self, h, name):
        self.h = h
        self.name = name
        self.cnt = 0


class Eng:
    def __init__(self, trk, name, eng, relaxed=False):
        self.name = name
        self.eng = eng
        self.sem = trk.new_sem("e_" + name)
        self.waited = {}
        self.pending = []
        self.relaxed = relaxed

    def wait(self, ev):
        assert ev.val is not None, "waiting on unresolved event"
        if self.waited.get(ev.sem, 0) >= ev.val:
            return
        self.eng.wait_ge(ev.sem.h, ev.val)
        self.waited[ev.sem] = ev.val


class Tracker:
    def __init__(self, nc, es):
        self.nc = nc
        self.es = es
        self.sems = []
        self.dsem_pool = []
        self.engs = []

    def new_sem(self, name):
        s = Sem(self.es.enter_context(self.nc.semaphore(name)), name)
        self.sems.append(s)
        return s

    def add_eng(self, name, eng, relaxed=False):
        e = Eng(self, name, eng, relaxed)
        self.engs.append(e)
        return e

    def _deps(self, E, reads, writes):
        for b in reads:
            if b.w is not None:
                self._wait(E, b.w, raw=True)
        for b in writes:
            if b.w is not None:
                self._wait(E, b.w, raw=True)
            for ev in b.r.values():
                self._wait(E, ev, raw=False)

    def _wait(self, E, ev, raw):
        if ev.sem is E.sem:
            if E.relaxed or ev.val is None:
                return
        E.wait(ev)

    def _record(self, ev, reads, writes):
        for b in reads:
            b.r[ev.sem] = ev
        for b in writes:
            b.w = ev
            b.r = {}

    def op(self, E, fn, reads=(), writes=(), inc=True):
        self._deps(E, reads, writes)
        ins = fn()
        if inc:
            E.sem.cnt += 1
            ins.then_inc(E.sem.h, 1)
            ev = Ev(E.sem, E.sem.cnt)
            for (pev, pr, pw) in E.pending:
                pev.val = ev.val
            E.pending = []
            self._record(ev, reads, writes)
        else:
            ev = Ev(E.sem, None)
            E.pending.append((ev, reads, writes))
            self._record(ev, reads, writes)
        return ins

    def get_dsem(self, b):
        if b.dsem is None:
            if self.dsem_pool:
                b.dsem = self.dsem_pool.pop()
            else:
                b.dsem = self.new_sem("d%d" % len(self.sems))
        return b.dsem

    def dma(self, Q, out, in_, reads=(), writes=(), owner=None, **kw):
        self._deps(Q, reads, writes)
        ds = self.get_dsem(owner)
        ins = Q.eng.dma_start(out=out, in_=in_, **kw)
        ds.cnt += 16
        ins.then_inc(ds.h, 16)
        ev = Ev(ds, ds.cnt)
        self._record(ev, reads, writes)
        return ins

    def release(self, bufs):
        for b in bufs:
            if b.dsem is not None:
                self.dsem_pool.append(b.dsem)
                b.dsem = None

    def barrier(self):
        for E in self.engs:
            assert not E.pending, E.name
        for E in self.engs:
            for X in self.engs:
                if X is not E and X.sem.cnt > 0:
                    E.wait(Ev(X.sem, X.sem.cnt))
            for s in self.sems:
                if s.name.startswith("d") and s.cnt > 0:
                    E.wait(Ev(s, s.cnt))


class Prog:
    def __init__(self, NL=4):
        self.NL = NL
        nc = self.nc = bass.Bass("TRN2", target_bir_lowering=False)
        self.es = ExitStack()
        es = self.es
        dt = nc.dram_tensor
        self.x_in = dt("x", [T_TOK, D], F32, kind="ExternalInput").ap()
        self.y_out = dt("y", [T_TOK, D], F32, kind="ExternalOutput").ap()
        self.w_in = []
        self.w_out = []
        self.ln_g = []
        self.ln_b = []
        self.lamv = {}
        self.subg = {}
        self.used_layers = range(DEPTH)
        if _os.environ.get("KDEBUG") in ("p0", "p1a", "p2a", "p3a"):
            self.used_layers = range(1)
        elif _os.environ.get("KDEBUG") == "b":
            self.used_layers = range(2)
        for i in self.used_layers:
            ncol = 4096 if i % 2 == 0 else 10240
            self.w_in.append(dt(f"w_in_{i}", [D, ncol], F32, kind="ExternalInput").ap())
            self.w_out.append(dt(f"w_out_{i}", [D, D], F32, kind="ExternalInput").ap())
            self.ln_g.append(dt(f"ln_g_{i}", [D], F32, kind="ExternalInput").ap())
            self.ln_b.append(dt(f"ln_b_{i}", [D], F32, kind="ExternalInput").ap())
            if i % 2 == 0:
                self.lamv[i] = dt(f"lamv_{i}", [4, 64], F32, kind="ExternalInput").ap()
                self.subg[i] = dt(f"subln_g_{i}", [128], F32, kind="ExternalInput").ap()
        self.rope_d = dt("rope", [3, 128, NTB * 32], F32, kind="ExternalInput").ap()
        self.cm_d = dt("cm", [128, 1], F32, kind="ExternalInput").ap()
        self.mb_d = dt("mb", [128, 3 * 256], BF16, kind="ExternalInput").ap()
        self.ident_d = dt("ident", [128, 128], BF16, kind="ExternalInput").ap()
        self.xs = [dt(f"xs{i}", [T_TOK, D], F32).ap() for i in range(2)]
        self.xT = dt("xT", [D, T_TOK], BF16).ap()
        self.gT = dt("gT", [D, T_TOK], BF16).ap()
        self.qT = dt("qT", [3, 8, 128, T_TOK], BF16).ap()
        self.kT = dt("kT", [3, 8, 128, T_TOK], BF16).ap()
        self.vaA = dt("vaA", [8, 128, NTB * 129], BF16).ap()
        self.vaB = dt("vaB", [3, 8, 128, NTB * 130], BF16).ap()
        self.yT = dt("yT", [8, 128, T_TOK], BF16).ap()
        self.OB = dt("OB", [3, T_TOK, 16 * 65], F32).ap()

        trk = self.trk = Tracker(nc, es)
        self.PE = trk.add_eng("pe", nc.tensor, relaxed=True)
        rlx = bool(_os.environ.get("KRELAX"))
        self.ACT = trk.add_eng("act", nc.scalar, relaxed=rlx)
        self.DVE = trk.add_eng("dve", nc.vector, relaxed=rlx)
        self.POOL = trk.add_eng("pool", nc.gpsimd, relaxed=rlx)
        self.SP = trk.add_eng("sp", nc.sync)
        self.psall = es.enter_context(nc.psum_tensor("psall", [128, 4096], F32))
        self.ps = [self.psall[:, i * 512:(i + 1) * 512] for i in range(8)]
        self.ident = es.enter_context(nc.sbuf_tensor("ident_sb", [128, 128], BF16))
        self.cm = es.enter_context(nc.sbuf_tensor("cm_sb", [128, 1], F32))
        self.mb = es.enter_context(nc.sbuf_tensor("mb_sb", [128, 3 * 256], BF16))
        self.b_const = Buf("const")
        trk.dma(self.SP, self.ident[:], self.ident_d, writes=[self.b_const], owner=self.b_const)
        trk.dma(self.SP, self.cm[:], self.cm_d, writes=[self.b_const], owner=self.b_const)
        trk.dma(self.SP, self.mb[:], self.mb_d, writes=[self.b_const], owner=self.b_const)
        trk.barrier()

    def sb(self, st, name, shape, dtype):
        self._uid = getattr(self, "_uid", 0) + 1
        return st.enter_context(self.nc.sbuf_tensor(f"{name}_u{self._uid}", shape, dtype))

    def emit_xT_block(self, tb, src_ap, src_buf, xbf, b_xbf, tp, b_tp, xTst, b_xTst, cast_eng):
        trk, nc = self.trk, self.nc
        sbi, tb4 = divmod(tb, 4)
        k = tb % 2
        if cast_eng is self.ACT:
            trk.op(self.ACT, lambda: nc.scalar.copy(out=xbf[k][:], in_=src_ap), reads=[src_buf], writes=[b_xbf[k]])
        else:
            trk.op(cast_eng, lambda: cast_eng.eng.tensor_copy(out=xbf[k][:], in_=src_ap), reads=[src_buf], writes=[b_xbf[k]])
        tpv = tp[k]
        for c in range(8):
            trk.op(self.PE, lambda c=c: nc.tensor.transpose(out=tpv[:, c * 128:(c + 1) * 128], in_=xbf[k][:, c * 128:(c + 1) * 128], identity=self.ident[:]),
                   reads=[b_xbf[k], self.b_const], writes=[b_tp[k]], inc=(c == 7))
        s = sbi % 2
        trk.op(self.DVE, lambda: nc.vector.tensor_copy(out=xTst[s][:, :, tb4 * 128:(tb4 + 1) * 128],
                                                       in_=tpv.rearrange("p (c t) -> p c t", c=8)),
               reads=[b_tp[k]], writes=[b_xTst[s]])
        if tb4 == 3:
            trk.dma(getattr(self, "xT_store_q", None) or self.SP, self.xT.rearrange("(c p) t -> p c t", p=128)[:, :, sbi * 512:(sbi + 1) * 512], xTst[s][:],
                    reads=[b_xTst[s]], owner=b_xTst[s])

    def tp_views(self, banks):
        return [self.ps[b][:].bitcast(BF16) for b in banks]

    def phase_p0(self):
        trk, nc = self.trk, self.nc
        with ExitStack() as st:
            xin = [self.sb(st, f"p0x{i}", [128, D], F32) for i in range(2)]
            b_xin = [Buf(f"xin{i}") for i in range(2)]
            xbf = [self.sb(st, f"p0xb{i}", [128, D], BF16) for i in range(2)]
            b_xbf = [Buf(f"xbf{i}") for i in range(2)]
            xTst = [self.sb(st, f"p0st{i}", [128, 8, 512], BF16) for i in range(2)]
            b_xTst = [Buf(f"xTst{i}") for i in range(2)]
            tp = self.tp_views([6, 7])
            b_tp = [Buf("tp0"), Buf("tp1")]
            for tb in CFG['p0_tbs']:
                k = tb % 2
                trk.dma(self.SP, xin[k][:], self.x_in[tb * 128:(tb + 1) * 128, :], writes=[b_xin[k]], owner=b_xin[k])
                self.emit_xT_block(tb, xin[k][:], b_xin[k], xbf, b_xbf, tp, b_tp, xTst, b_xTst, self.POOL)
            trk.barrier()
            trk.release(b_xin + b_xTst)

    def load_w_bf16(self, dst, w_ap, col0, ncols, b_w, st, tag=""):
        trk, nc = self.trk, self.nc
        step = 1024
        if not hasattr(st, "_wstg_t"):
            st._wstg_t = [self.sb(st, f"wstg{i}", [128, step], F32) for i in range(2)]
            st._wstg_b = [Buf(f"wstg{i}") for i in range(2)]
        stg = st._wstg_t
        b_stg = st._wstg_b
        i = 0
        for c in range(8):
            for c0 in range(0, ncols, step):
                n = min(step, ncols - c0)
                k = i % 2
                i += 1
                trk.dma(self.SP, stg[k][:, 0:n], w_ap[c * 128:(c + 1) * 128, col0 + c0:col0 + c0 + n],
                        writes=[b_stg[k]], owner=b_stg[k])
                trk.op(self.POOL, lambda k=k, n=n, c=c, c0=c0: nc.gpsimd.tensor_copy(out=dst[:, c, c0:c0 + n], in_=stg[k][:, 0:n]),
                       reads=[b_stg[k]], writes=[b_w])
        self._wstg = b_stg

    def emit_rope(self, U, b_U, qkb, b_qkb, rtab, b_rt, rb, tmpA, tmpB, b_tmp):
        trk, nc = self.trk, self.nc
        if "r" in _os.environ.get("KSKIP", ""):
            return
        Uv = U.rearrange("p (h e) -> p h e", e=64)
        x16 = Uv[:, :, 0:16]
        cc = rtab[:, rb * 32:rb * 32 + 16]
        ns = rtab[:, rb * 32 + 16:rb * 32 + 24]
        ps_ = rtab[:, rb * 32 + 24:rb * 32 + 32]
        ccb = bass.AP(cc.tensor, cc.offset, [list(cc.ap[0]), [0, 16], [1, 16]])
        nsb = bass.AP(ns.tensor, ns.offset, [list(ns.ap[0]), [0, 16], [1, 8]])
        psb = bass.AP(ps_.tensor, ps_.offset, [list(ps_.ap[0]), [0, 16], [1, 8]])
        tA = tmpA[:].rearrange("p (h e) -> p h e", e=16)
        tB = tmpB[:].rearrange("p (h e) -> p h e", e=16)
        mode = _os.environ.get("KROPE", "1")
        if mode == "1":
            trk.op(self.DVE, lambda: nc.vector.tensor_copy(out=tA, in_=x16), writes=[b_U, b_tmp[0]])
            trk.op(self.DVE, lambda: nc.vector.tensor_tensor(out=tB[:, :, 0:8], in0=tA[:, :, 8:16], in1=nsb, op=ALU.mult), reads=[b_tmp[0], b_rt], writes=[b_tmp[1]])
            trk.op(self.DVE, lambda: nc.vector.tensor_tensor(out=tB[:, :, 8:16], in0=tA[:, :, 0:8], in1=psb, op=ALU.mult), reads=[b_tmp[0], b_rt], writes=[b_tmp[1]])
            trk.op(self.DVE, lambda: nc.vector.tensor_tensor(out=tA, in0=tA, in1=ccb, op=ALU.mult), reads=[b_tmp[0], b_rt], writes=[b_tmp[0]])
        elif mode == "3":
            trk.op(self.DVE, lambda: nc.vector.tensor_copy(out=tA, in_=x16), writes=[b_U, b_tmp[0]])
            return
        elif mode == "4":
            trk.op(self.DVE, lambda: nc.vector.memset(tmpA[:], 0.0), writes=[b_tmp[0]])
            trk.op(self.DVE, lambda: nc.vector.memset(tmpB[:], 0.0), writes=[b_tmp[1]])
        else:
            trk.op(self.DVE, lambda: nc.vector.tensor_copy(out=tA, in_=x16), writes=[b_U, b_tmp[0]])
            trk.op(self.DVE, lambda: nc.vector.tensor_copy(out=tB, in_=x16), writes=[b_U, b_tmp[1]])
        qv = qkb.rearrange("p (h e) -> p h e", e=64)[:, :, 0:16]
        trk.op(self.DVE, lambda: nc.vector.tensor_tensor(out=qv, in0=tA, in1=tB, op=ALU.add), reads=[b_tmp[0], b_tmp[1]], writes=[b_qkb])

    def phase_p1a(self, l):
        trk, nc = self.trk, self.nc
        PE, ACT, DVE, POOL, SP = self.PE, self.ACT, self.DVE, self.POOL, self.SP
        with ExitStack() as st:
            wbf = self.sb(st, "wbf", [128, 8, 4096], BF16)
            b_w = Buf("w")
            self.load_w_bf16(wbf, self.w_in[l], 0, 4096, b_w, st)
            rtab = self.sb(st, "rtab", [128, NTB * 32], F32)
            b_rt = Buf("rt")
            trk.dma(SP, rtab[:], self.rope_d[0], writes=[b_rt], owner=b_rt)
            xTs = [self.sb(st, f"xTs{i}", [128, 8, 512], BF16) for i in range(2)]
            b_xTs = [Buf(f"xTs{i}") for i in range(2)]
            gst = [self.sb(st, f"gst{i}", [128, 8, 512], BF16) for i in range(2)]
            b_gst = [Buf(f"gst{i}") for i in range(2)]
            qst = [self.sb(st, f"qst{i}", [128, 8, 512], BF16) for i in range(2)]
            b_qst = [Buf(f"qst{i}") for i in range(2)]
            kst = [self.sb(st, f"kst{i}", [128, 8, 512], BF16) for i in range(2)]
            b_kst = [Buf(f"kst{i}") for i in range(2)]
            vst = [self.sb(st, f"vst{i}", [128, 8, 4 * 129], BF16) for i in range(2)]
            b_vst = [Buf(f"vst{i}") for i in range(2)]
            qkb = [self.sb(st, f"qkb{i}", [128, 1024], BF16) for i in range(2)]
            b_qkb = [Buf(f"qkb{i}") for i in range(2)]
            tmpA = self.sb(st, "rtA", [128, 256], F32)
            tmpB = self.sb(st, "rtB", [128, 256], F32)
            b_tmp = [Buf("rtA"), Buf("rtB")]
            for i in range(2):
                trk.op(POOL, lambda i=i: nc.gpsimd.memset(vst[i][:], 1.0), writes=[b_vst[i]])
            Ub = [(0, 1), (2, 3)]
            b_U = [Buf("U0"), Buf("U1")]
            b_G = [Buf("G0"), Buf("G1")]
            tp = self.tp_views([6, 7])
            b_tp = [Buf("tp0"), Buf("tp1")]
            ucnt = 0
            qkcnt = 0
            gcnt = 0
            pend_tp = []
            def load_x(sbi):
                trk.dma(SP, xTs[sbi % 2][:], self.xT.rearrange("(c p) t -> p c t", p=128)[:, :, sbi * 512:(sbi + 1) * 512],
                        writes=[b_xTs[sbi % 2]], owner=b_xTs[sbi % 2])
            p1sbs = list(CFG['p1_sbs'])
            load_x(p1sbs[0])
            for si_, sbi in enumerate(p1sbs):
                s = sbi % 2
                if si_ + 1 < len(p1sbs):
                    load_x(p1sbs[si_ + 1])
                for cb in range(8):
                    g = gcnt % 2
                    gcnt += 1
                    G = self.ps[4 + g]
                    for c in range(8):
                        trk.op(PE, lambda c=c: nc.tensor.matmul(G[:], lhsT=wbf[:, c, 3072 + cb * 128:3072 + (cb + 1) * 128],
                                                                rhs=xTs[s][:, c, :], start=(c == 0), stop=(c == 7)),
                               reads=[b_w, b_xTs[s]], writes=[b_G[g]], inc=(c == 7))
                    trk.op(ACT, lambda: nc.scalar.activation(out=gst[s][:, cb, :], in_=G[:], func=AF.Silu),
                           reads=[b_G[g]], writes=[b_gst[s]])
                trk.dma(SP, self.gT.rearrange("(c p) t -> p c t", p=128)[:, :, sbi * 512:(sbi + 1) * 512], gst[s][:],
                        reads=[b_gst[s]], owner=b_gst[s])
                for tb4 in range(4):
                    tb = sbi * 4 + tb4
                    for ui in range(3):
                        u = ucnt % 2
                        ucnt += 1
                        b0, b1 = Ub[u]
                        for n in range(2):
                            for c in range(8):
                                trk.op(PE, lambda n=n, c=c: nc.tensor.matmul(
                                    self.ps[(b0, b1)[n]][:], lhsT=xTs[s][:, c, tb4 * 128:(tb4 + 1) * 128],
                                    rhs=wbf[:, c, ui * 1024 + n * 512: ui * 1024 + (n + 1) * 512], start=(c == 0), stop=(c == 7)),
                                    reads=[b_w, b_xTs[s]], writes=[b_U[u]], inc=(n == 1 and c == 7))
                        Uap = self.pair_ap(b0)
                        while pend_tp:
                            pend_tp.pop(0)()
                        if ui < 2:
                            k = qkcnt % 2
                            qkcnt += 1
                            trk.op(ACT, lambda: nc.scalar.copy(out=qkb[k][:], in_=Uap), writes=[b_U[u], b_qkb[k]])
                            self.emit_rope(Uap, b_U[u], qkb[k][:], b_qkb[k], rtab, b_rt, tb, tmpA, tmpB, b_tmp)
                            def _tp(k=k, ui=ui, s=s, tb4=tb4):
                                tpv = tp[k]
                                for c in range(8):
                                    trk.op(PE, lambda c=c: nc.tensor.transpose(out=tpv[:, c * 128:(c + 1) * 128], in_=qkb[k][:, c * 128:(c + 1) * 128], identity=self.ident[:]),
                                           reads=[b_qkb[k], self.b_const], writes=[b_tp[k]], inc=(c == 7))
                                stg, b_stg = (qst, b_qst) if ui == 0 else (kst, b_kst)
                                trk.op(DVE, lambda: nc.vector.tensor_copy(out=stg[s][:, :, tb4 * 128:(tb4 + 1) * 128],
                                                                          in_=tpv.rearrange("p (c t) -> p c t", c=8)),
                                       reads=[b_tp[k]], writes=[b_stg[s]])
                            pend_tp.append(_tp)
                        else:
                            vv = vst[s][:].rearrange("p h (t e) -> p h t e", e=129)[:, :, tb4, 0:128]
                            trk.op(ACT, lambda: nc.scalar.copy(out=vv, in_=Uap.rearrange("p (h e) -> p h e", e=128)),
                                   reads=[b_U[u]], writes=[b_vst[s]])
                while pend_tp:
                    pend_tp.pop(0)()
                sl = slice(sbi * 512, (sbi + 1) * 512)
                trk.dma(SP, self.qT[0].rearrange("h p t -> p h t")[:, :, sl], qst[s][:], reads=[b_qst[s]], owner=b_qst[s])
                trk.dma(SP, self.kT[0].rearrange("h p t -> p h t")[:, :, sl], kst[s][:], reads=[b_kst[s]], owner=b_kst[s])
                trk.dma(SP, self.vaA.rearrange("h p x -> p h x")[:, :, sbi * 516:(sbi + 1) * 516], vst[s][:], reads=[b_vst[s]], owner=b_vst[s])
            trk.barrier()
            trk.release(self._wstg + [b_rt] + b_xTs + b_gst + b_qst + b_kst + b_vst)

    def pair_ap(self, b0):
        return self.ps_all_ap(b0, 2)

    def phase_p2a(self, l):
        trk, nc = self.trk, self.nc
        PE, ACT, DVE, POOL, SP = self.PE, self.ACT, self.DVE, self.POOL, self.SP
        lam_init = 0.8 - 0.6 * math.exp(-0.3 * l)
        NK = NTB
        with ExitStack() as st:
            kTh = [self.sb(st, f"kTh{i}", [128, T_TOK], BF16) for i in range(2)]
            qTh = [self.sb(st, f"qTh{i}", [128, T_TOK], BF16) for i in range(2)]
            vah = [self.sb(st, f"vah{i}", [128, NK * 129], BF16) for i in range(2)]
            vax = [self.sb(st, f"vax{i}", [128, NK * 129], BF16) for i in range(2)]
            b_kTh = [Buf(f"kTh{i}") for i in range(2)]
            b_qTh = [Buf(f"qTh{i}") for i in range(2)]
            b_vah = [Buf(f"vah{i}") for i in range(2)]
            b_vax = [Buf(f"vax{i}") for i in range(2)]
            ptt = [self.sb(st, f"ptt{b}", [128, 1024], BF16) for b in range(3)]
            pt = [[ptt[b][:, c * 512:(c + 1) * 512] for b in range(3)] for c in range(2)]
            b_pt = [Buf(f"pt{b}") for b in range(3)]
            osb = self.sb(st, "osb", [128, 8 * 129], F32)
            b_osb = Buf("osb")
            dt_ = self.sb(st, "dtmp", [128, 512], F32)
            b_dt = Buf("dtmp")
            junk = self.sb(st, "junk", [128, 128], F32)
            b_junk = Buf("junk")
            sm = self.sb(st, "sm", [128, 32], F32)
            b_rl, b_ssq, b_ln, b_rstd = Buf("rl"), Buf("ssq"), Buf("ln"), Buf("rstd")
            ybf = self.sb(st, "ybf", [128, 512], BF16)
            b_ybf = Buf("ybf")
            yst = [self.sb(st, f"yst{i}", [128, 512], BF16) for i in range(2)]
            b_yst = [Buf(f"yst{i}") for i in range(2)]
            lamt = self.sb(st, "lamt", [128, 4 * 64], F32)
            lams = self.sb(st, "lams", [128, 8], F32)
            b_lam = Buf("lam")
            b_lams = Buf("lams")
            epsb = self.sb(st, "epsb", [128, 1], F32)
            b_eps = Buf("eps")
            trk.op(POOL, lambda: nc.gpsimd.memset(epsb[:], EPS), writes=[b_eps])
            trk.dma(SP, lamt[:], self.lamv[l].rearrange("a b -> (a b)").partition_broadcast(128), writes=[b_lam], owner=b_lam)
            lt = lamt[:].rearrange("p (a b) -> p a b", b=64)
            trk.op(DVE, lambda: nc.vector.tensor_tensor(out=lt[:, 0, :], in0=lt[:, 0, :], in1=lt[:, 1, :], op=ALU.mult), reads=[b_lam], writes=[b_lam])
            trk.op(DVE, lambda: nc.vector.tensor_tensor(out=lt[:, 2, :], in0=lt[:, 2, :], in1=lt[:, 3, :], op=ALU.mult), reads=[b_lam], writes=[b_lam])
            trk.op(DVE, lambda: nc.vector.tensor_reduce(out=lams[:, 0:1], in_=lt[:, 0, :], axis=mybir.AxisListType.X, op=ALU.add), reads=[b_lam], writes=[b_lams])
            trk.op(DVE, lambda: nc.vector.tensor_reduce(out=lams[:, 1:2], in_=lt[:, 2, :], axis=mybir.AxisListType.X, op=ALU.add), reads=[b_lam], writes=[b_lams])
            trk.op(ACT, lambda: nc.scalar.activation(out=lams[:, 2:4], in_=lams[:, 0:2], func=AF.Exp), reads=[b_lams], writes=[b_lams])
            trk.op(DVE, lambda: nc.vector.scalar_tensor_tensor(out=lams[:, 4:5], in0=lams[:, 3:4], scalar=-lam_init, in1=lams[:, 2:3],
                                                               op0=ALU.add, op1=ALU.subtract), reads=[b_lams], writes=[b_lams])
            nlam = lams[:, 4:5]

            S_b = [[0, 2], [1, 3]]
            b_S = [Buf("S0"), Buf("S1")]
            O_b = [4, 5, 6]
            b_O = Buf("O")
            tpv = self.ps[7][:].bitcast(BF16)
            b_tpy = Buf("tpy")

            def acc_ap(c, j):
                a = c * 4 + j
                return self.ps[O_b[a // 3]][:, (a % 3) * 129:(a % 3) * 129 + 129]

            def load_head(h):
                s = h % 2
                TL = 1024 if SMALL else T_TOK
                VL = TL // 128 * 129
                trk.dma(SP, kTh[s][:, :TL], self.kT[0, h][:, :TL], writes=[b_kTh[s]], owner=b_kTh[s])
                trk.dma(SP, vah[s][:, :VL], self.vaA[h][:, :VL], writes=[b_vah[s]], owner=b_vah[s])
                trk.dma(SP, qTh[s][:, :TL], self.qT[0, h][:, :TL], writes=[b_qTh[s]], owner=b_qTh[s])
                trk.op(POOL, lambda: nc.gpsimd.tensor_scalar(out=vax[s][:, :VL], in0=vah[s][:, :VL], scalar1=self.cm[:, 0:1], scalar2=1.0, op0=ALU.mult, op1=ALU.mult),
                       reads=[b_vah[s], self.b_const], writes=[b_vax[s]])

            tcount = [0]
            ycount = [0]

            def qk(h, qt, kb):
                s = h % 2
                t = tcount[0]
                sl = t % 2
                for c in range(2):
                    trk.op(PE, lambda c=c: nc.tensor.matmul(self.ps[S_b[c][sl]][:], lhsT=kTh[s][c * 64:(c + 1) * 64, kb * 128:(kb + 1) * 128],
                                                            rhs=qTh[s][c * 64:(c + 1) * 64, qt * 512:(qt + 1) * 512], start=True, stop=True),
                           reads=[b_kTh[s], b_qTh[s]], writes=[b_S[sl]], inc=(c == 1))
                p3 = t % 3
                trk.op(ACT, lambda: nc.scalar.activation(out=ptt[p3][:], in_=self.psall[:, 2 * sl * 512:(2 * sl + 2) * 512], func=AF.Exp, scale=0.125),
                       reads=[b_S[sl]], writes=[b_pt[p3]])
                tcount[0] += 1
                return p3

            def av(h, qt, kb, p3):
                s = h % 2
                qhalf = (qt * 512) // 4096
                khalf = (kb * 128) // 4096
                vsrc, b_vsrc = (vah[s], b_vah[s]) if qhalf == khalf else (vax[s], b_vax[s])
                for c in range(2):
                    for j in range(4):
                        a = c * 4 + j
                        trk.op(PE, lambda c=c, j=j, a=a: nc.tensor.matmul(acc_ap(c, j), lhsT=pt[c][p3][:, j * 128:(j + 1) * 128],
                                                                          rhs=vsrc[:, kb * 129:(kb + 1) * 129],
                                                                          start=(kb == KBS[0] and a % 3 == 0), stop=(kb == KBS[-1]), skip_group_check=True),
                               reads=[b_pt[p3], b_vsrc], writes=[b_O], inc=(a == 7))

            def epilogue_stages(h, qt):
                def s0():
                    for b in range(3):
                        n = 387 if b < 2 else 258
                        trk.op(DVE, lambda b=b, n=n: nc.vector.tensor_copy(out=osb[:, b * 387:b * 387 + n], in_=self.ps[O_b[b]][:, 0:n]),
                               reads=[b_O], writes=[b_osb])
                ov = osb[:].rearrange("p (a e) -> p a e", e=129)

                def s1():
                    trk.op(DVE, lambda: nc.vector.reciprocal(out=sm[:, 0:8], in_=ov[:, :, 128]), reads=[b_osb], writes=[b_rl])
                    trk.op(DVE, lambda: nc.vector.tensor_scalar(out=sm[:, 8:12], in0=sm[:, 4:8], scalar1=nlam, scalar2=None, op0=ALU.mult),
                           reads=[b_rl, b_lams], writes=[b_rl])
                    for j in range(4):
                        dj = dt_[:, j * 128:(j + 1) * 128]
                        trk.op(DVE, lambda j=j, dj=dj: nc.vector.tensor_scalar(out=dj, in0=ov[:, j, 0:128], scalar1=sm[:, j:j + 1], scalar2=None, op0=ALU.mult),
                               reads=[b_osb, b_rl], writes=[b_dt])
                        trk.op(DVE, lambda j=j, dj=dj: nc.vector.scalar_tensor_tensor(out=dj, in0=ov[:, 4 + j, 0:128], scalar=sm[:, 8 + j:9 + j], in1=dj,
                                                                                      op0=ALU.mult, op1=ALU.add),
                               reads=[b_osb, b_rl, b_dt], writes=[b_dt])
                        trk.op(DVE, lambda j=j, dj=dj: nc.vector.scalar_tensor_tensor(out=junk[:], in0=dj, scalar=1.0, in1=dj, op0=ALU.mult, op1=ALU.mult,
                                                                                      accum_out=sm[:, 12 + j:13 + j]),
                               reads=[b_dt], writes=[b_junk, b_ssq])

                def s2():
                    trk.op(ACT, lambda: nc.scalar.activation(out=sm[:, 16:20], in_=sm[:, 12:16], func=AF.Ln, scale=1.0 / 128.0, bias=epsb[:, 0:1]),
                           reads=[b_ssq, b_eps], writes=[b_ln])
                    trk.op(ACT, lambda: nc.scalar.activation(out=sm[:, 20:24], in_=sm[:, 16:20], func=AF.Exp, scale=-0.5),
                           reads=[b_ln], writes=[b_rstd])

                def s3():
                    for j in range(4):
                        trk.op(DVE, lambda j=j: nc.vector.tensor_scalar(out=ybf[:, j * 128:(j + 1) * 128], in0=dt_[:, j * 128:(j + 1) * 128],
                                                                        scalar1=sm[:, 20 + j:21 + j], scalar2=None, op0=ALU.mult),
                               reads=[b_dt, b_rstd], writes=[b_ybf])
                    for j in range(4):
                        trk.op(PE, lambda j=j: nc.tensor.transpose(out=tpv[:, j * 128:(j + 1) * 128], in_=ybf[:, j * 128:(j + 1) * 128], identity=self.ident[:]),
                               reads=[b_ybf, self.b_const], writes=[b_tpy], inc=(j == 3))

                def s4():
                    y = ycount[0] % 2
                    ycount[0] += 1
                    trk.op(DVE, lambda: nc.vector.tensor_copy(out=yst[y][:], in_=tpv[:, 0:512]), reads=[b_tpy], writes=[b_yst[y]])
                    trk.dma(SP, self.yT[h][:, qt * 512:(qt + 1) * 512], yst[y][:], reads=[b_yst[y]], owner=b_yst[y])
                return [s0, s1, s2, s3, s4]

            heads = list(CFG['p2_heads'])
            KBS = CFG['p2_kbs']
            steps = [(hi, h, qt, kb) for hi, h in enumerate(heads) for qt in CFG['p2_qts'] for kb in KBS]
            n = len(steps)
            load_head(heads[0])
            if len(heads) > 1:
                load_head(heads[1])
            pend = []
            p3s = {}
            sched = (1, 2, 3, 4, 5) if SMALL else (1, 3, 8, 12, 16)

            def do_qk(i):
                hi, h, qt, kb = steps[i]
                p3s[i] = qk(h, qt, kb)

            def do_av(i):
                hi, h, qt, kb = steps[i]
                av(h, qt, kb, p3s.pop(i))
                if (i + 1 == n or steps[i + 1][0] != hi) and hi + 2 < len(heads):
                    load_head(heads[hi + 2])
                ki = KBS.index(kb)
                if kb == KBS[-1]:
                    assert not pend
                    pend.extend(epilogue_stages(h, qt))
                    pend.pop(0)()
                elif pend and ki in sched:
                    pend.pop(0)()

            do_qk(0)
            for i in range(n):
                if i + 1 < n:
                    do_qk(i + 1)
                if i >= 1:
                    do_av(i - 1)
            do_av(n - 1)
            for f in pend:
                f()
            trk.barrier()
            trk.release(b_kTh + b_qTh + b_vah + b_yst + [b_lam])

    def phase_p3(self, l, x_src, x_dst, kindB):
        trk, nc = self.trk, self.nc
        PE, ACT, DVE, POOL, SP = self.PE, self.ACT, self.DVE, self.POOL, self.SP
        lam_init = 0.8 - 0.6 * math.exp(-0.3 * l)
        last = (l == self.NL - 1)
        with ExitStack() as st:
            wob = self.sb(st, "wob", [128, 8, 1024], BF16)
            b_w = Buf("wo")
            self.load_w_bf16(wob, self.w_out[l], 0, 1024, b_w, st)
            gb = self.sb(st, "lng", [128, D], F32)
            bb = self.sb(st, "lnb", [128, D], F32)
            b_gb = Buf("gb")
            b_bb = Buf("bb")
            trk.dma(SP, gb[:], self.ln_g[l].partition_broadcast(128), writes=[b_gb], owner=b_gb)
            trk.dma(SP, bb[:], self.ln_b[l].partition_broadcast(128), writes=[b_bb], owner=b_bb)
            epsb = self.sb(st, "epsb3", [128, 1], F32)
            b_eps = Buf("eps")
            trk.op(POOL, lambda: nc.gpsimd.memset(epsb[:], EPS), writes=[b_eps])
            if not kindB:
                sg = self.sb(st, "sg", [128, 1], F32)
                b_sg = Buf("sg")
                trk.dma(SP, sg[:], self.subg[l].rearrange("(p o) -> p o", o=1), writes=[b_sg], owner=b_sg)
                trk.op(DVE, lambda: nc.vector.tensor_scalar(out=sg[:], in0=sg[:], scalar1=(1.0 - lam_init), scalar2=None, op0=ALU.mult), reads=[b_sg], writes=[b_sg])
                trk.op(DVE, lambda: nc.vector.tensor_scalar(out=wob[:].rearrange("p h n -> p (h n)"), in0=wob[:].rearrange("p h n -> p (h n)"),
                                                            scalar1=sg[:, 0:1], scalar2=None, op0=ALU.mult), reads=[b_w, b_sg], writes=[b_w])
            gTs = [self.sb(st, f"gTs{i}", [128, 8, 512], BF16) for i in range(2)]
            b_gTs = [Buf(f"gTs{i}") for i in range(2)]
            ypT = [self.sb(st, f"ypT{i}", [128, 8, 512], BF16) for i in range(2)]
            b_ypT = [Buf(f"ypT{i}") for i in range(2)]
            if not kindB:
                yTs = [self.sb(st, f"yTs{i}", [128, 8, 512], BF16) for i in range(2)]
                b_yTs = [Buf(f"yTs{i}") for i in range(2)]
            else:
                Og = [[self.sb(st, f"Og{g}{i}", [128, 16 * 65], F32) for i in range(3)] for g in range(3)]
                b_Og = [[Buf(f"Og{g}{i}") for i in range(3)] for g in range(3)]
                rlb = self.sb(st, "rlb", [128, 16], F32)
                b_rlb = Buf("rlb")
                obf = [self.sb(st, f"obf{i}", [128, D], BF16) for i in range(2)]
                b_obf = [Buf(f"obf{i}") for i in range(2)]
            xin = [self.sb(st, f"xin{i}", [128, D], F32) for i in range(3)]
            b_xin = [Buf(f"xin{i}") for i in range(3)]
            z = [self.sb(st, f"z{i}", [128, D], F32) for i in range(2)]
            b_z = [Buf(f"z{i}") for i in range(2)]
            xo = [self.sb(st, f"xo{i}", [128, D], F32) for i in range(2)]
            b_xo = [Buf(f"xo{i}") for i in range(2)]
            stats = self.sb(st, "stats", [128, 16], F32)
            b_stats = Buf("stats")
            mv = self.sb(st, "mv", [128, 8], F32)
            b_mv = Buf("mv")
            xbf = [self.sb(st, f"xbf{i}", [128, D], BF16) for i in range(2)]
            b_xbf = [Buf(f"xbf{i}") for i in range(2)]
            xTst = [self.sb(st, f"xTst{i}", [128, 8, 512], BF16) for i in range(2)]
            b_xTst = [Buf(f"xTst{i}") for i in range(2)]
            b_F = [Buf("F0"), Buf("F1")]
            tp = self.tp_views([4, 5])
            b_tp = [Buf("tp0"), Buf("tp1")]
            tpy = self.tp_views([6, 7])
            b_tpy = [Buf("tpy0"), Buf("tpy1")]
            gTv = self.gT.rearrange("(c p) t -> p c t", p=128)
            pend_x = []
            def issue_sb(sbi):
                s = sbi % 2
                sl = slice(sbi * 512, (sbi + 1) * 512)
                trk.dma(SP, gTs[s][:], gTv[:, :, sl], writes=[b_gTs[s]], owner=b_gTs[s])
                if not kindB:
                    trk.dma(SP, yTs[s][:], self.yT.rearrange("h p t -> p h t")[:, :, sl], writes=[b_yTs[s]], owner=b_yTs[s])

            def issue_tb(tb):
                k = tb % 3
                trk.dma(SP, xin[k][:], x_src[tb * 128:(tb + 1) * 128, :], writes=[b_xin[k]], owner=b_xin[k])
                if kindB:
                    for g, (win, dil) in enumerate(GROUPS):
                        src = self.OB[g].rearrange("(r m) e -> m r e", r=dil)[tb * 128 // dil: tb * 128 // dil + 128 // dil, :, :]
                        trk.dma(SP, Og[g][k][:], src, writes=[b_Og[g][k]], owner=b_Og[g][k])

            self.xT_store_q = ACT
            sbs = list(CFG['p3_sbs'])
            tbs = [(si, sbi, tb4) for si, sbi in enumerate(sbs) for tb4 in range(4)]
            ntb = len(tbs)

            def front(i):
                si, sbi, tb4 = tbs[i]
                s = sbi % 2
                tb = sbi * 4 + tb4
                k = tb % 2
                if tb4 == 0:
                    if si + 1 < len(sbs):
                        issue_sb(sbs[si + 1])
                    if not kindB:
                        trk.op(POOL, lambda: nc.gpsimd.tensor_tensor(out=ypT[s][:], in0=yTs[s][:], in1=gTs[s][:], op=ALU.mult),
                               reads=[b_yTs[s], b_gTs[s]], writes=[b_ypT[s]])
                if kindB:
                    k3 = tb % 3
                    trk.op(POOL, lambda: nc.gpsimd.tensor_tensor(out=Og[0][k3][:], in0=Og[0][k3][:], in1=Og[1][k3][:], op=ALU.add),
                           reads=[b_Og[0][k3], b_Og[1][k3]], writes=[b_Og[0][k3]])
                    trk.op(POOL, lambda: nc.gpsimd.tensor_tensor(out=Og[0][k3][:], in0=Og[0][k3][:], in1=Og[2][k3][:], op=ALU.add),
                           reads=[b_Og[0][k3], b_Og[2][k3]], writes=[b_Og[0][k3]])
                    Uv = Og[0][k3][:].rearrange("p (h e) -> p h e", e=65)
                    trk.op(DVE, lambda: nc.vector.reciprocal(out=rlb[:], in_=Uv[:, :, 64]), reads=[b_Og[0][k3]], writes=[b_rlb])
                    rl3 = bass.AP(rlb[:].tensor, rlb[:].offset, [list(rlb[:].ap[0]), [1, 16], [0, 64]])
                    trk.op(DVE, lambda: nc.vector.tensor_tensor(out=obf[k][:].rearrange("p (h e) -> p h e", e=64), in0=Uv[:, :, 0:64], in1=rl3, op=ALU.mult),
                           reads=[b_Og[0][k3], b_rlb], writes=[b_obf[k]])
                    for c in range(8):
                        trk.op(PE, lambda c=c: nc.tensor.transpose(out=tpy[k][:, c * 128:(c + 1) * 128], in_=obf[k][:, c * 128:(c + 1) * 128], identity=self.ident[:]),
                               reads=[b_obf[k], self.b_const], writes=[b_tpy[k]], inc=(c == 7))
                    trk.op(DVE, lambda: nc.vector.tensor_tensor(out=ypT[s][:, :, tb4 * 128:(tb4 + 1) * 128], in0=tpy[k].rearrange("p (c t) -> p c t", c=8),
                                                                in1=gTs[s][:, :, tb4 * 128:(tb4 + 1) * 128], op=ALU.mult),
                           reads=[b_tpy[k], b_gTs[s]], writes=[b_ypT[s]])
                f = k
                for n in range(2):
                    for h in range(8):
                        trk.op(PE, lambda n=n, h=h: nc.tensor.matmul(self.ps[2 * f + n][:], lhsT=ypT[s][:, h, tb4 * 128:(tb4 + 1) * 128],
                                                                     rhs=wob[:, h, n * 512:(n + 1) * 512], start=(h == 0), stop=(h == 7)),
                               reads=[b_ypT[s], b_w], writes=[b_F[f]], inc=(n == 1 and h == 7))

            def back(i):
                si, sbi, tb4 = tbs[i]
                tb = sbi * 4 + tb4
                k = tb % 2
                f = k
                while pend_x:
                    pend_x.pop(0)()
                Fap = self.ps_all_ap(2 * f, 2)
                k3 = tb % 3
                trk.op(DVE, lambda: nc.vector.scalar_tensor_tensor(out=z[k][:], in0=xin[k3][:], scalar=ALPHA, in1=Fap, op0=ALU.mult, op1=ALU.add),
                       reads=[b_xin[k3], b_F[f]], writes=[b_z[k]])
                for n in range(2):
                    trk.op(DVE, lambda n=n: nc.vector.bn_stats(out=stats[:, n * 6:(n + 1) * 6], in_=z[k][:, n * 512:(n + 1) * 512]),
                           reads=[b_z[k]], writes=[b_stats])
                trk.op(DVE, lambda: nc.vector.bn_aggr(out=mv[:, 0:2], in_=stats[:, 0:12]), reads=[b_stats], writes=[b_mv])
                trk.op(ACT, lambda: nc.scalar.activation(out=mv[:, 2:3], in_=mv[:, 1:2], func=AF.Ln, bias=epsb[:, 0:1]), reads=[b_mv, b_eps], writes=[b_mv])
                trk.op(ACT, lambda: nc.scalar.activation(out=mv[:, 3:4], in_=mv[:, 2:3], func=AF.Exp, scale=-0.5), reads=[b_mv], writes=[b_mv])
                trk.op(DVE, lambda: nc.vector.tensor_scalar(out=z[k][:], in0=z[k][:], scalar1=mv[:, 0:1], scalar2=mv[:, 3:4], op0=ALU.subtract, op1=ALU.mult),
                       reads=[b_z[k], b_mv], writes=[b_z[k]])
                trk.op(DVE, lambda: nc.vector.tensor_tensor(out=z[k][:], in0=z[k][:], in1=gb[:], op=ALU.mult), reads=[b_z[k], b_gb], writes=[b_z[k]])
                trk.op(POOL, lambda: nc.gpsimd.tensor_tensor(out=xo[k][:], in0=z[k][:], in1=bb[:], op=ALU.add), reads=[b_z[k], b_bb], writes=[b_xo[k]])
                trk.dma(ACT, x_dst[tb * 128:(tb + 1) * 128, :], xo[k][:], reads=[b_xo[k]], owner=b_xo[k])
                if not last:
                    pend_x.append(lambda tb=tb, k=k: self.emit_xT_block(tb, xo[k][:], b_xo[k], xbf, b_xbf, tp, b_tp, xTst, b_xTst, self.ACT))

            issue_sb(sbs[0])
            issue_tb(tbs[0][1] * 4 + tbs[0][2])
            if ntb > 1:
                issue_tb(tbs[1][1] * 4 + tbs[1][2])
            front(0)
            for i in range(ntb):
                if i + 2 < ntb:
                    issue_tb(tbs[i + 2][1] * 4 + tbs[i + 2][2])
                if i + 1 < ntb:
                    front(i + 1)
                back(i)
            while pend_x:
                pend_x.pop(0)()
            self.xT_store_q = None
            trk.barrier()
            rel = self._wstg + [b_gb, b_bb] + b_gTs + b_xin + b_xo + b_xTst
            if kindB:
                rel += [b for g in b_Og for b in g]
            else:
                rel += b_yTs + [b_sg]
            trk.release(rel)

    def phase_p1b(self, l):
        trk, nc = self.trk, self.nc
        PE, ACT, DVE, POOL, SP = self.PE, self.ACT, self.DVE, self.POOL, self.SP
        with ExitStack() as st:
            wbf = self.sb(st, "wbfB", [128, 8, 3072], BF16)
            b_w = Buf("w")
            wg = self.sb(st, "wgB", [128, 8, 1024], BF16)
            b_wg = Buf("wg")
            rtab = self.sb(st, "rtabB", [128, NTB * 32], F32)
            b_rt = Buf("rt")
            xTw = [self.sb(st, f"xTw{i}", [128, 8, 2048], BF16) for i in range(2)]
            b_xTw = [Buf(f"xTw{i}") for i in range(2)]
            gst = self.sb(st, "gstB", [128, 8, 512], BF16)
            b_gst = Buf("gst")
            qst = self.sb(st, "qstB", [128, 8, 512], BF16)
            b_qst = Buf("qst")
            kst = self.sb(st, "kstB", [128, 8, 512], BF16)
            b_kst = Buf("kst")
            vst = self.sb(st, "vstB", [128, 8, 4, 130], BF16)
            b_vst = Buf("vst")
            qkb = [self.sb(st, f"qkbB{i}", [128, 1024], BF16) for i in range(2)]
            b_qkb = [Buf(f"qkb{i}") for i in range(2)]
            tmpA = self.sb(st, "rtAB", [128, 256], F32)
            tmpB = self.sb(st, "rtBB", [128, 256], F32)
            b_tmp = [Buf("rtA"), Buf("rtB")]
            trk.op(POOL, lambda: nc.gpsimd.memset(vst[:], 1.0), writes=[b_vst])
            Ub = [(0, 1), (2, 3)]
            b_U = [Buf("U0"), Buf("U1")]
            b_G = [Buf("G0"), Buf("G1")]
            tp = self.tp_views([6, 7])
            b_tp = [Buf("tp0"), Buf("tp1")]
            ucnt = 0
            qkcnt = 0
            gcnt = 0
            wcnt = 0
            pend_tp = []
            self.load_w_bf16(wg, self.w_in[l], 9216, 1024, b_wg, st, tag="g")
            wst_all = list(self._wstg)
            xTv = self.xT.rearrange("(c p) t -> p c t", p=128)
            gTv = self.gT.rearrange("(c p) t -> p c t", p=128)
            windows = range(1) if SMALL else range(4)
            for g, (win, dil) in enumerate(GROUPS):
                L = T_TOK // dil
                nkt = L // 128
                self.load_w_bf16(wbf, self.w_in[l], g * 3072, 3072, b_w, st, tag=f"w{g}")
                wst_all += list(self._wstg)
                trk.dma(SP, rtab[:], self.rope_d[g], writes=[b_rt], owner=b_rt)
                for w in windows:
                    xw = wcnt % 2
                    wcnt += 1
                    if wcnt == 1:
                        trk.dma(SP, xTw[xw][:], xTv[:, :, w * 2048:(w + 1) * 2048], writes=[b_xTw[xw]], owner=b_xTw[xw])
                    wl = list(windows)
                    nxt_w = wl[wl.index(w) + 1] if wl.index(w) + 1 < len(wl) else (wl[0] if g < 2 else None)
                    if nxt_w is not None:
                        nx = wcnt % 2
                        trk.dma(SP, xTw[nx][:], xTv[:, :, nxt_w * 2048:(nxt_w + 1) * 2048], writes=[b_xTw[nx]], owner=b_xTw[nx])
                    if g == 0:
                        for sb4 in range(4):
                            sbi = w * 4 + sb4
                            for cb in range(8):
                                gi = gcnt % 2
                                gcnt += 1
                                G = self.ps[4 + gi]
                                for c in range(8):
                                    trk.op(PE, lambda c=c, cb=cb, G=G: nc.tensor.matmul(G, lhsT=wg[:, c, cb * 128:(cb + 1) * 128],
                                                                                        rhs=xTw[xw][:, c, sb4 * 512:(sb4 + 1) * 512], start=(c == 0), stop=(c == 7)),
                                           reads=[b_wg, b_xTw[xw]], writes=[b_G[gi]], inc=(c == 7))
                                trk.op(ACT, lambda cb=cb, G=G: nc.scalar.activation(out=gst[:, cb, :], in_=G, func=AF.Silu),
                                       reads=[b_G[gi]], writes=[b_gst])
                            trk.dma(SP, gTv[:, :, sbi * 512:(sbi + 1) * 512], gst[:], reads=[b_gst], owner=b_gst)
                    nj = 16 // dil
                    for r in range(dil):
                        for j in range(nj):
                            rb = r * nkt + (w * 2048 // dil) // 128 + j
                            slot = j % 4
                            run_end = (slot == 3) or (j == nj - 1)
                            tok0 = r + dil * 128 * j
                            for ui in range(3):
                                u = ucnt % 2
                                ucnt += 1
                                b0, b1 = Ub[u]
                                for n in range(2):
                                    for c in range(8):
                                        lw = xTw[xw][:, c, tok0:tok0 + (127 * dil + 1):dil] if dil > 1 else xTw[xw][:, c, tok0:tok0 + 128]
                                        trk.op(PE, lambda n=n, c=c, lw=lw, b0=b0, b1=b1, ui=ui: nc.tensor.matmul(
                                            self.ps[(b0, b1)[n]], lhsT=lw,
                                            rhs=wbf[:, c, ui * 1024 + n * 512: ui * 1024 + (n + 1) * 512], start=(c == 0), stop=(c == 7)),
                                            reads=[b_w, b_xTw[xw]], writes=[b_U[u]], inc=(n == 1 and c == 7))
                                Uap = self.pair_ap(b0)
                                while pend_tp:
                                    pend_tp.pop(0)()
                                if ui < 2:
                                    k = qkcnt % 2
                                    qkcnt += 1
                                    trk.op(ACT, lambda k=k, Uap=Uap: nc.scalar.copy(out=qkb[k][:], in_=Uap), writes=[b_U[u], b_qkb[k]])
                                    self.emit_rope(Uap, b_U[u], qkb[k][:], b_qkb[k], rtab, b_rt, rb, tmpA, tmpB, b_tmp)
                                    def _tp(k=k, ui=ui, slot=slot):
                                        tpv = tp[k]
                                        for c in range(8):
                                            trk.op(PE, lambda c=c: nc.tensor.transpose(out=tpv[:, c * 128:(c + 1) * 128], in_=qkb[k][:, c * 128:(c + 1) * 128], identity=self.ident[:]),
                                                   reads=[b_qkb[k], self.b_const], writes=[b_tp[k]], inc=(c == 7))
                                        stg, b_stg = (qst, b_qst) if ui == 0 else (kst, b_kst)
                                        trk.op(DVE, lambda: nc.vector.tensor_copy(out=stg[:, :, slot * 128:(slot + 1) * 128],
                                                                                  in_=tpv.rearrange("p (c t) -> p c t", c=8)),
                                               reads=[b_tp[k]], writes=[b_stg])
                                    pend_tp.append(_tp)
                                else:
                                    vv = vst[:, :, slot, :].rearrange("p h (a e) -> p h a e", e=65)[:, :, :, 0:64]
                                    trk.op(ACT, lambda vv=vv, Uap=Uap: nc.scalar.copy(out=vv, in_=Uap.rearrange("p (h a e) -> p h a e", a=2, e=64)),
                                           reads=[b_U[u]], writes=[b_vst])
                            if run_end:
                                while pend_tp:
                                    pend_tp.pop(0)()
                                nb = slot + 1
                                rb0 = rb - slot
                                trk.dma(SP, self.qT[g].rearrange("h p t -> p h t")[:, :, rb0 * 128:(rb0 + nb) * 128], qst[:, :, 0:nb * 128], reads=[b_qst], owner=b_qst)
                                trk.dma(SP, self.kT[g].rearrange("h p t -> p h t")[:, :, rb0 * 128:(rb0 + nb) * 128], kst[:, :, 0:nb * 128], reads=[b_kst], owner=b_kst)
                                trk.dma(SP, self.vaB[g].rearrange("h p (b x) -> p h b x", x=130)[:, :, rb0:rb0 + nb, :], vst[:, :, 0:nb, :], reads=[b_vst], owner=b_vst)
            trk.barrier()
            trk.release(wst_all + [b_rt, b_gst, b_qst, b_kst, b_vst] + b_xTw)

    def phase_p2b(self, l):
        trk, nc = self.trk, self.nc
        PE, ACT, DVE, POOL, SP = self.PE, self.ACT, self.DVE, self.POOL, self.SP
        with ExitStack() as st:
            kTh = [self.sb(st, f"kThB{i}", [128, T_TOK], BF16) for i in range(3)]
            qTh = [self.sb(st, f"qThB{i}", [128, T_TOK], BF16) for i in range(3)]
            vah = [self.sb(st, f"vahB{i}", [128, NTB, 130], BF16) for i in range(3)]
            b_kTh = [Buf(f"kTh{i}") for i in range(3)]
            b_qTh = [Buf(f"qTh{i}") for i in range(3)]
            b_vah = [Buf(f"vah{i}") for i in range(3)]
            pt = [self.sb(st, f"ptB{i}", [128, 2, 256], BF16) for i in range(3)]
            b_pt = [Buf(f"pt{i}") for i in range(3)]
            ost = [self.sb(st, f"ostB{i}", [128, 130], F32) for i in range(4)]
            b_ost = [Buf(f"ost{i}") for i in range(4)]
            b_S = [Buf("S0"), Buf("S1")]
            b_O = [Buf("O0"), Buf("O1"), Buf("O2")]
            O_b = [4, 5, 6]
            tcount = [0]
            ocount = [0]
            lcount = [0]
            TL = 2048 if SMALL else T_TOK

            def load(g, hp):
                s = lcount[0] % 3
                lcount[0] += 1
                win, dil = GROUPS[g]
                Lf = T_TOK // dil
                Ll = TL // dil
                pieces = [(r * Lf, r * Lf + Ll) for r in range(dil)] if SMALL else [(0, T_TOK)]
                for (a, b) in pieces:
                    trk.dma(SP, kTh[s][:, a:b], self.kT[g, hp][:, a:b], writes=[b_kTh[s]], owner=b_kTh[s])
                    trk.dma(SP, qTh[s][:, a:b], self.qT[g, hp][:, a:b], writes=[b_qTh[s]], owner=b_qTh[s])
                    trk.dma(SP, vah[s][:, a // 128:b // 128, :], self.vaB[g, hp].rearrange("p (b x) -> p b x", x=130)[:, a // 128:b // 128, :],
                            writes=[b_vah[s]], owner=b_vah[s])
                return s

            def tile_geom(g, r, j):
                win, dil = GROUPS[g]
                Lf = T_TOK // dil
                nktf = Lf // 128
                nkt = (TL // dil) // 128
                c0 = 0 if j > 0 else 64
                c1 = 256 if j < nkt - 1 else 192
                Lh = Lf // 2
                var = 0
                if j == Lh // 128 - 1:
                    var = 1
                elif j == Lh // 128:
                    var = 2
                return Lf, nktf, nkt, c0, c1, var

            def qk(s, g, r, j):
                L, nktf, nkt, c0, c1, var = tile_geom(g, r, j)
                t = tcount[0]
                tcount[0] += 1
                sl = t % 2
                kt = r * nktf + j
                q0 = r * L + 128 * j - 64 + c0
                nco = c1 - c0
                for h2 in range(2):
                    Sb = self.ps[2 * sl + h2][:, 0:nco]
                    trk.op(PE, lambda h2=h2, Sb=Sb: nc.tensor.matmul(Sb, lhsT=kTh[s][h2 * 64:(h2 + 1) * 64, kt * 128:(kt + 1) * 128],
                                                                    rhs=qTh[s][h2 * 64:(h2 + 1) * 64, q0:q0 + nco], start=True, stop=True),
                           reads=[b_kTh[s], b_qTh[s]], writes=[b_S[sl]], inc=(h2 == 1))
                p3 = t % 3
                Sv = self.psall[:, 2 * sl * 512:(2 * sl + 2) * 512].rearrange("p (b n) -> p b n", b=2)[:, :, 0:nco]
                trk.op(ACT, lambda: nc.scalar.activation(out=pt[p3][:, :, 0:nco], in_=Sv, func=AF.Exp, scale=0.125),
                       reads=[b_S[sl]], writes=[b_pt[p3]])
                mk = self.mb[:, var * 256 + c0:var * 256 + c1]
                mk3 = bass.AP(mk.tensor, mk.offset, [list(mk.ap[0]), [0, 2], [1, nco]])
                trk.op(DVE, lambda: nc.vector.tensor_tensor(out=pt[p3][:, :, 0:nco], in0=pt[p3][:, :, 0:nco], in1=mk3, op=ALU.mult),
                       reads=[self.b_const], writes=[b_pt[p3]])
                return p3

            def av(s, g, hp, r, j, p3):
                L, nktf, nkt, c0, c1, var = tile_geom(g, r, j)
                kt = r * nktf + j
                parts = []
                parts.append((j - 1, c0, 128, j == 0, True))
                parts.append((j, 128, c1, True, j == nkt - 1))
                for (ch, a0, a1, first, lastc) in parts:
                    rows = a1 - a0
                    ob = (ch + 1) % 3
                    for h2 in range(2):
                        trk.op(PE, lambda h2=h2, a0=a0, a1=a1, rows=rows, ob=ob, first=first: nc.tensor.matmul(
                            self.ps[O_b[ob]][0:rows, h2 * 65:(h2 + 1) * 65], lhsT=pt[p3][:, h2, a0 - c0:a1 - c0],
                            rhs=vah[s][:, kt, h2 * 65:(h2 + 1) * 65], start=(first and h2 == 0), stop=lastc, skip_group_check=True),
                            reads=[b_pt[p3], b_vah[s]], writes=[b_O[ob]], inc=(h2 == 1))
                    if lastc:
                        o = ocount[0] % 4
                        ocount[0] += 1
                        trk.op(DVE, lambda rows=rows, ob=ob, o=o: nc.vector.tensor_copy(out=ost[o][0:rows, :], in_=self.ps[O_b[ob]][0:rows, 0:130]),
                               reads=[b_O[ob]], writes=[b_ost[o]])
                        ridx0 = r * L + (128 * ch + 64 if ch >= 0 else 0)
                        if ch == nkt - 1:
                            ridx0 = r * L + nkt * 128 - 64
                        trk.dma(SP, self.OB[g][ridx0:ridx0 + rows, hp * 130:(hp + 1) * 130], ost[o][0:rows, :], reads=[b_ost[o]], owner=b_ost[o])

            todo = [(g, hp) for g in range(3) for hp in range(8)]
            steps = []
            for ti, (g, hp) in enumerate(todo):
                win, dil = GROUPS[g]
                nkt = (TL // dil) // 128
                for r in range(dil):
                    for j in range(nkt):
                        steps.append((ti, g, hp, r, j))
            n = len(steps)
            bufs = {}
            for ti0 in range(min(3, len(todo))):
                bufs[ti0] = load(*todo[ti0])
            p3s = {}

            def do_qk(i):
                ti, g, hp, r, j = steps[i]
                p3s[i] = qk(bufs[ti], g, r, j)

            def do_av(i):
                ti, g, hp, r, j = steps[i]
                av(bufs[ti], g, hp, r, j, p3s.pop(i))
                if (i + 1 == n or steps[i + 1][0] != ti) and ti + 3 < len(todo):
                    bufs[ti + 3] = load(*todo[ti + 3])

            do_qk(0)
            for i in range(n):
                if i + 1 < n:
                    do_qk(i + 1)
                if i >= 1:
                    do_av(i - 1)
            do_av(n - 1)
            trk.barrier()
            trk.release(b_kTh + b_qTh + b_vah + b_ost)

    def ps_all_ap(self, b0, n):
        return self.psall[:, b0 * 512:(b0 + n) * 512]

    def dump(self, src_ap):
        b = Buf("dump")
        yv = self.y_out.rearrange("(a b) c -> a (b c)", b=8)
        for i in range(8):
            for j in range(4):
                self.trk.dma(self.POOL, yv[i * 128:(i + 1) * 128, j * 2048:(j + 1) * 2048], src_ap[i * 128:(i + 1) * 128, j * 2048:(j + 1) * 2048], owner=b)
        self.trk.barrier()

    def build(self):
        import os
        dbg = os.environ.get("KDEBUG", "")
        self.phase_p0()
        if dbg == "p0":
            self.es.close()
            return self.nc
        if dbg == "p1a":
            self.phase_p1a(0)
            self.es.close()
            return self.nc
        if dbg == "p2a":
            self.phase_p1a(0)
            self.phase_p2a(0)
            if _os.environ.get("KTWICE"):
                self.phase_p2a(0)
            self.es.close()
            return self.nc
        if dbg == "b":
            sub = _os.environ.get("KBSUB", "123")
            if "1" in sub:
                self.phase_p1b(1)
            if "2" in sub:
                self.phase_p2b(1)
            if "3" in sub:
                self.phase_p3(1, self.x_in, self.y_out, kindB=True)
            self.es.close()
            return self.nc
        if dbg == "p3a":
            self.phase_p1a(0)
            self.phase_p2a(0)
            self.phase_p3(0, self.x_in, self.y_out, kindB=False)
            self.es.close()
            return self.nc
        cur = self.x_in
        for l in range(self.NL):
            dst = self.y_out if l == self.NL - 1 else self.xs[l % 2]
            if l % 2 == 0:
                self.phase_p1a(l)
                self.phase_p2a(l)
                self.phase_p3(l, cur, dst, kindB=False)
            else:
                self.phase_p1b(l)
                self.phase_p2b(l)
                self.phase_p3(l, cur, dst, kindB=True)
            cur = dst
        self.es.close()
        return self.nc


def rope_table(pos):
    half = 8
    inv = (np.float32(THETA) ** (-np.arange(half, dtype=np.float32) / np.float32(half))).astype(np.float32)
    ang = pos.astype(np.float32)[:, None] * inv[None, :]
    c = np.cos(ang).astype(np.float32)
    s = np.sin(ang).astype(np.float32)
    return np.concatenate([c, c, -s, s], axis=1).astype(np.float32)


def host_consts(is_sample):
    t = np.arange(T_TOK)
    pos = (t % 4096) if is_sample else t
    ropes = []
    for (win, dil) in GROUPS:
        L = T_TOK // dil
        ridx = np.arange(T_TOK)
        r, m = ridx // L, ridx % L
        tok = m * dil + r
        tab = rope_table(pos[tok])
        ropes.append(tab.reshape(NTB, 128, 32).transpose(1, 0, 2).reshape(128, NTB * 32))
    rope = np.stack(ropes).astype(np.float32)
    cm = np.full((128, 1), 0.0 if is_sample else 1.0, np.float32)
    kk = np.arange(128)[:, None]
    cc = np.arange(256)[None, :]
    band = (cc >= kk) & (cc <= kk + 128)
    base = np.where(band, 1.0, 0.0).astype(np.float32)
    hi = base.copy()
    lo = base.copy()
    if is_sample:
        hi[:, 192:] = 0.0
        lo[:, :64] = 0.0
    mb = np.concatenate([base, hi, lo], axis=1).astype(ml_dtypes.bfloat16)
    ident = np.eye(128, dtype=np.float32).astype(ml_dtypes.bfloat16)
    return {"rope": rope, "cm": cm, "mb": mb, "ident": ident}


_PROG_CACHE = {}


def make_in_maps(inputs):
    xp = np.ascontiguousarray(inputs["x_prompt"], dtype=np.float32)
    xs = np.ascontiguousarray(inputs["x_sample"], dtype=np.float32)
    shared = {}
    for i in range(DEPTH):
        shared[f"w_in_{i}"] = np.ascontiguousarray(inputs[f"w_in_{i}"], dtype=np.float32)
        shared[f"w_out_{i}"] = np.ascontiguousarray(inputs[f"w_out_{i}"], dtype=np.float32)
        shared[f"ln_g_{i}"] = np.ascontiguousarray(inputs[f"ln_g_{i}"], dtype=np.float32)
        shared[f"ln_b_{i}"] = np.ascontiguousarray(inputs[f"ln_b_{i}"], dtype=np.float32)
        if i % 2 == 0:
            shared[f"lamv_{i}"] = np.stack([inputs[f"lam_q1_{i}"], inputs[f"lam_k1_{i}"], inputs[f"lam_q2_{i}"], inputs[f"lam_k2_{i}"]]).astype(np.float32)
            shared[f"subln_g_{i}"] = np.ascontiguousarray(inputs[f"subln_g_{i}"], dtype=np.float32)
    cp = host_consts(False)
    cs = host_consts(True)
    in_maps = []
    for c in range(8):
        m = dict(shared)
        if c < 4:
            m["x"] = xp[c]
            m.update(cp)
        else:
            j = c - 4
            m["x"] = np.ascontiguousarray(xs[2 * j:2 * j + 2].reshape(T_TOK, D))
            m.update(cs)
        in_maps.append(m)
    return in_maps


def filter_maps(in_maps):
    if _os.environ.get("KDEBUG"):
        keep = lambda k: not (len(k) > 2 and k[-2] == "_" and k[-1].isdigit() and int(k[-1]) >= {"b": 2}.get(_os.environ["KDEBUG"], 1))
        in_maps = [{k: v for k, v in m.items() if keep(k)} for m in in_maps]
    return in_maps


def run(inputs, NL=4):
    if NL not in _PROG_CACHE:
        _PROG_CACHE[NL] = Prog(NL).build()
    nc = _PROG_CACHE[NL]
    in_maps = filter_maps(make_in_maps(inputs))
    res = run_bass_kernel_spmd(nc, in_maps, core_ids=list(range(8)))
    outs = [r["y"] for r in res.results]
    y_prompt = np.stack(outs[:4]).astype(np.float32)
    y_sample = np.concatenate([o.reshape(2, 4096, D) for o in outs[4:]], axis=0).astype(np.float32)
    return y_prompt, y_sample


def kernel(**inputs):
    return run(inputs, NL=4)
```

```python
import math
from contextlib import ExitStack

import numpy as np
import ml_dtypes
import concourse.bass as bass
import concourse.mybir as mybir
from concourse.bass_utils import run_bass_kernel_spmd

F32 = mybir.dt.float32
BF16 = mybir.dt.bfloat16
AF = mybir.ActivationFunctionType
ALU = mybir.AluOpType

T_TOK = 8192
D = 1024
NTB = T_TOK // 128
NSB = T_TOK // 512
DEPTH = 4
ALPHA = (2.0 * DEPTH) ** 0.25
EPS = 1e-5
THETA = 500000.0
NEG = -30000.0
GROUPS = ((128, 1), (512, 4), (2048, 16))

import os as _os
SMALL = bool(_os.environ.get("KSMALL"))
CFG = dict(p0_tbs=range(NTB), p1_sbs=range(NSB), p2_heads=range(8), p2_qts=range(16), p2_kbs=list(range(NTB)), p3_sbs=range(NSB))
if SMALL:
    CFG = dict(p0_tbs=range(8), p1_sbs=range(2), p2_heads=range(8), p2_qts=[0, 1], p2_kbs=list(range(8)), p3_sbs=range(2))
    if _os.environ.get("KDEBUG") == "b":
        CFG.update(p0_tbs=range(16), p3_sbs=range(4))


class Ev:
    __slots__ = ("sem", "val")

    def __init__(self, sem, val):
        self.sem = sem
        self.val = val


class Buf:
    def __init__(self, name):
        self.name = name
        self.w = None
        self.r = {}
        self.dsem = None


class Sem:
    def __init__(self, h, name):
        self.h = h
        self.name = name
        self.cnt = 0


class Eng:
    def __init__(self, trk, name, eng, relaxed=False):
        self.name = name
        self.eng = eng
        self.sem = trk.new_sem("e_" + name)
        self.waited = {}
        self.pending = []
        self.relaxed = relaxed

    def wait(self, ev):
        assert ev.val is not None, "waiting on unresolved event"
        if self.waited.get(ev.sem, 0) >= ev.val:
            return
        self.eng.wait_ge(ev.sem.h, ev.val)
        self.waited[ev.sem] = ev.val


class Tracker:
    def __init__(self, nc, es):
        self.nc = nc
        self.es = es
        self.sems = []
        self.dsem_pool = []
        self.engs = []

    def new_sem(self, name):
        s = Sem(self.es.enter_context(self.nc.semaphore(name)), name)
        self.sems.append(s)
        return s

    def add_eng(self, name, eng, relaxed=False):
        e = Eng(self, name, eng, relaxed)
        self.engs.append(e)
        return e

    def _deps(self, E, reads, writes):
        for b in reads:
            if b.w is not None:
                self._wait(E, b.w, raw=True)
        for b in writes:
            if b.w is not None:
                self._wait(E, b.w, raw=True)
            for ev in b.r.values():
                self._wait(E, ev, raw=False)

    def _wait(self, E, ev, raw):
        if ev.sem is E.sem:
            if E.relaxed or ev.val is None:
                return
        E.wait(ev)

    def _record(self, ev, reads, writes):
        for b in reads:
            b.r[ev.sem] = ev
        for b in writes:
            b.w = ev
            b.r = {}

    def op(self, E, fn, reads=(), writes=(), inc=True):
        self._deps(E, reads, writes)
        ins = fn()
        if inc:
            E.sem.cnt += 1
            ins.then_inc(E.sem.h, 1)
            ev = Ev(E.sem, E.sem.cnt)
            for (pev, pr, pw) in E.pending:
                pev.val = ev.val
            E.pending = []
            self._record(ev, reads, writes)
        else:
            ev = Ev(E.sem, None)
            E.pending.append((ev, reads, writes))
            self._record(ev, reads, writes)
        return ins

    def get_dsem(self, b):
        if b.dsem is None:
            if self.dsem_pool:
                b.dsem = self.dsem_pool.pop()
            else:
                b.dsem = self.new_sem("d%d" % len(self.sems))
        return b.dsem

    def dma(self, Q, out, in_, reads=(), writes=(), owner=None, **kw):
        self._deps(Q, reads, writes)
        ds = self.get_dsem(owner)
        ins = Q.eng.dma_start(out=out, in_=in_, **kw)
        ds.cnt += 16
        ins.then_inc(ds.h, 16)
        ev = Ev(ds, ds.cnt)
        self._record(ev, reads, writes)
        return ins

    def release(self, bufs):
        for b in bufs:
            if b.dsem is not None:
                self.dsem_pool.append(b.dsem)
                b.dsem = None

    def barrier(self):
        for E in self.engs:
            assert not E.pending, E.name
        for E in self.engs:
            for X in self.engs:
                if X is not E and X.sem.cnt > 0:
                    E.wait(Ev(X.sem, X.sem.cnt))
            for s in self.sems:
                if s.name.startswith("d") and s.cnt > 0:
                    E.wait(Ev(s, s.cnt))


class Prog:
    def __init__(self, NL=4):
        self.NL = NL
        nc = self.nc = bass.Bass("TRN2", target_bir_lowering=False)
        self.es = ExitStack()
        es = self.es
        dt = nc.dram_tensor
        self.x_in = dt("x", [T_TOK, D], F32, kind="ExternalInput").ap()
        self.y_out = dt("y", [T_TOK, D], F32, kind="ExternalOutput").ap()
        self.w_in = []
        self.w_out = []
        self.ln_g = []
        self.ln_b = []
        self.lamv = {}
        self.subg = {}
        self.used_layers = range(DEPTH)
        if _os.environ.get("KDEBUG") in ("p0", "p1a", "p2a", "p3a"):
            self.used_layers = range(1)
        elif _os.environ.get("KDEBUG") == "b":
            self.used_layers = range(2)
        for i in self.used_layers:
            ncol = 4096 if i % 2 == 0 else 10240
            self.w_in.append(dt(f"w_in_{i}", [D, ncol], F32, kind="ExternalInput").ap())
            self.w_out.append(dt(f"w_out_{i}", [D, D], F32, kind="ExternalInput").ap())
            self.ln_g.append(dt(f"ln_g_{i}", [D], F32, kind="ExternalInput").ap())
            self.ln_b.append(dt(f"ln_b_{i}", [D], F32, kind="ExternalInput").ap())
            if i % 2 == 0:
                self.lamv[i] = dt(f"lamv_{i}", [4, 64], F32, kind="ExternalInput").ap()
                self.subg[i] = dt(f"subln_g_{i}", [128], F32, kind="ExternalInput").ap()
        self.rope_d = dt("rope", [3, 128, NTB * 32], F32, kind="ExternalInput").ap()
        self.cm_d = dt("cm", [128, 1], F32, kind="ExternalInput").ap()
        self.mb_d = dt("mb", [128, 3 * 256], BF16, kind="ExternalInput").ap()
        self.ident_d = dt("ident", [128, 128], BF16, kind="ExternalInput").ap()
        self.xs = [dt(f"xs{i}", [T_TOK, D], F32).ap() for i in range(2)]
        self.xT = dt("xT", [D, T_TOK], BF16).ap()
        self.gT = dt("gT", [D, T_TOK], BF16).ap()
        self.qT = dt("qT", [3, 8, 128, T_TOK], BF16).ap()
        self.kT = dt("kT", [3, 8, 128, T_TOK], BF16).ap()
        self.vaA = dt("vaA", [8, 128, NTB * 129], BF16).ap()
        self.vaB = dt("vaB", [3, 8, 128, NTB * 130], BF16).ap()
        self.yT = dt("yT", [8, 128, T_TOK], BF16).ap()
        self.OB = dt("OB", [3, T_TOK, 16 * 65], F32).ap()

        trk = self.trk = Tracker(nc, es)
        self.PE = trk.add_eng("pe", nc.tensor, relaxed=True)
        rlx = bool(_os.environ.get("KRELAX"))
        self.ACT = trk.add_eng("act", nc.scalar, relaxed=rlx)
        self.DVE = trk.add_eng("dve", nc.vector, relaxed=rlx)
        self.POOL = trk.add_eng("pool", nc.gpsimd, relaxed=rlx)
        self.SP = trk.add_eng("sp", nc.sync)
        self.psall = es.enter_context(nc.psum_tensor("psall", [128, 4096], F32))
        self.ps = [self.psall[:, i * 512:(i + 1) * 512] for i in range(8)]
        self.ident = es.enter_context(nc.sbuf_tensor("ident_sb", [128, 128], BF16))
        self.cm = es.enter_context(nc.sbuf_tensor("cm_sb", [128, 1], F32))
        self.mb = es.enter_context(nc.sbuf_tensor("mb_sb", [128, 3 * 256], BF16))
        self.b_const = Buf("const")
        trk.dma(self.SP, self.ident[:], self.ident_d, writes=[self.b_const], owner=self.b_const)
        trk.dma(self.SP, self.cm[:], self.cm_d, writes=[self.b_const], owner=self.b_const)
        trk.dma(self.SP, self.mb[:], self.mb_d, writes=[self.b_const], owner=self.b_const)
        trk.barrier()

    def sb(self, st, name, shape, dtype):
        self._uid = getattr(self, "_uid", 0) + 1
        return st.enter_context(self.nc.sbuf_tensor(f"{name}_u{self._uid}", shape, dtype))

    def emit_xT_block(self, tb, src_ap, src_buf, xbf, b_xbf, tp, b_tp, xTst, b_xTst, cast_eng):
        trk, nc = self.trk, self.nc
        sbi, tb4 = divmod(tb, 4)
        k = tb % 2
        if cast_eng is self.ACT:
            trk.op(self.ACT, lambda: nc.scalar.copy(out=xbf[k][:], in_=src_ap), reads=[src_buf], writes=[b_xbf[k]])
        else:
            trk.op(cast_eng, lambda: cast_eng.eng.tensor_copy(out=xbf[k][:], in_=src_ap), reads=[src_buf], writes=[b_xbf[k]])
        tpv = tp[k]
        for c in range(8):
            trk.op(self.PE, lambda c=c: nc.tensor.transpose(out=tpv[:, c * 128:(c + 1) * 128], in_=xbf[k][:, c * 128:(c + 1) * 128], identity=self.ident[:]),
                   reads=[b_xbf[k], self.b_const], writes=[b_tp[k]], inc=(c == 7))
        s = sbi % 2
        trk.op(self.DVE, lambda: nc.vector.tensor_copy(out=xTst[s][:, :, tb4 * 128:(tb4 + 1) * 128],
                                                       in_=tpv.rearrange("p (c t) -> p c t", c=8)),
               reads=[b_tp[k]], writes=[b_xTst[s]])
        if tb4 == 3:
            trk.dma(getattr(self, "xT_store_q", None) or self.SP, self.xT.rearrange("(c p) t -> p c t", p=128)[:, :, sbi * 512:(sbi + 1) * 512], xTst[s][:],
                    reads=[b_xTst[s]], owner=b_xTst[s])

    def tp_views(self, banks):
        return [self.ps[b][:].bitcast(BF16) for b in banks]

    def phase_p0(self):
        trk, nc = self.trk, self.nc
        with ExitStack() as st:
            xin = [self.sb(st, f"p0x{i}", [128, D], F32) for i in range(2)]
            b_xin = [Buf(f"xin{i}") for i in range(2)]
            xbf = [self.sb(st, f"p0xb{i}", [128, D], BF16) for i in range(2)]
            b_xbf = [Buf(f"xbf{i}") for i in range(2)]
            xTst = [self.sb(st, f"p0st{i}", [128, 8, 512], BF16) for i in range(2)]
            b_xTst = [Buf(f"xTst{i}") for i in range(2)]
            tp = self.tp_views([6, 7])
            b_tp = [Buf("tp0"), Buf("tp1")]
            for tb in CFG['p0_tbs']:
                k = tb % 2
                trk.dma(self.SP, xin[k][:], self.x_in[tb * 128:(tb + 1) * 128, :], writes=[b_xin[k]], owner=b_xin[k])
                self.emit_xT_block(tb, xin[k][:], b_xin[k], xbf, b_xbf, tp, b_tp, xTst, b_xTst, self.POOL)
            trk.barrier()
            trk.release(b_xin + b_xTst)

    def load_w_bf16(self, dst, w_ap, col0, ncols, b_w, st, tag=""):
        trk, nc = self.trk, self.nc
        step = 1024
        if not hasattr(st, "_wstg_t"):
            st._wstg_t = [self.sb(st, f"wstg{i}", [128, step], F32) for i in range(2)]
            st._wstg_b = [Buf(f"wstg{i}") for i in range(2)]
        stg = st._wstg_t
        b_stg = st._wstg_b
        i = 0
        for c in range(8):
            for c0 in range(0, ncols, step):
                n = min(step, ncols - c0)
                k = i % 2
                i += 1
                trk.dma(self.SP, stg[k][:, 0:n], w_ap[c * 128:(c + 1) * 128, col0 + c0:col0 + c0 + n],
                        writes=[b_stg[k]], owner=b_stg[k])
                trk.op(self.POOL, lambda k=k, n=n, c=c, c0=c0: nc.gpsimd.tensor_copy(out=dst[:, c, c0:c0 + n], in_=stg[k][:, 0:n]),
                       reads=[b_stg[k]], writes=[b_w])
        self._wstg = b_stg

    def emit_rope(self, U, b_U, qkb, b_qkb, rtab, b_rt, rb, tmpA, tmpB, b_tmp):
        trk, nc = self.trk, self.nc
        if "r" in _os.environ.get("KSKIP", ""):
            return
        Uv = U.rearrange("p (h e) -> p h e", e=64)
        x16 = Uv[:, :, 0:16]
        cc = rtab[:, rb * 32:rb * 32 + 16]
        ns = rtab[:, rb * 32 + 16:rb * 32 + 24]
        ps_ = rtab[:, rb * 32 + 24:rb * 32 + 32]
        ccb = bass.AP(cc.tensor, cc.offset, [list(cc.ap[0]), [0, 16], [1, 16]])
        nsb = bass.AP(ns.tensor, ns.offset, [list(ns.ap[0]), [0, 16], [1, 8]])
        psb = bass.AP(ps_.tensor, ps_.offset, [list(ps_.ap[0]), [0, 16], [1, 8]])
        tA = tmpA[:].rearrange("p (h e) -> p h e", e=16)
        tB = tmpB[:].rearrange("p (h e) -> p h e", e=16)
        mode = _os.environ.get("KROPE", "1")
        if mode == "1":
            trk.op(self.DVE, lambda: nc.vector.tensor_copy(out=tA, in_=x16), writes=[b_U, b_tmp[0]])
            trk.op(self.DVE, lambda: nc.vector.tensor_tensor(out=tB[:, :, 0:8], in0=tA[:, :, 8:16], in1=nsb, op=ALU.mult), reads=[b_tmp[0], b_rt], writes=[b_tmp[1]])
            trk.op(self.DVE, lambda: nc.vector.tensor_tensor(out=tB[:, :, 8:16], in0=tA[:, :, 0:8], in1=psb, op=ALU.mult), reads=[b_tmp[0], b_rt], writes=[b_tmp[1]])
            trk.op(self.DVE, lambda: nc.vector.tensor_tensor(out=tA, in0=tA, in1=ccb, op=ALU.mult), reads=[b_tmp[0], b_rt], writes=[b_tmp[0]])
        elif mode == "3":
            trk.op(self.DVE, lambda: nc.vector.tensor_copy(out=tA, in_=x16), writes=[b_U, b_tmp[0]])
            return
        elif mode == "4":
            trk.op(self.DVE, lambda: nc.vector.memset(tmpA[:], 0.0), writes=[b_tmp[0]])
            trk.op(self.DVE, lambda: nc.vector.memset(tmpB[:], 0.0), writes=[b_tmp[1]])
        else:
            trk.op(self.DVE, lambda: nc.vector.tensor_copy(out=tA, in_=x16), writes=[b_U, b_tmp[0]])
            trk.op(self.DVE, lambda: nc.vector.tensor_copy(out=tB, in_=x16), writes=[b_U, b_tmp[1]])
        qv = qkb.rearrange("p (h e) -> p h e", e=64)[:, :, 0:16]
        trk.op(self.DVE, lambda: nc.vector.tensor_tensor(out=qv, in0=tA, in1=tB, op=ALU.add), reads=[b_tmp[0], b_tmp[1]], writes=[b_qkb])

    def phase_p1a(self, l):
        trk, nc = self.trk, self.nc
        PE, ACT, DVE, POOL, SP = self.PE, self.ACT, self.DVE, self.POOL, self.SP
        with ExitStack() as st:
            wbf = self.sb(st, "wbf", [128, 8, 4096], BF16)
            b_w = Buf("w")
            self.load_w_bf16(wbf, self.w_in[l], 0, 4096, b_w, st)
            rtab = self.sb(st, "rtab", [128, NTB * 32], F32)
            b_rt = Buf("rt")
            trk.dma(SP, rtab[:], self.rope_d[0], writes=[b_rt], owner=b_rt)
            xTs = [self.sb(st, f"xTs{i}", [128, 8, 512], BF16) for i in range(2)]
            b_xTs = [Buf(f"xTs{i}") for i in range(2)]
            gst = [self.sb(st, f"gst{i}", [128, 8, 512], BF16) for i in range(2)]
            b_gst = [Buf(f"gst{i}") for i in range(2)]
            qst = [self.sb(st, f"qst{i}", [128, 8, 512], BF16) for i in range(2)]
            b_qst = [Buf(f"qst{i}") for i in range(2)]
            kst = [self.sb(st, f"kst{i}", [128, 8, 512], BF16) for i in range(2)]
            b_kst = [Buf(f"kst{i}") for i in range(2)]
            vst = [self.sb(st, f"vst{i}", [128, 8, 4 * 129], BF16) for i in range(2)]
            b_vst = [Buf(f"vst{i}") for i in range(2)]
            qkb = [self.sb(st, f"qkb{i}", [128, 1024], BF16) for i in range(2)]
            b_qkb = [Buf(f"qkb{i}") for i in range(2)]
            tmpA = self.sb(st, "rtA", [128, 256], F32)
            tmpB = self.sb(st, "rtB", [128, 256], F32)
            b_tmp = [Buf("rtA"), Buf("rtB")]
            for i in range(2):
                trk.op(POOL, lambda i=i: nc.gpsimd.memset(vst[i][:], 1.0), writes=[b_vst[i]])
            Ub = [(0, 1), (2, 3)]
            b_U = [Buf("U0"), Buf("U1")]
            b_G = [Buf("G0"), Buf("G1")]
            tp = self.tp_views([6, 7])
            b_tp = [Buf("tp0"), Buf("tp1")]
            ucnt = 0
            qkcnt = 0
            gcnt = 0
            pend_tp = []
            def load_x(sbi):
                trk.dma(SP, xTs[sbi % 2][:], self.xT.rearrange("(c p) t -> p c t", p=128)[:, :, sbi * 512:(sbi + 1) * 512],
                        writes=[b_xTs[sbi % 2]], owner=b_xTs[sbi % 2])
            p1sbs = list(CFG['p1_sbs'])
            load_x(p1sbs[0])
            for si_, sbi in enumerate(p1sbs):
                s = sbi % 2
                if si_ + 1 < len(p1sbs):
                    load_x(p1sbs[si_ + 1])
                for cb in range(8):
                    g = gcnt % 2
                    gcnt += 1
                    G = self.ps[4 + g]
                    for c in range(8):
                        trk.op(PE, lambda c=c: nc.tensor.matmul(G[:], lhsT=wbf[:, c, 3072 + cb * 128:3072 + (cb + 1) * 128],
                                                                rhs=xTs[s][:, c, :], start=(c == 0), stop=(c == 7)),
                               reads=[b_w, b_xTs[s]], writes=[b_G[g]], inc=(c == 7))
                    trk.op(ACT, lambda: nc.scalar.activation(out=gst[s][:, cb, :], in_=G[:], func=AF.Silu),
                           reads=[b_G[g]], writes=[b_gst[s]])
                trk.dma(SP, self.gT.rearrange("(c p) t -> p c t", p=128)[:, :, sbi * 512:(sbi + 1) * 512], gst[s][:],
                        reads=[b_gst[s]], owner=b_gst[s])
                for tb4 in range(4):
                    tb = sbi * 4 + tb4
                    for ui in range(3):
                        u = ucnt % 2
                        ucnt += 1
                        b0, b1 = Ub[u]
                        for n in range(2):
                            for c in range(8):
                                trk.op(PE, lambda n=n, c=c: nc.tensor.matmul(
                                    self.ps[(b0, b1)[n]][:], lhsT=xTs[s][:, c, tb4 * 128:(tb4 + 1) * 128],
                                    rhs=wbf[:, c, ui * 1024 + n * 512: ui * 1024 + (n + 1) * 512], start=(c == 0), stop=(c == 7)),
                                    reads=[b_w, b_xTs[s]], writes=[b_U[u]], inc=(n == 1 and c == 7))
                        Uap = self.pair_ap(b0)
                        while pend_tp:
                            pend_tp.pop(0)()
                        if ui < 2:
                            k = qkcnt % 2
                            qkcnt += 1
                            trk.op(ACT, lambda: nc.scalar.copy(out=qkb[k][:], in_=Uap), writes=[b_U[u], b_qkb[k]])
                            self.emit_rope(Uap, b_U[u], qkb[k][:], b_qkb[k], rtab, b_rt, tb, tmpA, tmpB, b_tmp)
                            def _tp(k=k, ui=ui, s=s, tb4=tb4):
                                tpv = tp[k]
                                for c in range(8):
                                    trk.op(PE, lambda c=c: nc.tensor.transpose(out=tpv[:, c * 128:(c + 1) * 128], in_=qkb[k][:, c * 128:(c + 1) * 128], identity=self.ident[:]),
                                           reads=[b_qkb[k], self.b_const], writes=[b_tp[k]], inc=(c == 7))
                                stg, b_stg = (qst, b_qst) if ui == 0 else (kst, b_kst)
                                trk.op(DVE, lambda: nc.vector.tensor_copy(out=stg[s][:, :, tb4 * 128:(tb4 + 1) * 128],
                                                                          in_=tpv.rearrange("p (c t) -> p c t", c=8)),
                                       reads=[b_tp[k]], writes=[b_stg[s]])
                            pend_tp.append(_tp)
                        else:
                            vv = vst[s][:].rearrange("p h (t e) -> p h t e", e=129)[:, :, tb4, 0:128]
                            trk.op(ACT, lambda: nc.scalar.copy(out=vv, in_=Uap.rearrange("p (h e) -> p h e", e=128)),
                                   reads=[b_U[u]], writes=[b_vst[s]])
                while pend_tp:
                    pend_tp.pop(0)()
                sl = slice(sbi * 512, (sbi + 1) * 512)
                trk.dma(SP, self.qT[0].rearrange("h p t -> p h t")[:, :, sl], qst[s][:], reads=[b_qst[s]], owner=b_qst[s])
                trk.dma(SP, self.kT[0].rearrange("h p t -> p h t")[:, :, sl], kst[s][:], reads=[b_kst[s]], owner=b_kst[s])
                trk.dma(SP, self.vaA.rearrange("h p x -> p h x")[:, :, sbi * 516:(sbi + 1) * 516], vst[s][:], reads=[b_vst[s]], owner=b_vst[s])
            trk.barrier()
            trk.release(self._wstg + [b_rt] + b_xTs + b_gst + b_qst + b_kst + b_vst)

    def pair_ap(self, b0):
        return self.ps_all_ap(b0, 2)

    def phase_p2a(self, l):
        trk, nc = self.trk, self.nc
        PE, ACT, DVE, POOL, SP = self.PE, self.ACT, self.DVE, self.POOL, self.SP
        lam_init = 0.8 - 0.6 * math.exp(-0.3 * l)
        NK = NTB
        with ExitStack() as st:
            kTh = [self.sb(st, f"kTh{i}", [128, T_TOK], BF16) for i in range(2)]
            qTh = [self.sb(st, f"qTh{i}", [128, T_TOK], BF16) for i in range(2)]
            vah = [self.sb(st, f"vah{i}", [128, NK * 129], BF16) for i in range(2)]
            vax = [self.sb(st, f"vax{i}", [128, NK * 129], BF16) for i in range(2)]
            b_kTh = [Buf(f"kTh{i}") for i in range(2)]
            b_qTh = [Buf(f"qTh{i}") for i in range(2)]
            b_vah = [Buf(f"vah{i}") for i in range(2)]
            b_vax = [Buf(f"vax{i}") for i in range(2)]
            ptt = [self.sb(st, f"ptt{b}", [128, 1024], BF16) for b in range(3)]
            pt = [[ptt[b][:, c * 512:(c + 1) * 512] for b in range(3)] for c in range(2)]
            b_pt = [Buf(f"pt{b}") for b in range(3)]
            osb = self.sb(st, "osb", [128, 8 * 129], F32)
            b_osb = Buf("osb")
            dt_ = self.sb(st, "dtmp", [128, 512], F32)
            b_dt = Buf("dtmp")
            junk = self.sb(st, "junk", [128, 128], F32)
            b_junk = Buf("junk")
            sm = self.sb(st, "sm", [128, 32], F32)
            b_rl, b_ssq, b_ln, b_rstd = Buf("rl"), Buf("ssq"), Buf("ln"), Buf("rstd")
            ybf = self.sb(st, "ybf", [128, 512], BF16)
            b_ybf = Buf("ybf")
            yst = [self.sb(st, f"yst{i}", [128, 512], BF16) for i in range(2)]
            b_yst = [Buf(f"yst{i}") for i in range(2)]
            lamt = self.sb(st, "lamt", [128, 4 * 64], F32)
            lams = self.sb(st, "lams", [128, 8], F32)
            b_lam = Buf("lam")
            b_lams = Buf("lams")
            epsb = self.sb(st, "epsb", [128, 1], F32)
            b_eps = Buf("eps")
            trk.op(POOL, lambda: nc.gpsimd.memset(epsb[:], EPS), writes=[b_eps])
            trk.dma(SP, lamt[:], self.lamv[l].rearrange("a b -> (a b)").partition_broadcast(128), writes=[b_lam], owner=b_lam)
            lt = lamt[:].rearrange("p (a b) -> p a b", b=64)
            trk.op(DVE, lambda: nc.vector.tensor_tensor(out=lt[:, 0, :], in0=lt[:, 0, :], in1=lt[:, 1, :], op=ALU.mult), reads=[b_lam], writes=[b_lam])
            trk.op(DVE, lambda: nc.vector.tensor_tensor(out=lt[:, 2, :], in0=lt[:, 2, :], in1=lt[:, 3, :], op=ALU.mult), reads=[b_lam], writes=[b_lam])
            trk.op(DVE, lambda: nc.vector.tensor_reduce(out=lams[:, 0:1], in_=lt[:, 0, :], axis=mybir.AxisListType.X, op=ALU.add), reads=[b_lam], writes=[b_lams])
            trk.op(DVE, lambda: nc.vector.tensor_reduce(out=lams[:, 1:2], in_=lt[:, 2, :], axis=mybir.AxisListType.X, op=ALU.add), reads=[b_lam], writes=[b_lams])
            trk.op(ACT, lambda: nc.scalar.activation(out=lams[:, 2:4], in_=lams[:, 0:2], func=AF.Exp), reads=[b_lams], writes=[b_lams])
            trk.op(DVE, lambda: nc.vector.scalar_tensor_tensor(out=lams[:, 4:5], in0=lams[:, 3:4], scalar=-lam_init, in1=lams[:, 2:3],
                                                               op0=ALU.add, op1=ALU.subtract), reads=[b_lams], writes=[b_lams])
            nlam = lams[:, 4:5]

            S_b = [[0, 2], [1, 3]]
            b_S = [Buf("S0"), Buf("S1")]
            O_b = [4, 5, 6]
            b_O = Buf("O")
            tpv = self.ps[7][:].bitcast(BF16)
            b_tpy = Buf("tpy")

            def acc_ap(c, j):
                a = c * 4 + j
                return self.ps[O_b[a // 3]][:, (a % 3) * 129:(a % 3) * 129 + 129]

            def load_head(h):
                s = h % 2
                TL = 1024 if SMALL else T_TOK
                VL = TL // 128 * 129
                trk.dma(SP, kTh[s][:, :TL], self.kT[0, h][:, :TL], writes=[b_kTh[s]], owner=b_kTh[s])
                trk.dma(SP, vah[s][:, :VL], self.vaA[h][:, :VL], writes=[b_vah[s]], owner=b_vah[s])
                trk.dma(SP, qTh[s][:, :TL], self.qT[0, h][:, :TL], writes=[b_qTh[s]], owner=b_qTh[s])
                trk.op(POOL, lambda: nc.gpsimd.tensor_scalar(out=vax[s][:, :VL], in0=vah[s][:, :VL], scalar1=self.cm[:, 0:1], scalar2=1.0, op0=ALU.mult, op1=ALU.mult),
                       reads=[b_vah[s], self.b_const], writes=[b_vax[s]])

            tcount = [0]
            ycount = [0]

            def qk(h, qt, kb):
                s = h % 2
                t = tcount[0]
                sl = t % 2
                for c in range(2):
                    trk.op(PE, lambda c=c: nc.tensor.matmul(self.ps[S_b[c][sl]][:], lhsT=kTh[s][c * 64:(c + 1) * 64, kb * 128:(kb + 1) * 128],
                                                            rhs=qTh[s][c * 64:(c + 1) * 64, qt * 512:(qt + 1) * 512], start=True, stop=True),
                           reads=[b_kTh[s], b_qTh[s]], writes=[b_S[sl]], inc=(c == 1))
                p3 = t % 3
                trk.op(ACT, lambda: nc.scalar.activation(out=ptt[p3][:], in_=self.psall[:, 2 * sl * 512:(2 * sl + 2) * 512], func=AF.Exp, scale=0.125),
                       reads=[b_S[sl]], writes=[b_pt[p3]])
                tcount[0] += 1
                return p3

            def av(h, qt, kb, p3):
                s = h % 2
                qhalf = (qt * 512) // 4096
                khalf = (kb * 128) // 4096
                vsrc, b_vsrc = (vah[s], b_vah[s]) if qhalf == khalf else (vax[s], b_vax[s])
                for c in range(2):
                    for j in range(4):
                        a = c * 4 + j
                        trk.op(PE, lambda c=c, j=j, a=a: nc.tensor.matmul(acc_ap(c, j), lhsT=pt[c][p3][:, j * 128:(j + 1) * 128],
                                                                          rhs=vsrc[:, kb * 129:(kb + 1) * 129],
                                                                          start=(kb == KBS[0] and a % 3 == 0), stop=(kb == KBS[-1]), skip_group_check=True),
                               reads=[b_pt[p3], b_vsrc], writes=[b_O], inc=(a == 7))

            def epilogue_stages(h, qt):
                def s0():
                    for b in range(3):
                        n = 387 if b < 2 else 258
                        trk.op(DVE, lambda b=b, n=n: nc.vector.tensor_copy(out=osb[:, b * 387:b * 387 + n], in_=self.ps[O_b[b]][:, 0:n]),
                               reads=[b_O], writes=[b_osb])
                ov = osb[:].rearrange("p (a e) -> p a e", e=129)

                def s1():
                    trk.op(DVE, lambda: nc.vector.reciprocal(out=sm[:, 0:8], in_=ov[:, :, 128]), reads=[b_osb], writes=[b_rl])
                    trk.op(DVE, lambda: nc.vector.tensor_scalar(out=sm[:, 8:12], in0=sm[:, 4:8], scalar1=nlam, scalar2=None, op0=ALU.mult),
                           reads=[b_rl, b_lams], writes=[b_rl])
                    for j in range(4):
                        dj = dt_[:, j * 128:(j + 1) * 128]
                        trk.op(DVE, lambda j=j, dj=dj: nc.vector.tensor_scalar(out=dj, in0=ov[:, j, 0:128], scalar1=sm[:, j:j + 1], scalar2=None, op0=ALU.mult),
                               reads=[b_osb, b_rl], writes=[b_dt])
                        trk.op(DVE, lambda j=j, dj=dj: nc.vector.scalar_tensor_tensor(out=dj, in0=ov[:, 4 + j, 0:128], scalar=sm[:, 8 + j:9 + j], in1=dj,
                                                                                      op0=ALU.mult, op1=ALU.add),
                               reads=[b_osb, b_rl, b_dt], writes=[b_dt])
                        trk.op(DVE, lambda j=j, dj=dj: nc.vector.scalar_tensor_tensor(out=junk[:], in0=dj, scalar=1.0, in1=dj, op0=ALU.mult, op1=ALU.mult,
                                                                                      accum_out=sm[:, 12 + j:13 + j]),
                               reads=[b_dt], writes=[b_junk, b_ssq])

                def s2():
                    trk.op(ACT, lambda: nc.scalar.activation(out=sm[:, 16:20], in_=sm[:, 12:16], func=AF.Ln, scale=1.0 / 128.0, bias=epsb[:, 0:1]),
                           reads=[b_ssq, b_eps], writes=[b_ln])
                    trk.op(ACT, lambda: nc.scalar.activation(out=sm[:, 20:24], in_=sm[:, 16:20], func=AF.Exp, scale=-0.5),
                           reads=[b_ln], writes=[b_rstd])

                def s3():
                    for j in range(4):
                        trk.op(DVE, lambda j=j: nc.vector.tensor_scalar(out=ybf[:, j * 128:(j + 1) * 128], in0=dt_[:, j * 128:(j + 1) * 128],
                                                                        scalar1=sm[:, 20 + j:21 + j], scalar2=None, op0=ALU.mult),
                               reads=[b_dt, b_rstd], writes=[b_ybf])
                    for j in range(4):
                        trk.op(PE, lambda j=j: nc.tensor.transpose(out=tpv[:, j * 128:(j + 1) * 128], in_=ybf[:, j * 128:(j + 1) * 128], identity=self.ident[:]),
                               reads=[b_ybf, self.b_const], writes=[b_tpy], inc=(j == 3))

                def s4():
                    y = ycount[0] % 2
                    ycount[0] += 1
                    trk.op(DVE, lambda: nc.vector.tensor_copy(out=yst[y][:], in_=tpv[:, 0:512]), reads=[b_tpy], writes=[b_yst[y]])
                    trk.dma(SP, self.yT[h][:, qt * 512:(qt + 1) * 512], yst[y][:], reads=[b_yst[y]], owner=b_yst[y])
                return [s0, s1, s2, s3, s4]

            heads = list(CFG['p2_heads'])
            KBS = CFG['p2_kbs']
            steps = [(hi, h, qt, kb) for hi, h in enumerate(heads) for qt in CFG['p2_qts'] for kb in KBS]
            n = len(steps)
            load_head(heads[0])
            if len(heads) > 1:
                load_head(heads[1])
            pend = []
            p3s = {}
            sched = (1, 2, 3, 4, 5) if SMALL else (1, 3, 8, 12, 16)

            def do_qk(i):
                hi, h, qt, kb = steps[i]
                p3s[i] = qk(h, qt, kb)

            def do_av(i):
                hi, h, qt, kb = steps[i]
                av(h, qt, kb, p3s.pop(i))
                if (i + 1 == n or steps[i + 1][0] != hi) and hi + 2 < len(heads):
                    load_head(heads[hi + 2])
                ki = KBS.index(kb)
                if kb == KBS[-1]:
                    assert not pend
                    pend.extend(epilogue_stages(h, qt))
                    pend.pop(0)()
                elif pend and ki in sched:
                    pend.pop(0)()

            do_qk(0)
            for i in range(n):
                if i + 1 < n:
                    do_qk(i + 1)
                if i >= 1:
                    do_av(i - 1)
            do_av(n - 1)
            for f in pend:
                f()
            trk.barrier()
            trk.release(b_kTh + b_qTh + b_vah + b_yst + [b_lam])

    def phase_p3(self, l, x_src, x_dst, kindB):
        trk, nc = self.trk, self.nc
        PE, ACT, DVE, POOL, SP = self.PE, self.ACT, self.DVE, self.POOL, self.SP
        lam_init = 0.8 - 0.6 * math.exp(-0.3 * l)
        last = (l == self.NL - 1)
        with ExitStack() as st:
            wob = self.sb(st, "wob", [128, 8, 1024], BF16)
            b_w = Buf("wo")
            self.load_w_bf16(wob, self.w_out[l], 0, 1024, b_w, st)
            gb = self.sb(st, "lng", [128, D], F32)
            bb = self.sb(st, "lnb", [128, D], F32)
            b_gb = Buf("gb")
            b_bb = Buf("bb")
            trk.dma(SP, gb[:], self.ln_g[l].partition_broadcast(128), writes=[b_gb], owner=b_gb)
            trk.dma(SP, bb[:], self.ln_b[l].partition_broadcast(128), writes=[b_bb], owner=b_bb)
            epsb = self.sb(st, "epsb3", [128, 1], F32)
            b_eps = Buf("eps")
            trk.op(POOL, lambda: nc.gpsimd.memset(epsb[:], EPS), writes=[b_eps])
            if not kindB:
                sg = self.sb(st, "sg", [128, 1], F32)
                b_sg = Buf("sg")
                trk.dma(SP, sg[:], self.subg[l].rearrange("(p o) -> p o", o=1), writes=[b_sg], owner=b_sg)
                trk.op(DVE, lambda: nc.vector.tensor_scalar(out=sg[:], in0=sg[:], scalar1=(1.0 - lam_init), scalar2=None, op0=ALU.mult), reads=[b_sg], writes=[b_sg])
                trk.op(DVE, lambda: nc.vector.tensor_scalar(out=wob[:].rearrange("p h n -> p (h n)"), in0=wob[:].rearrange("p h n -> p (h n)"),
                                                            scalar1=sg[:, 0:1], scalar2=None, op0=ALU.mult), reads=[b_w, b_sg], writes=[b_w])
            gTs = [self.sb(st, f"gTs{i}", [128, 8, 512], BF16) for i in range(2)]
            b_gTs = [Buf(f"gTs{i}") for i in range(2)]
            ypT = [self.sb(st, f"ypT{i}", [128, 8, 512], BF16) for i in range(2)]
            b_ypT = [Buf(f"ypT{i}") for i in range(2)]
            if not kindB:
                yTs = [self.sb(st, f"yTs{i}", [128, 8, 512], BF16) for i in range(2)]
                b_yTs = [Buf(f"yTs{i}") for i in range(2)]
            else:
                Og = [[self.sb(st, f"Og{g}{i}", [128, 16 * 65], F32) for i in range(3)] for g in range(3)]
                b_Og = [[Buf(f"Og{g}{i}") for i in range(3)] for g in range(3)]
                rlb = self.sb(st, "rlb", [128, 16], F32)
                b_rlb = Buf("rlb")
                obf = [self.sb(st, f"obf{i}", [128, D], BF16) for i in range(2)]
                b_obf = [Buf(f"obf{i}") for i in range(2)]
            xin = [self.sb(st, f"xin{i}", [128, D], F32) for i in range(3)]
            b_xin = [Buf(f"xin{i}") for i in range(3)]
            z = [self.sb(st, f"z{i}", [128, D], F32) for i in range(2)]
            b_z = [Buf(f"z{i}") for i in range(2)]
            xo = [self.sb(st, f"xo{i}", [128, D], F32) for i in range(2)]
            b_xo = [Buf(f"xo{i}") for i in range(2)]
            stats = self.sb(st, "stats", [128, 16], F32)
            b_stats = Buf("stats")
            mv = self.sb(st, "mv", [128, 8], F32)
            b_mv = Buf("mv")
            xbf = [self.sb(st, f"xbf{i}", [128, D], BF16) for i in range(2)]
            b_xbf = [Buf(f"xbf{i}") for i in range(2)]
            xTst = [self.sb(st, f"xTst{i}", [128, 8, 512], BF16) for i in range(2)]
            b_xTst = [Buf(f"xTst{i}") for i in range(2)]
            b_F = [Buf("F0"), Buf("F1")]
            tp = self.tp_views([4, 5])
            b_tp = [Buf("tp0"), Buf("tp1")]
            tpy = self.tp_views([6, 7])
            b_tpy = [Buf("tpy0"), Buf("tpy1")]
            gTv = self.gT.rearrange("(c p) t -> p c t", p=128)
            pend_x = []
            def issue_sb(sbi):
                s = sbi % 2
                sl = slice(sbi * 512, (sbi + 1) * 512)
                trk.dma(SP, gTs[s][:], gTv[:, :, sl], writes=[b_gTs[s]], owner=b_gTs[s])
                if not kindB:
                    trk.dma(SP, yTs[s][:], self.yT.rearrange("h p t -> p h t")[:, :, sl], writes=[b_yTs[s]], owner=b_yTs[s])

            def issue_tb(tb):
                k = tb % 3
                trk.dma(SP, xin[k][:], x_src[tb * 128:(tb + 1) * 128, :], writes=[b_xin[k]], owner=b_xin[k])
                if kindB:
                    for g, (win, dil) in enumerate(GROUPS):
                        src = self.OB[g].rearrange("(r m) e -> m r e", r=dil)[tb * 128 // dil: tb * 128 // dil + 128 // dil, :, :]
                        trk.dma(SP, Og[g][k][:], src, writes=[b_Og[g][k]], owner=b_Og[g][k])

            self.xT_store_q = ACT
            sbs = list(CFG['p3_sbs'])
            tbs = [(si, sbi, tb4) for si, sbi in enumerate(sbs) for tb4 in range(4)]
            ntb = len(tbs)

            def front(i):
                si, sbi, tb4 = tbs[i]
                s = sbi % 2
                tb = sbi * 4 + tb4
                k = tb % 2
                if tb4 == 0:
                    if si + 1 < len(sbs):
                        issue_sb(sbs[si + 1])
                    if not kindB:
                        trk.op(POOL, lambda: nc.gpsimd.tensor_tensor(out=ypT[s][:], in0=yTs[s][:], in1=gTs[s][:], op=ALU.mult),
                               reads=[b_yTs[s], b_gTs[s]], writes=[b_ypT[s]])
                if kindB:
                    k3 = tb % 3
                    trk.op(POOL, lambda: nc.gpsimd.tensor_tensor(out=Og[0][k3][:], in0=Og[0][k3][:], in1=Og[1][k3][:], op=ALU.add),
                           reads=[b_Og[0][k3], b_Og[1][k3]], writes=[b_Og[0][k3]])
                    trk.op(POOL, lambda: nc.gpsimd.tensor_tensor(out=Og[0][k3][:], in0=Og[0][k3][:], in1=Og[2][k3][:], op=ALU.add),
                           reads=[b_Og[0][k3], b_Og[2][k3]], writes=[b_Og[0][k3]])
                    Uv = Og[0][k3][:].rearrange("p (h e) -> p h e", e=65)
                    trk.op(DVE, lambda: nc.vector.reciprocal(out=rlb[:], in_=Uv[:, :, 64]), reads=[b_Og[0][k3]], writes=[b_rlb])
                    rl3 = bass.AP(rlb[:].tensor, rlb[:].offset, [list(rlb[:].ap[0]), [1, 16], [0, 64]])
                    trk.op(DVE, lambda: nc.vector.tensor_tensor(out=obf[k][:].rearrange("p (h e) -> p h e", e=64), in0=Uv[:, :, 0:64], in1=rl3, op=ALU.mult),
                           reads=[b_Og[0][k3], b_rlb], writes=[b_obf[k]])
                    for c in range(8):
                        trk.op(PE, lambda c=c: nc.tensor.transpose(out=tpy[k][:, c * 128:(c + 1) * 128], in_=obf[k][:, c * 128:(c + 1) * 128], identity=self.ident[:]),
                               reads=[b_obf[k], self.b_const], writes=[b_tpy[k]], inc=(c == 7))
                    trk.op(DVE, lambda: nc.vector.tensor_tensor(out=ypT[s][:, :, tb4 * 128:(tb4 + 1) * 128], in0=tpy[k].rearrange("p (c t) -> p c t", c=8),
                                                                in1=gTs[s][:, :, tb4 * 128:(tb4 + 1) * 128], op=ALU.mult),
                           reads=[b_tpy[k], b_gTs[s]], writes=[b_ypT[s]])
                f = k
                for n in range(2):
                    for h in range(8):
                        trk.op(PE, lambda n=n, h=h: nc.tensor.matmul(self.ps[2 * f + n][:], lhsT=ypT[s][:, h, tb4 * 128:(tb4 + 1) * 128],
                                                                     rhs=wob[:, h, n * 512:(n + 1) * 512], start=(h == 0), stop=(h == 7)),
                               reads=[b_ypT[s], b_w], writes=[b_F[f]], inc=(n == 1 and h == 7))

            def back(i):
                si, sbi, tb4 = tbs[i]
                tb = sbi * 4 + tb4
                k = tb % 2
                f = k
                Fap = self.ps_all_ap(2 * f, 2)
                k3 = tb % 3
                trk.op(DVE, lambda: nc.vector.scalar_tensor_tensor(out=z[k][:], in0=xin[k3][:], scalar=ALPHA, in1=Fap, op0=ALU.mult, op1=ALU.add),
                       reads=[b_xin[k3], b_F[f]], writes=[b_z[k]])
                for n in range(2):
                    trk.op(DVE, lambda n=n: nc.vector.bn_stats(out=stats[:, n * 6:(n + 1) * 6], in_=z[k][:, n * 512:(n + 1) * 512]),
                           reads=[b_z[k]], writes=[b_stats])
                trk.op(DVE, lambda: nc.vector.bn_aggr(out=mv[:, 0:2], in_=stats[:, 0:12]), reads=[b_stats], writes=[b_mv])
                trk.op(ACT, lambda: nc.scalar.activation(out=mv[:, 2:3], in_=mv[:, 1:2], func=AF.Ln, bias=epsb[:, 0:1]), reads=[b_mv, b_eps], writes=[b_mv])
                trk.op(ACT, lambda: nc.scalar.activation(out=mv[:, 3:4], in_=mv[:, 2:3], func=AF.Exp, scale=-0.5), reads=[b_mv], writes=[b_mv])
                trk.op(DVE, lambda: nc.vector.tensor_scalar(out=z[k][:], in0=z[k][:], scalar1=mv[:, 0:1], scalar2=mv[:, 3:4], op0=ALU.subtract, op1=ALU.mult),
                       reads=[b_z[k], b_mv], writes=[b_z[k]])
                trk.op(DVE, lambda: nc.vector.tensor_tensor(out=z[k][:], in0=z[k][:], in1=gb[:], op=ALU.mult), reads=[b_z[k], b_gb], writes=[b_z[k]])
                trk.op(POOL, lambda: nc.gpsimd.tensor_tensor(out=xo[k][:], in0=z[k][:], in1=bb[:], op=ALU.add), reads=[b_z[k], b_bb], writes=[b_xo[k]])
                trk.dma(ACT, x_dst[tb * 128:(tb + 1) * 128, :], xo[k][:], reads=[b_xo[k]], owner=b_xo[k])
                while pend_x:
                    pend_x.pop(0)()
                if not last:
                    pend_x.append(lambda tb=tb, k=k: self.emit_xT_block(tb, xo[k][:], b_xo[k], xbf, b_xbf, tp, b_tp, xTst, b_xTst, self.ACT))

            issue_sb(sbs[0])
            issue_tb(tbs[0][1] * 4 + tbs[0][2])
            if ntb > 1:
                issue_tb(tbs[1][1] * 4 + tbs[1][2])
            front(0)
            for i in range(ntb):
                if i + 2 < ntb:
                    issue_tb(tbs[i + 2][1] * 4 + tbs[i + 2][2])
                if i + 1 < ntb:
                    front(i + 1)
                back(i)
            while pend_x:
                pend_x.pop(0)()
            self.xT_store_q = None
            trk.barrier()
            rel = self._wstg + [b_gb, b_bb] + b_gTs + b_xin + b_xo + b_xTst
            if kindB:
                rel += [b for g in b_Og for b in g]
            else:
                rel += b_yTs + [b_sg]
            trk.release(rel)

    def phase_p1b(self, l):
        trk, nc = self.trk, self.nc
        PE, ACT, DVE, POOL, SP = self.PE, self.ACT, self.DVE, self.POOL, self.SP
        with ExitStack() as st:
            wbf = self.sb(st, "wbfB", [128, 8, 3072], BF16)
            b_w = Buf("w")
            wg = self.sb(st, "wgB", [128, 8, 1024], BF16)
            b_wg = Buf("wg")
            rtab = self.sb(st, "rtabB", [128, NTB * 32], F32)
            b_rt = Buf("rt")
            xTw = [self.sb(st, f"xTw{i}", [128, 8, 2048], BF16) for i in range(2)]
            b_xTw = [Buf(f"xTw{i}") for i in range(2)]
            gst = self.sb(st, "gstB", [128, 8, 512], BF16)
            b_gst = Buf("gst")
            qst = self.sb(st, "qstB", [128, 8, 512], BF16)
            b_qst = Buf("qst")
            kst = self.sb(st, "kstB", [128, 8, 512], BF16)
            b_kst = Buf("kst")
            vst = self.sb(st, "vstB", [128, 8, 4, 130], BF16)
            b_vst = Buf("vst")
            qkb = [self.sb(st, f"qkbB{i}", [128, 1024], BF16) for i in range(2)]
            b_qkb = [Buf(f"qkb{i}") for i in range(2)]
            tmpA = self.sb(st, "rtAB", [128, 256], F32)
            tmpB = self.sb(st, "rtBB", [128, 256], F32)
            b_tmp = [Buf("rtA"), Buf("rtB")]
            trk.op(POOL, lambda: nc.gpsimd.memset(vst[:], 1.0), writes=[b_vst])
            Ub = [(0, 1), (2, 3)]
            b_U = [Buf("U0"), Buf("U1")]
            b_G = [Buf("G0"), Buf("G1")]
            tp = self.tp_views([6, 7])
            b_tp = [Buf("tp0"), Buf("tp1")]
            ucnt = 0
            qkcnt = 0
            gcnt = 0
            wcnt = 0
            pend_tp = []
            self.load_w_bf16(wg, self.w_in[l], 9216, 1024, b_wg, st, tag="g")
            wst_all = list(self._wstg)
            xTv = self.xT.rearrange("(c p) t -> p c t", p=128)
            gTv = self.gT.rearrange("(c p) t -> p c t", p=128)
            windows = range(1) if SMALL else range(4)
            for g, (win, dil) in enumerate(GROUPS):
                L = T_TOK // dil
                nkt = L // 128
                self.load_w_bf16(wbf, self.w_in[l], g * 3072, 3072, b_w, st, tag=f"w{g}")
                wst_all += list(self._wstg)
                trk.dma(SP, rtab[:], self.rope_d[g], writes=[b_rt], owner=b_rt)
                for w in windows:
                    xw = wcnt % 2
                    wcnt += 1
                    if wcnt == 1:
                        trk.dma(SP, xTw[xw][:], xTv[:, :, w * 2048:(w + 1) * 2048], writes=[b_xTw[xw]], owner=b_xTw[xw])
                    wl = list(windows)
                    nxt_w = wl[wl.index(w) + 1] if wl.index(w) + 1 < len(wl) else (wl[0] if g < 2 else None)
                    if nxt_w is not None:
                        nx = wcnt % 2
                        trk.dma(SP, xTw[nx][:], xTv[:, :, nxt_w * 2048:(nxt_w + 1) * 2048], writes=[b_xTw[nx]], owner=b_xTw[nx])
                    if g == 0:
                        for sb4 in range(4):
                            sbi = w * 4 + sb4
                            for cb in range(8):
                                gi = gcnt % 2
                                gcnt += 1
                                G = self.ps[4 + gi]
                                for c in range(8):
                                    trk.op(PE, lambda c=c, cb=cb, G=G: nc.tensor.matmul(G, lhsT=wg[:, c, cb * 128:(cb + 1) * 128],
                                                                                        rhs=xTw[xw][:, c, sb4 * 512:(sb4 + 1) * 512], start=(c == 0), stop=(c == 7)),
                                           reads=[b_wg, b_xTw[xw]], writes=[b_G[gi]], inc=(c == 7))
                                trk.op(ACT, lambda cb=cb, G=G: nc.scalar.activation(out=gst[:, cb, :], in_=G, func=AF.Silu),
                                       reads=[b_G[gi]], writes=[b_gst])
                            trk.dma(SP, gTv[:, :, sbi * 512:(sbi + 1) * 512], gst[:], reads=[b_gst], owner=b_gst)
                    nj = 16 // dil
                    for r in range(dil):
                        for j in range(nj):
                            rb = r * nkt + (w * 2048 // dil) // 128 + j
                            slot = j % 4
                            run_end = (slot == 3) or (j == nj - 1)
                            tok0 = r + dil * 128 * j
                            for ui in range(3):
                                u = ucnt % 2
                                ucnt += 1
                                b0, b1 = Ub[u]
                                for n in range(2):
                                    for c in range(8):
                                        lw = xTw[xw][:, c, tok0:tok0 + (127 * dil + 1):dil] if dil > 1 else xTw[xw][:, c, tok0:tok0 + 128]
                                        trk.op(PE, lambda n=n, c=c, lw=lw, b0=b0, b1=b1, ui=ui: nc.tensor.matmul(
                                            self.ps[(b0, b1)[n]], lhsT=lw,
                                            rhs=wbf[:, c, ui * 1024 + n * 512: ui * 1024 + (n + 1) * 512], start=(c == 0), stop=(c == 7)),
                                            reads=[b_w, b_xTw[xw]], writes=[b_U[u]], inc=(n == 1 and c == 7))
                                Uap = self.pair_ap(b0)
                                while pend_tp:
                                    pend_tp.pop(0)()
                                if ui < 2:
                                    k = qkcnt % 2
                                    qkcnt += 1
                                    trk.op(ACT, lambda k=k, Uap=Uap: nc.scalar.copy(out=qkb[k][:], in_=Uap), writes=[b_U[u], b_qkb[k]])
                                    self.emit_rope(Uap, b_U[u], qkb[k][:], b_qkb[k], rtab, b_rt, rb, tmpA, tmpB, b_tmp)
                                    def _tp(k=k, ui=ui, slot=slot):
                                        tpv = tp[k]
                                        for c in range(8):
                                            trk.op(PE, lambda c=c: nc.tensor.transpose(out=tpv[:, c * 128:(c + 1) * 128], in_=qkb[k][:, c * 128:(c + 1) * 128], identity=self.ident[:]),
                                                   reads=[b_qkb[k], self.b_const], writes=[b_tp[k]], inc=(c == 7))
                                        stg, b_stg = (qst, b_qst) if ui == 0 else (kst, b_kst)
                                        trk.op(DVE, lambda: nc.vector.tensor_copy(out=stg[:, :, slot * 128:(slot + 1) * 128],
                                                                                  in_=tpv.rearrange("p (c t) -> p c t", c=8)),
                                               reads=[b_tp[k]], writes=[b_stg])
                                    pend_tp.append(_tp)
                                else:
                                    vv = vst[:, :, slot, :].rearrange("p h (a e) -> p h a e", e=65)[:, :, :, 0:64]
                                    trk.op(ACT, lambda vv=vv, Uap=Uap: nc.scalar.copy(out=vv, in_=Uap.rearrange("p (h a e) -> p h a e", a=2, e=64)),
                                           reads=[b_U[u]], writes=[b_vst])
                            if run_end:
                                while pend_tp:
                                    pend_tp.pop(0)()
                                nb = slot + 1
                                rb0 = rb - slot
                                trk.dma(SP, self.qT[g].rearrange("h p t -> p h t")[:, :, rb0 * 128:(rb0 + nb) * 128], qst[:, :, 0:nb * 128], reads=[b_qst], owner=b_qst)
                                trk.dma(SP, self.kT[g].rearrange("h p t -> p h t")[:, :, rb0 * 128:(rb0 + nb) * 128], kst[:, :, 0:nb * 128], reads=[b_kst], owner=b_kst)
                                trk.dma(SP, self.vaB[g].rearrange("h p (b x) -> p h b x", x=130)[:, :, rb0:rb0 + nb, :], vst[:, :, 0:nb, :], reads=[b_vst], owner=b_vst)
            trk.barrier()
            trk.release(wst_all + [b_rt, b_gst, b_qst, b_kst, b_vst] + b_xTw)

    def phase_p2b(self, l):
        trk, nc = self.trk, self.nc
        PE, ACT, DVE, POOL, SP = self.PE, self.ACT, self.DVE, self.POOL, self.SP
        with ExitStack() as st:
            kTh = [self.sb(st, f"kThB{i}", [128, T_TOK], BF16) for i in range(3)]
            qTh = [self.sb(st, f"qThB{i}", [128, T_TOK], BF16) for i in range(3)]
            vah = [self.sb(st, f"vahB{i}", [128, NTB, 130], BF16) for i in range(3)]
            b_kTh = [Buf(f"kTh{i}") for i in range(3)]
            b_qTh = [Buf(f"qTh{i}") for i in range(3)]
            b_vah = [Buf(f"vah{i}") for i in range(3)]
            pt = [self.sb(st, f"ptB{i}", [128, 2, 256], BF16) for i in range(3)]
            b_pt = [Buf(f"pt{i}") for i in range(3)]
            ost = [self.sb(st, f"ostB{i}", [128, 130], F32) for i in range(4)]
            b_ost = [Buf(f"ost{i}") for i in range(4)]
            b_S = [Buf("S0"), Buf("S1")]
            b_O = [Buf("O0"), Buf("O1"), Buf("O2")]
            O_b = [4, 5, 6]
            tcount = [0]
            ocount = [0]
            lcount = [0]
            TL = 2048 if SMALL else T_TOK

            def load(g, hp):
                s = lcount[0] % 3
                lcount[0] += 1
                win, dil = GROUPS[g]
                Lf = T_TOK // dil
                Ll = TL // dil
                pieces = [(r * Lf, r * Lf + Ll) for r in range(dil)] if SMALL else [(0, T_TOK)]
                for (a, b) in pieces:
                    trk.dma(SP, kTh[s][:, a:b], self.kT[g, hp][:, a:b], writes=[b_kTh[s]], owner=b_kTh[s])
                    trk.dma(SP, qTh[s][:, a:b], self.qT[g, hp][:, a:b], writes=[b_qTh[s]], owner=b_qTh[s])
                    trk.dma(SP, vah[s][:, a // 128:b // 128, :], self.vaB[g, hp].rearrange("p (b x) -> p b x", x=130)[:, a // 128:b // 128, :],
                            writes=[b_vah[s]], owner=b_vah[s])
                return s

            def tile_geom(g, r, j):
                win, dil = GROUPS[g]
                Lf = T_TOK // dil
                nktf = Lf // 128
                nkt = (TL // dil) // 128
                c0 = 0 if j > 0 else 64
                c1 = 256 if j < nkt - 1 else 192
                Lh = Lf // 2
                var = 0
                if j == Lh // 128 - 1:
                    var = 1
                elif j == Lh // 128:
                    var = 2
                return Lf, nktf, nkt, c0, c1, var

            def qk(s, g, r, j):
                L, nktf, nkt, c0, c1, var = tile_geom(g, r, j)
                t = tcount[0]
                tcount[0] += 1
                sl = t % 2
                kt = r * nktf + j
                q0 = r * L + 128 * j - 64 + c0
                nco = c1 - c0
                for h2 in range(2):
                    Sb = self.ps[2 * sl + h2][:, 0:nco]
                    trk.op(PE, lambda h2=h2, Sb=Sb: nc.tensor.matmul(Sb, lhsT=kTh[s][h2 * 64:(h2 + 1) * 64, kt * 128:(kt + 1) * 128],
                                                                    rhs=qTh[s][h2 * 64:(h2 + 1) * 64, q0:q0 + nco], start=True, stop=True),
                           reads=[b_kTh[s], b_qTh[s]], writes=[b_S[sl]], inc=(h2 == 1))
                p3 = t % 3
                Sv = self.psall[:, 2 * sl * 512:(2 * sl + 2) * 512].rearrange("p (b n) -> p b n", b=2)[:, :, 0:nco]
                trk.op(ACT, lambda: nc.scalar.activation(out=pt[p3][:, :, 0:nco], in_=Sv, func=AF.Exp, scale=0.125),
                       reads=[b_S[sl]], writes=[b_pt[p3]])
                mk = self.mb[:, var * 256 + c0:var * 256 + c1]
                mk3 = bass.AP(mk.tensor, mk.offset, [list(mk.ap[0]), [0, 2], [1, nco]])
                trk.op(DVE, lambda: nc.vector.tensor_tensor(out=pt[p3][:, :, 0:nco], in0=pt[p3][:, :, 0:nco], in1=mk3, op=ALU.mult),
                       reads=[self.b_const], writes=[b_pt[p3]])
                return p3

            def av(s, g, hp, r, j, p3):
                L, nktf, nkt, c0, c1, var = tile_geom(g, r, j)
                kt = r * nktf + j
                parts = []
                parts.append((j - 1, c0, 128, j == 0, True))
                parts.append((j, 128, c1, True, j == nkt - 1))
                for (ch, a0, a1, first, lastc) in parts:
                    rows = a1 - a0
                    ob = (ch + 1) % 3
                    for h2 in range(2):
                        trk.op(PE, lambda h2=h2, a0=a0, a1=a1, rows=rows, ob=ob, first=first: nc.tensor.matmul(
                            self.ps[O_b[ob]][0:rows, h2 * 65:(h2 + 1) * 65], lhsT=pt[p3][:, h2, a0 - c0:a1 - c0],
                            rhs=vah[s][:, kt, h2 * 65:(h2 + 1) * 65], start=(first and h2 == 0), stop=lastc, skip_group_check=True),
                            reads=[b_pt[p3], b_vah[s]], writes=[b_O[ob]], inc=(h2 == 1))
                    if lastc:
                        o = ocount[0] % 4
                        ocount[0] += 1
                        trk.op(DVE, lambda rows=rows, ob=ob, o=o: nc.vector.tensor_copy(out=ost[o][0:rows, :], in_=self.ps[O_b[ob]][0:rows, 0:130]),
                               reads=[b_O[ob]], writes=[b_ost[o]])
                        ridx0 = r * L + (128 * ch + 64 if ch >= 0 else 0)
                        if ch == nkt - 1:
                            ridx0 = r * L + nkt * 128 - 64
                        trk.dma(SP, self.OB[g][ridx0:ridx0 + rows, hp * 130:(hp + 1) * 130], ost[o][0:rows, :], reads=[b_ost[o]], owner=b_ost[o])

            todo = [(g, hp) for g in range(3) for hp in range(8)]
            steps = []
            for ti, (g, hp) in enumerate(todo):
                win, dil = GROUPS[g]
                nkt = (TL // dil) // 128
                for r in range(dil):
                    for j in range(nkt):
                        steps.append((ti, g, hp, r, j))
            n = len(steps)
            bufs = {}
            for ti0 in range(min(3, len(todo))):
                bufs[ti0] = load(*todo[ti0])
            p3s = {}

            def do_qk(i):
                ti, g, hp, r, j = steps[i]
                p3s[i] = qk(bufs[ti], g, r, j)

            def do_av(i):
                ti, g, hp, r, j = steps[i]
                av(bufs[ti], g, hp, r, j, p3s.pop(i))
                if (i + 1 == n or steps[i + 1][0] != ti) and ti + 3 < len(todo):
                    bufs[ti + 3] = load(*todo[ti + 3])

            do_qk(0)
            for i in range(n):
                if i + 1 < n:
                    do_qk(i + 1)
                if i >= 1:
                    do_av(i - 1)
            do_av(n - 1)
            trk.barrier()
            trk.release(b_kTh + b_qTh + b_vah + b_ost)

    def ps_all_ap(self, b0, n):
        return self.psall[:, b0 * 512:(b0 + n) * 512]

    def dump(self, src_ap):
        b = Buf("dump")
        yv = self.y_out.rearrange("(a b) c -> a (b c)", b=8)
        for i in range(8):
            for j in range(4):
                self.trk.dma(self.POOL, yv[i * 128:(i + 1) * 128, j * 2048:(j + 1) * 2048], src_ap[i * 128:(i + 1) * 128, j * 2048:(j + 1) * 2048], owner=b)
        self.trk.barrier()

    def build(self):
        import os
        dbg = os.environ.get("KDEBUG", "")
        self.phase_p0()
        if dbg == "p0":
            self.es.close()
            return self.nc
        if dbg == "p1a":
            self.phase_p1a(0)
            self.es.close()
            return self.nc
        if dbg == "p2a":
            self.phase_p1a(0)
            self.phase_p2a(0)
            if _os.environ.get("KTWICE"):
                self.phase_p2a(0)
            self.es.close()
            return self.nc
        if dbg == "b":
            sub = _os.environ.get("KBSUB", "123")
            if "1" in sub:
                self.phase_p1b(1)
            if "2" in sub:
                self.phase_p2b(1)
            if "3" in sub:
                self.phase_p3(1, self.x_in, self.y_out, kindB=True)
            self.es.close()
            return self.nc
        if dbg == "p3a":
            self.phase_p1a(0)
            self.phase_p2a(0)
            self.phase_p3(0, self.x_in, self.y_out, kindB=False)
            self.es.close()
            return self.nc
        cur = self.x_in
        for l in range(self.NL):
            dst = self.y_out if l == self.NL - 1 else self.xs[l % 2]
            if l % 2 == 0:
                self.phase_p1a(l)
                self.phase_p2a(l)
                self.phase_p3(l, cur, dst, kindB=False)
            else:
                self.phase_p1b(l)
                self.phase_p2b(l)
                self.phase_p3(l, cur, dst, kindB=True)
            cur = dst
        self.es.close()
        return self.nc


def rope_table(pos):
    half = 8
    inv = (np.float32(THETA) ** (-np.arange(half, dtype=np.float32) / np.float32(half))).astype(np.float32)
    ang = pos.astype(np.float32)[:, None] * inv[None, :]
    c = np.cos(ang).astype(np.float32)
    s = np.sin(ang).astype(np.float32)
    return np.concatenate([c, c, -s, s], axis=1).astype(np.float32)


def host_consts(is_sample):
    t = np.arange(T_TOK)
    pos = (t % 4096) if is_sample else t
    ropes = []
    for (win, dil) in GROUPS:
        L = T_TOK // dil
        ridx = np.arange(T_TOK)
        r, m = ridx // L, ridx % L
        tok = m * dil + r
        tab = rope_table(pos[tok])
        ropes.append(tab.reshape(NTB, 128, 32).transpose(1, 0, 2).reshape(128, NTB * 32))
    rope = np.stack(ropes).astype(np.float32)
    cm = np.full((128, 1), 0.0 if is_sample else 1.0, np.float32)
    kk = np.arange(128)[:, None]
    cc = np.arange(256)[None, :]
    band = (cc >= kk) & (cc <= kk + 128)
    base = np.where(band, 1.0, 0.0).astype(np.float32)
    hi = base.copy()
    lo = base.copy()
    if is_sample:
        hi[:, 192:] = 0.0
        lo[:, :64] = 0.0
    mb = np.concatenate([base, hi, lo], axis=1).astype(ml_dtypes.bfloat16)
    ident = np.eye(128, dtype=np.float32).astype(ml_dtypes.bfloat16)
    return {"rope": rope, "cm": cm, "mb": mb, "ident": ident}


_PROG_CACHE = {}


def make_in_maps(inputs):
    xp = np.ascontiguousarray(inputs["x_prompt"], dtype=np.float32)
    xs = np.ascontiguousarray(inputs["x_sample"], dtype=np.float32)
    shared = {}
    for i in range(DEPTH):
        shared[f"w_in_{i}"] = np.ascontiguousarray(inputs[f"w_in_{i}"], dtype=np.float32)
        shared[f"w_out_{i}"] = np.ascontiguousarray(inputs[f"w_out_{i}"], dtype=np.float32)
        shared[f"ln_g_{i}"] = np.ascontiguousarray(inputs[f"ln_g_{i}"], dtype=np.float32)
        shared[f"ln_b_{i}"] = np.ascontiguousarray(inputs[f"ln_b_{i}"], dtype=np.float32)
        if i % 2 == 0:
            shared[f"lamv_{i}"] = np.stack([inputs[f"lam_q1_{i}"], inputs[f"lam_k1_{i}"], inputs[f"lam_q2_{i}"], inputs[f"lam_k2_{i}"]]).astype(np.float32)
            shared[f"subln_g_{i}"] = np.ascontiguousarray(inputs[f"subln_g_{i}"], dtype=np.float32)
    cp = host_consts(False)
    cs = host_consts(True)
    in_maps = []
    for c in range(8):
        m = dict(shared)
        if c < 4:
            m["x"] = xp[c]
            m.update(cp)
        else:
            j = c - 4
            m["x"] = np.ascontiguousarray(xs[2 * j:2 * j + 2].reshape(T_TOK, D))
            m.update(cs)
        in_maps.append(m)
    return in_maps


def filter_maps(in_maps):
    if _os.environ.get("KDEBUG"):
        keep = lambda k: not (len(k) > 2 and k[-2] == "_" and k[-1].isdigit() and int(k[-1]) >= {"b": 2}.get(_os.environ["KDEBUG"], 1))
        in_maps = [{k: v for k, v in m.items() if keep(k)} for m in in_maps]
    return in_maps


def run(inputs, NL=4):
    if NL not in _PROG_CACHE:
        _PROG_CACHE[NL] = Prog(NL).build()
    nc = _PROG_CACHE[NL]
    in_maps = filter_maps(make_in_maps(inputs))
    res = run_bass_kernel_spmd(nc, in_maps, core_ids=list(range(8)))
    outs = [r["y"] for r in res.results]
    y_prompt = np.stack(outs[:4]).astype(np.float32)
    y_sample = np.concatenate([o.reshape(2, 4096, D) for o in outs[4:]], axis=0).astype(np.float32)
    return y_prompt, y_sample


def kernel(**inputs):
    return run(inputs, NL=4)
```

```python
import math
from contextlib import ExitStack

import numpy as np
import ml_dtypes
import concourse.bass as bass
import concourse.mybir as mybir
from concourse.bass_utils import run_bass_kernel_spmd

F32 = mybir.dt.float32
BF16 = mybir.dt.bfloat16
AF = mybir.ActivationFunctionType
ALU = mybir.AluOpType

T_TOK = 8192
D = 1024
NTB = T_TOK // 128
NSB = T_TOK // 512
DEPTH = 4
ALPHA = (2.0 * DEPTH) ** 0.25
EPS = 1e-5
THETA = 500000.0
NEG = -30000.0
GROUPS = ((128, 1), (512, 4), (2048, 16))

import os as _os
SMALL = bool(_os.environ.get("KSMALL"))
CFG = dict(p0_tbs=range(NTB), p1_sbs=range(NSB), p2_heads=range(8), p2_qts=range(16), p2_kbs=list(range(NTB)), p3_sbs=range(NSB))
if SMALL:
    CFG = dict(p0_tbs=range(8), p1_sbs=range(2), p2_heads=range(8), p2_qts=[0, 1], p2_kbs=list(range(8)), p3_sbs=range(2))
    if _os.environ.get("KDEBUG") == "b":
        CFG.update(p0_tbs=range(16), p3_sbs=range(4))


class Ev:
    __slots__ = ("sem", "val")

    def __init__(self, sem, val):
        self.sem = sem
        self.val = val


class Buf:
    def __init__(self, name):
        self.name = name
        self.w = None
        self.r = {}
        self.dsem = None


class Sem:
    def __init__(self, h, name):
        self.h = h
        self.name = name
        self.cnt = 0


class Eng:
    def __init__(self, trk, name, eng, relaxed=False):
        self.name = name
        self.eng = eng
        self.sem = trk.new_sem("e_" + name)
        self.waited = {}
        self.pending = []
        self.relaxed = relaxed

    def wait(self, ev):
        assert ev.val is not None, "waiting on unresolved event"
        if self.waited.get(ev.sem, 0) >= ev.val:
            return
        self.eng.wait_ge(ev.sem.h, ev.val)
        self.waited[ev.sem] = ev.val


class Tracker:
    def __init__(self, nc, es):
        self.nc = nc
        self.es = es
        self.sems = []
        self.dsem_pool = []
        self.engs = []

    def new_sem(self, name):
        s = Sem(self.es.enter_context(self.nc.semaphore(name)), name)
        self.sems.append(s)
        return s

    def add_eng(self, name, eng, relaxed=False):
        e = Eng(self, name, eng, relaxed)
        self.engs.append(e)
        return e

    def _deps(self, E, reads, writes):
        for b in reads:
            if b.w is not None:
                self._wait(E, b.w, raw=True)
        for b in writes:
            if b.w is not None:
                self._wait(E, b.w, raw=True)
            for ev in b.r.values():
                self._wait(E, ev, raw=False)

    def _wait(self, E, ev, raw):
        if ev.sem is E.sem:
            if E.relaxed or ev.val is None:
                return
        E.wait(ev)

    def _record(self, ev, reads, writes):
        for b in reads:
            b.r[ev.sem] = ev
        for b in writes:
            b.w = ev
            b.r = {}

    def op(self, E, fn, reads=(), writes=(), inc=True):
        self._deps(E, reads, writes)
        ins = fn()
        if inc:
            E.sem.cnt += 1
            ins.then_inc(E.sem.h, 1)
            ev = Ev(E.sem, E.sem.cnt)
            for (pev, pr, pw) in E.pending:
                pev.val = ev.val
            E.pending = []
            self._record(ev, reads, writes)
        else:
            ev = Ev(E.sem, None)
            E.pending.append((ev, reads, writes))
            self._record(ev, reads, writes)
        return ins

    def get_dsem(self, b):
        if b.dsem is None:
            if self.dsem_pool:
                b.dsem = self.dsem_pool.pop()
            else:
                b.dsem = self.new_sem("d%d" % len(self.sems))
        return b.dsem

    def dma(self, Q, out, in_, reads=(), writes=(), owner=None, **kw):
        self._deps(Q, reads, writes)
        ds = self.get_dsem(owner)
        ins = Q.eng.dma_start(out=out, in_=in_, **kw)
        ds.cnt += 16
        ins.then_inc(ds.h, 16)
        ev = Ev(ds, ds.cnt)
        self._record(ev, reads, writes)
        return ins

    def release(self, bufs):
        for b in bufs:
            if b.dsem is not None:
                self.dsem_pool.append(b.dsem)
                b.dsem = None

    def barrier(self):
        for E in self.engs:
            assert not E.pending, E.name
        for E in self.engs:
            for X in self.engs:
                if X is not E and X.sem.cnt > 0:
                    E.wait(Ev(X.sem, X.sem.cnt))
            for s in self.sems:
                if s.name.startswith("d") and s.cnt > 0:
                    E.wait(Ev(s, s.cnt))


class Prog:
    def __init__(self, NL=4):
        self.NL = NL
        nc = self.nc = bass.Bass("TRN2", target_bir_lowering=False)
        self.es = ExitStack()
        es = self.es
        dt = nc.dram_tensor
        self.x_in = dt("x", [T_TOK, D], F32, kind="ExternalInput").ap()
        self.y_out = dt("y", [T_TOK, D], F32, kind="ExternalOutput").ap()
        self.w_in = []
        self.w_out = []
        self.ln_g = []
        self.ln_b = []
        self.lamv = {}
        self.subg = {}
        self.used_layers = range(DEPTH)
        if _os.environ.get("KDEBUG") in ("p0", "p1a", "p2a", "p3a"):
            self.used_layers = range(1)
        elif _os.environ.get("KDEBUG") == "b":
            self.used_layers = range(2)
        for i in self.used_layers:
            ncol = 4096 if i % 2 == 0 else 10240
            self.w_in.append(dt(f"w_in_{i}", [D, ncol], F32, kind="ExternalInput").ap())
            self.w_out.append(dt(f"w_out_{i}", [D, D], F32, kind="ExternalInput").ap())
            self.ln_g.append(dt(f"ln_g_{i}", [D], F32, kind="ExternalInput").ap())
            self.ln_b.append(dt(f"ln_b_{i}", [D], F32, kind="ExternalInput").ap())
            if i % 2 == 0:
                self.lamv[i] = dt(f"lamv_{i}", [4, 64], F32, kind="ExternalInput").ap()
                self.subg[i] = dt(f"subln_g_{i}", [128], F32, kind="ExternalInput").ap()
        self.rope_d = dt("rope", [3, 128, NTB * 32], F32, kind="ExternalInput").ap()
        self.cm_d = dt("cm", [128, 1], F32, kind="ExternalInput").ap()
        self.mb_d = dt("mb", [128, 3 * 256], BF16, kind="ExternalInput").ap()
        self.ident_d = dt("ident", [128, 128], BF16, kind="ExternalInput").ap()
        self.xs = [dt(f"xs{i}", [T_TOK, D], F32).ap() for i in range(2)]
        self.xT = dt("xT", [D, T_TOK], BF16).ap()
        self.gT = dt("gT", [D, T_TOK], BF16).ap()
        self.qT = dt("qT", [3, 8, 128, T_TOK], BF16).ap()
        self.kT = dt("kT", [3, 8, 128, T_TOK], BF16).ap()
        self.vaA = dt("vaA", [8, 128, NTB * 129], BF16).ap()
        self.vaB = dt("vaB", [3, 8, 128, NTB * 130], BF16).ap()
        self.yT = dt("yT", [8, 128, T_TOK], BF16).ap()
        self.OB = dt("OB", [3, T_TOK, 16 * 65], F32).ap()

        trk = self.trk = Tracker(nc, es)
        self.PE = trk.add_eng("pe", nc.tensor, relaxed=True)
        rlx = bool(_os.environ.get("KRELAX"))
        self.ACT = trk.add_eng("act", nc.scalar, relaxed=rlx)
        self.DVE = trk.add_eng("dve", nc.vector, relaxed=rlx)
        self.POOL = trk.add_eng("pool", nc.gpsimd, relaxed=rlx)
        self.SP = trk.add_eng("sp", nc.sync)
        self.psall = es.enter_context(nc.psum_tensor("psall", [128, 4096], F32))
        self.ps = [self.psall[:, i * 512:(i + 1) * 512] for i in range(8)]
        self.ident = es.enter_context(nc.sbuf_tensor("ident_sb", [128, 128], BF16))
        self.cm = es.enter_context(nc.sbuf_tensor("cm_sb", [128, 1], F32))
        self.mb = es.enter_context(nc.sbuf_tensor("mb_sb", [128, 3 * 256], BF16))
        self.b_const = Buf("const")
        trk.dma(self.SP, self.ident[:], self.ident_d, writes=[self.b_const], owner=self.b_const)
        trk.dma(self.SP, self.cm[:], self.cm_d, writes=[self.b_const], owner=self.b_const)
        trk.dma(self.SP, self.mb[:], self.mb_d, writes=[self.b_const], owner=self.b_const)
        trk.barrier()

    def sb(self, st, name, shape, dtype):
        self._uid = getattr(self, "_uid", 0) + 1
        return st.enter_context(self.nc.sbuf_tensor(f"{name}_u{self._uid}", shape, dtype))

    def emit_xT_block(self, tb, src_ap, src_buf, xbf, b_xbf, tp, b_tp, xTst, b_xTst, cast_eng):
        trk, nc = self.trk, self.nc
        sbi, tb4 = divmod(tb, 4)
        k = tb % 2
        if cast_eng is self.ACT:
            trk.op(self.ACT, lambda: nc.scalar.copy(out=xbf[k][:], in_=src_ap), reads=[src_buf], writes=[b_xbf[k]])
        else:
            trk.op(cast_eng, lambda: cast_eng.eng.tensor_copy(out=xbf[k][:], in_=src_ap), reads=[src_buf], writes=[b_xbf[k]])
        tpv = tp[k]
        for c in range(8):
            trk.op(self.PE, lambda c=c: nc.tensor.transpose(out=tpv[:, c * 128:(c + 1) * 128], in_=xbf[k][:, c * 128:(c + 1) * 128], identity=self.ident[:]),
                   reads=[b_xbf[k], self.b_const], writes=[b_tp[k]], inc=(c == 7))
        s = sbi % 2
        trk.op(self.DVE, lambda: nc.vector.tensor_copy(out=xTst[s][:, :, tb4 * 128:(tb4 + 1) * 128],
                                                       in_=tpv.rearrange("p (c t) -> p c t", c=8)),
               reads=[b_tp[k]], writes=[b_xTst[s]])
        if tb4 == 3:
            trk.dma(getattr(self, "xT_store_q", None) or self.SP, self.xT.rearrange("(c p) t -> p c t", p=128)[:, :, sbi * 512:(sbi + 1) * 512], xTst[s][:],
                    reads=[b_xTst[s]], owner=b_xTst[s])

    def tp_views(self, banks):
        return [self.ps[b][:].bitcast(BF16) for b in banks]

    def phase_p0(self):
        trk, nc = self.trk, self.nc
        with ExitStack() as st:
            xin = [self.sb(st, f"p0x{i}", [128, D], F32) for i in range(2)]
            b_xin = [Buf(f"xin{i}") for i in range(2)]
            xbf = [self.sb(st, f"p0xb{i}", [128, D], BF16) for i in range(2)]
            b_xbf = [Buf(f"xbf{i}") for i in range(2)]
            xTst = [self.sb(st, f"p0st{i}", [128, 8, 512], BF16) for i in range(2)]
            b_xTst = [Buf(f"xTst{i}") for i in range(2)]
            tp = self.tp_views([6, 7])
            b_tp = [Buf("tp0"), Buf("tp1")]
            for tb in CFG['p0_tbs']:
                k = tb % 2
                trk.dma(self.SP, xin[k][:], self.x_in[tb * 128:(tb + 1) * 128, :], writes=[b_xin[k]], owner=b_xin[k])
                self.emit_xT_block(tb, xin[k][:], b_xin[k], xbf, b_xbf, tp, b_tp, xTst, b_xTst, self.ACT)
            trk.barrier()
            trk.release(b_xin + b_xTst)

    def load_w_bf16(self, dst, w_ap, col0, ncols, b_w, st, tag=""):
        trk, nc = self.trk, self.nc
        step = 1024
        if not hasattr(st, "_wstg_t"):
            st._wstg_t = [self.sb(st, f"wstg{i}", [128, step], F32) for i in range(2)]
            st._wstg_b = [Buf(f"wstg{i}") for i in range(2)]
        stg = st._wstg_t
        b_stg = st._wstg_b
        i = 0
        for c in range(8):
            for c0 in range(0, ncols, step):
                n = min(step, ncols - c0)
                k = i % 2
                i += 1
                trk.dma(self.SP, stg[k][:, 0:n], w_ap[c * 128:(c + 1) * 128, col0 + c0:col0 + c0 + n],
                        writes=[b_stg[k]], owner=b_stg[k])
                trk.op(self.POOL, lambda k=k, n=n, c=c, c0=c0: nc.gpsimd.tensor_copy(out=dst[:, c, c0:c0 + n], in_=stg[k][:, 0:n]),
                       reads=[b_stg[k]], writes=[b_w])
        self._wstg = b_stg

    def emit_rope(self, U, b_U, qkb, b_qkb, rtab, b_rt, rb, tmpA, tmpB, b_tmp):
        trk, nc = self.trk, self.nc
        if "r" in _os.environ.get("KSKIP", ""):
            return
        Uv = U.rearrange("p (h e) -> p h e", e=64)
        x16 = Uv[:, :, 0:16]
        cc = rtab[:, rb * 32:rb * 32 + 16]
        ns = rtab[:, rb * 32 + 16:rb * 32 + 24]
        ps_ = rtab[:, rb * 32 + 24:rb * 32 + 32]
        ccb = bass.AP(cc.tensor, cc.offset, [list(cc.ap[0]), [0, 16], [1, 16]])
        nsb = bass.AP(ns.tensor, ns.offset, [list(ns.ap[0]), [0, 16], [1, 8]])
        psb = bass.AP(ps_.tensor, ps_.offset, [list(ps_.ap[0]), [0, 16], [1, 8]])
        tA = tmpA[:].rearrange("p (h e) -> p h e", e=16)
        tB = tmpB[:].rearrange("p (h e) -> p h e", e=16)
        mode = _os.environ.get("KROPE", "1")
        if mode == "1":
            trk.op(self.DVE, lambda: nc.vector.tensor_copy(out=tA, in_=x16), writes=[b_U, b_tmp[0]])
            trk.op(self.DVE, lambda: nc.vector.tensor_tensor(out=tB[:, :, 0:8], in0=tA[:, :, 8:16], in1=nsb, op=ALU.mult), reads=[b_tmp[0], b_rt], writes=[b_tmp[1]])
            trk.op(self.DVE, lambda: nc.vector.tensor_tensor(out=tB[:, :, 8:16], in0=tA[:, :, 0:8], in1=psb, op=ALU.mult), reads=[b_tmp[0], b_rt], writes=[b_tmp[1]])
            trk.op(self.DVE, lambda: nc.vector.tensor_tensor(out=tA, in0=tA, in1=ccb, op=ALU.mult), reads=[b_tmp[0], b_rt], writes=[b_tmp[0]])
        elif mode == "3":
            trk.op(self.DVE, lambda: nc.vector.tensor_copy(out=tA, in_=x16), writes=[b_U, b_tmp[0]])
            return
        elif mode == "4":
            trk.op(self.DVE, lambda: nc.vector.memset(tmpA[:], 0.0), writes=[b_tmp[0]])
            trk.op(self.DVE, lambda: nc.vector.memset(tmpB[:], 0.0), writes=[b_tmp[1]])
        else:
            trk.op(self.DVE, lambda: nc.vector.tensor_copy(out=tA, in_=x16), writes=[b_U, b_tmp[0]])
            trk.op(self.DVE, lambda: nc.vector.tensor_copy(out=tB, in_=x16), writes=[b_U, b_tmp[1]])
        qv = qkb.rearrange("p (h e) -> p h e", e=64)[:, :, 0:16]
        trk.op(self.DVE, lambda: nc.vector.tensor_tensor(out=qv, in0=tA, in1=tB, op=ALU.add), reads=[b_tmp[0], b_tmp[1]], writes=[b_qkb])

    def phase_p1a(self, l):
        trk, nc = self.trk, self.nc
        PE, ACT, DVE, POOL, SP = self.PE, self.ACT, self.DVE, self.POOL, self.SP
        with ExitStack() as st:
            wbf = self.sb(st, "wbf", [128, 8, 4096], BF16)
            b_w = Buf("w")
            self.load_w_bf16(wbf, self.w_in[l], 0, 4096, b_w, st)
            rtab = self.sb(st, "rtab", [128, NTB * 32], F32)
            b_rt = Buf("rt")
            trk.dma(SP, rtab[:], self.rope_d[0], writes=[b_rt], owner=b_rt)
            xTs = [self.sb(st, f"xTs{i}", [128, 8, 512], BF16) for i in range(2)]
            b_xTs = [Buf(f"xTs{i}") for i in range(2)]
            gst = [self.sb(st, f"gst{i}", [128, 8, 512], BF16) for i in range(2)]
            b_gst = [Buf(f"gst{i}") for i in range(2)]
            qst = [self.sb(st, f"qst{i}", [128, 8, 512], BF16) for i in range(2)]
            b_qst = [Buf(f"qst{i}") for i in range(2)]
            kst = [self.sb(st, f"kst{i}", [128, 8, 512], BF16) for i in range(2)]
            b_kst = [Buf(f"kst{i}") for i in range(2)]
            vst = [self.sb(st, f"vst{i}", [128, 8, 4 * 129], BF16) for i in range(2)]
            b_vst = [Buf(f"vst{i}") for i in range(2)]
            qkb = [self.sb(st, f"qkb{i}", [128, 1024], BF16) for i in range(2)]
            b_qkb = [Buf(f"qkb{i}") for i in range(2)]
            tmpA = self.sb(st, "rtA", [128, 256], F32)
            tmpB = self.sb(st, "rtB", [128, 256], F32)
            b_tmp = [Buf("rtA"), Buf("rtB")]
            for i in range(2):
                trk.op(POOL, lambda i=i: nc.gpsimd.memset(vst[i][:], 1.0), writes=[b_vst[i]])
            Ub = [(0, 1), (2, 3)]
            b_U = [Buf("U0"), Buf("U1")]
            b_G = [Buf("G0"), Buf("G1")]
            tp = self.tp_views([6, 7])
            b_tp = [Buf("tp0"), Buf("tp1")]
            ucnt = 0
            qkcnt = 0
            gcnt = 0
            pend_tp = []
            def load_x(sbi):
                trk.dma(SP, xTs[sbi % 2][:], self.xT.rearrange("(c p) t -> p c t", p=128)[:, :, sbi * 512:(sbi + 1) * 512],
                        writes=[b_xTs[sbi % 2]], owner=b_xTs[sbi % 2])
            p1sbs = list(CFG['p1_sbs'])
            load_x(p1sbs[0])
            for si_, sbi in enumerate(p1sbs):
                s = sbi % 2
                if si_ + 1 < len(p1sbs):
                    load_x(p1sbs[si_ + 1])
                for cb in range(8):
                    g = gcnt % 2
                    gcnt += 1
                    G = self.ps[4 + g]
                    for c in range(8):
                        trk.op(PE, lambda c=c: nc.tensor.matmul(G[:], lhsT=wbf[:, c, 3072 + cb * 128:3072 + (cb + 1) * 128],
                                                                rhs=xTs[s][:, c, :], start=(c == 0), stop=(c == 7)),
                               reads=[b_w, b_xTs[s]], writes=[b_G[g]], inc=(c == 7))
                    trk.op(ACT, lambda: nc.scalar.activation(out=gst[s][:, cb, :], in_=G[:], func=AF.Silu),
                           reads=[b_G[g]], writes=[b_gst[s]])
                trk.dma(SP, self.gT.rearrange("(c p) t -> p c t", p=128)[:, :, sbi * 512:(sbi + 1) * 512], gst[s][:],
                        reads=[b_gst[s]], owner=b_gst[s])
                for tb4 in range(4):
                    tb = sbi * 4 + tb4
                    for ui in range(3):
                        u = ucnt % 2
                        ucnt += 1
                        b0, b1 = Ub[u]
                        for n in range(2):
                            for c in range(8):
                                trk.op(PE, lambda n=n, c=c: nc.tensor.matmul(
                                    self.ps[(b0, b1)[n]][:], lhsT=xTs[s][:, c, tb4 * 128:(tb4 + 1) * 128],
                                    rhs=wbf[:, c, ui * 1024 + n * 512: ui * 1024 + (n + 1) * 512], start=(c == 0), stop=(c == 7)),
                                    reads=[b_w, b_xTs[s]], writes=[b_U[u]], inc=(n == 1 and c == 7))
                        Uap = self.pair_ap(b0)
                        while pend_tp:
                            pend_tp.pop(0)()
                        if ui < 2:
                            k = qkcnt % 2
                            qkcnt += 1
                            trk.op(ACT, lambda: nc.scalar.copy(out=qkb[k][:], in_=Uap), writes=[b_U[u], b_qkb[k]])
                            self.emit_rope(Uap, b_U[u], qkb[k][:], b_qkb[k], rtab, b_rt, tb, tmpA, tmpB, b_tmp)
                            def _tp(k=k, ui=ui, s=s, tb4=tb4):
                                tpv = tp[k]
                                for c in range(8):
                                    trk.op(PE, lambda c=c: nc.tensor.transpose(out=tpv[:, c * 128:(c + 1) * 128], in_=qkb[k][:, c * 128:(c + 1) * 128], identity=self.ident[:]),
                                           reads=[b_qkb[k], self.b_const], writes=[b_tp[k]], inc=(c == 7))
                                stg, b_stg = (qst, b_qst) if ui == 0 else (kst, b_kst)
                                trk.op(DVE, lambda: nc.vector.tensor_copy(out=stg[s][:, :, tb4 * 128:(tb4 + 1) * 128],
                                                                          in_=tpv.rearrange("p (c t) -> p c t", c=8)),
                                       reads=[b_tp[k]], writes=[b_stg[s]])
                            pend_tp.append(_tp)
                        else:
                            vv = vst[s][:].rearrange("p h (t e) -> p h t e", e=129)[:, :, tb4, 0:128]
                            trk.op(ACT, lambda: nc.scalar.copy(out=vv, in_=Uap.rearrange("p (h e) -> p h e", e=128)),
                                   reads=[b_U[u]], writes=[b_vst[s]])
                while pend_tp:
                    pend_tp.pop(0)()
                sl = slice(sbi * 512, (sbi + 1) * 512)
                trk.dma(SP, self.qT[0].rearrange("h p t -> p h t")[:, :, sl], qst[s][:], reads=[b_qst[s]], owner=b_qst[s])
                trk.dma(SP, self.kT[0].rearrange("h p t -> p h t")[:, :, sl], kst[s][:], reads=[b_kst[s]], owner=b_kst[s])
                trk.dma(SP, self.vaA.rearrange("h p x -> p h x")[:, :, sbi * 516:(sbi + 1) * 516], vst[s][:], reads=[b_vst[s]], owner=b_vst[s])
            trk.barrier()
            trk.release(self._wstg + [b_rt] + b_xTs + b_gst + b_qst + b_kst + b_vst)

    def pair_ap(self, b0):
        return self.ps_all_ap(b0, 2)

    def phase_p2a(self, l):
        trk, nc = self.trk, self.nc
        PE, ACT, DVE, POOL, SP = self.PE, self.ACT, self.DVE, self.POOL, self.SP
        lam_init = 0.8 - 0.6 * math.exp(-0.3 * l)
        NK = NTB
        with ExitStack() as st:
            kTh = [self.sb(st, f"kTh{i}", [128, T_TOK], BF16) for i in range(2)]
            qTh = [self.sb(st, f"qTh{i}", [128, T_TOK], BF16) for i in range(2)]
            vah = [self.sb(st, f"vah{i}", [128, NK * 129], BF16) for i in range(2)]
            vax = [self.sb(st, f"vax{i}", [128, NK * 129], BF16) for i in range(2)]
            b_kTh = [Buf(f"kTh{i}") for i in range(2)]
            b_qTh = [Buf(f"qTh{i}") for i in range(2)]
            b_vah = [Buf(f"vah{i}") for i in range(2)]
            b_vax = [Buf(f"vax{i}") for i in range(2)]
            ptt = [self.sb(st, f"ptt{b}", [128, 1024], BF16) for b in range(3)]
            pt = [[ptt[b][:, c * 512:(c + 1) * 512] for b in range(3)] for c in range(2)]
            b_pt = [Buf(f"pt{b}") for b in range(3)]
            osb = self.sb(st, "osb", [128, 8 * 129], F32)
            b_osb = Buf("osb")
            dt_ = self.sb(st, "dtmp", [128, 512], F32)
            b_dt = Buf("dtmp")
            junk = self.sb(st, "junk", [128, 128], F32)
            b_junk = Buf("junk")
            sm = self.sb(st, "sm", [128, 32], F32)
            b_rl, b_ssq, b_ln, b_rstd = Buf("rl"), Buf("ssq"), Buf("ln"), Buf("rstd")
            ybf = self.sb(st, "ybf", [128, 512], BF16)
            b_ybf = Buf("ybf")
            yst = [self.sb(st, f"yst{i}", [128, 512], BF16) for i in range(2)]
            b_yst = [Buf(f"yst{i}") for i in range(2)]
            lamt = self.sb(st, "lamt", [128, 4 * 64], F32)
            lams = self.sb(st, "lams", [128, 8], F32)
            b_lam = Buf("lam")
            b_lams = Buf("lams")
            epsb = self.sb(st, "epsb", [128, 1], F32)
            b_eps = Buf("eps")
            trk.op(POOL, lambda: nc.gpsimd.memset(epsb[:], EPS), writes=[b_eps])
            trk.dma(SP, lamt[:], self.lamv[l].rearrange("a b -> (a b)").partition_broadcast(128), writes=[b_lam], owner=b_lam)
            lt = lamt[:].rearrange("p (a b) -> p a b", b=64)
            trk.op(DVE, lambda: nc.vector.tensor_tensor(out=lt[:, 0, :], in0=lt[:, 0, :], in1=lt[:, 1, :], op=ALU.mult), reads=[b_lam], writes=[b_lam])
            trk.op(DVE, lambda: nc.vector.tensor_tensor(out=lt[:, 2, :], in0=lt[:, 2, :], in1=lt[:, 3, :], op=ALU.mult), reads=[b_lam], writes=[b_lam])
            trk.op(DVE, lambda: nc.vector.tensor_reduce(out=lams[:, 0:1], in_=lt[:, 0, :], axis=mybir.AxisListType.X, op=ALU.add), reads=[b_lam], writes=[b_lams])
            trk.op(DVE, lambda: nc.vector.tensor_reduce(out=lams[:, 1:2], in_=lt[:, 2, :], axis=mybir.AxisListType.X, op=ALU.add), reads=[b_lam], writes=[b_lams])
            trk.op(ACT, lambda: nc.scalar.activation(out=lams[:, 2:4], in_=lams[:, 0:2], func=AF.Exp), reads=[b_lams], writes=[b_lams])
            trk.op(DVE, lambda: nc.vector.scalar_tensor_tensor(out=lams[:, 4:5], in0=lams[:, 3:4], scalar=-lam_init, in1=lams[:, 2:3],
                                                               op0=ALU.add, op1=ALU.subtract), reads=[b_lams], writes=[b_lams])
            nlam = lams[:, 4:5]

            S_b = [[0, 2], [1, 3]]
            b_S = [Buf("S0"), Buf("S1")]
            O_b = [4, 5, 6]
            b_O = Buf("O")
            tpv = self.ps[7][:].bitcast(BF16)
            b_tpy = Buf("tpy")

            def acc_ap(c, j):
                a = c * 4 + j
                return self.ps[O_b[a // 3]][:, (a % 3) * 129:(a % 3) * 129 + 129]

            def load_head(h):
                s = h % 2
                TL = 1024 if SMALL else T_TOK
                VL = TL // 128 * 129
                trk.dma(SP, kTh[s][:, :TL], self.kT[0, h][:, :TL], writes=[b_kTh[s]], owner=b_kTh[s])
                trk.dma(SP, vah[s][:, :VL], self.vaA[h][:, :VL], writes=[b_vah[s]], owner=b_vah[s])
                trk.dma(SP, qTh[s][:, :TL], self.qT[0, h][:, :TL], writes=[b_qTh[s]], owner=b_qTh[s])
                trk.op(POOL, lambda: nc.gpsimd.tensor_scalar(out=vax[s][:, :VL], in0=vah[s][:, :VL], scalar1=self.cm[:, 0:1], scalar2=1.0, op0=ALU.mult, op1=ALU.mult),
                       reads=[b_vah[s], self.b_const], writes=[b_vax[s]])

            tcount = [0]
            ycount = [0]

            def qk(h, qt, kb):
                s = h % 2
                t = tcount[0]
                sl = t % 2
                for c in range(2):
                    trk.op(PE, lambda c=c: nc.tensor.matmul(self.ps[S_b[c][sl]][:], lhsT=kTh[s][c * 64:(c + 1) * 64, kb * 128:(kb + 1) * 128],
                                                            rhs=qTh[s][c * 64:(c + 1) * 64, qt * 512:(qt + 1) * 512], start=True, stop=True),
                           reads=[b_kTh[s], b_qTh[s]], writes=[b_S[sl]], inc=(c == 1))
                p3 = t % 3
                trk.op(ACT, lambda: nc.scalar.activation(out=ptt[p3][:], in_=self.psall[:, 2 * sl * 512:(2 * sl + 2) * 512], func=AF.Exp, scale=0.125),
                       reads=[b_S[sl]], writes=[b_pt[p3]])
                tcount[0] += 1
                return p3

            def av(h, qt, kb, p3):
                s = h % 2
                qhalf = (qt * 512) // 4096
                khalf = (kb * 128) // 4096
                vsrc, b_vsrc = (vah[s], b_vah[s]) if qhalf == khalf else (vax[s], b_vax[s])
                for c in range(2):
                    for j in range(4):
                        a = c * 4 + j
                        trk.op(PE, lambda c=c, j=j, a=a: nc.tensor.matmul(acc_ap(c, j), lhsT=pt[c][p3][:, j * 128:(j + 1) * 128],
                                                                          rhs=vsrc[:, kb * 129:(kb + 1) * 129],
                                                                          start=(kb == KBS[0] and a % 3 == 0), stop=(kb == KBS[-1]), skip_group_check=True),
                               reads=[b_pt[p3], b_vsrc], writes=[b_O], inc=(a == 7))

            def epilogue_stages(h, qt):
                def s0():
                    for b in range(3):
                        n = 387 if b < 2 else 258
                        trk.op(DVE, lambda b=b, n=n: nc.vector.tensor_copy(out=osb[:, b * 387:b * 387 + n], in_=self.ps[O_b[b]][:, 0:n]),
                               reads=[b_O], writes=[b_osb])
                ov = osb[:].rearrange("p (a e) -> p a e", e=129)

                def s1():
                    trk.op(DVE, lambda: nc.vector.reciprocal(out=sm[:, 0:8], in_=ov[:, :, 128]), reads=[b_osb], writes=[b_rl])
                    trk.op(DVE, lambda: nc.vector.tensor_scalar(out=sm[:, 8:12], in0=sm[:, 4:8], scalar1=nlam, scalar2=None, op0=ALU.mult),
                           reads=[b_rl, b_lams], writes=[b_rl])
                    for j in range(4):
                        dj = dt_[:, j * 128:(j + 1) * 128]
                        trk.op(DVE, lambda j=j, dj=dj: nc.vector.tensor_scalar(out=dj, in0=ov[:, j, 0:128], scalar1=sm[:, j:j + 1], scalar2=None, op0=ALU.mult),
                               reads=[b_osb, b_rl], writes=[b_dt])
                        trk.op(DVE, lambda j=j, dj=dj: nc.vector.scalar_tensor_tensor(out=dj, in0=ov[:, 4 + j, 0:128], scalar=sm[:, 8 + j:9 + j], in1=dj,
                                                                                      op0=ALU.mult, op1=ALU.add),
                               reads=[b_osb, b_rl, b_dt], writes=[b_dt])
                        trk.op(DVE, lambda j=j, dj=dj: nc.vector.scalar_tensor_tensor(out=junk[:], in0=dj, scalar=1.0, in1=dj, op0=ALU.mult, op1=ALU.mult,
                                                                                      accum_out=sm[:, 12 + j:13 + j]),
                               reads=[b_dt], writes=[b_junk, b_ssq])

                def s2():
                    trk.op(ACT, lambda: nc.scalar.activation(out=sm[:, 16:20], in_=sm[:, 12:16], func=AF.Ln, scale=1.0 / 128.0, bias=epsb[:, 0:1]),
                           reads=[b_ssq, b_eps], writes=[b_ln])
                    trk.op(ACT, lambda: nc.scalar.activation(out=sm[:, 20:24], in_=sm[:, 16:20], func=AF.Exp, scale=-0.5),
                           reads=[b_ln], writes=[b_rstd])

                def s3():
                    for j in range(4):
                        trk.op(DVE, lambda j=j: nc.vector.tensor_scalar(out=ybf[:, j * 128:(j + 1) * 128], in0=dt_[:, j * 128:(j + 1) * 128],
                                                                        scalar1=sm[:, 20 + j:21 + j], scalar2=None, op0=ALU.mult),
                               reads=[b_dt, b_rstd], writes=[b_ybf])
                    for j in range(4):
                        trk.op(PE, lambda j=j: nc.tensor.transpose(out=tpv[:, j * 128:(j + 1) * 128], in_=ybf[:, j * 128:(j + 1) * 128], identity=self.ident[:]),
                               reads=[b_ybf, self.b_const], writes=[b_tpy], inc=(j == 3))

                def s4():
                    y = ycount[0] % 2
                    ycount[0] += 1
                    trk.op(DVE, lambda: nc.vector.tensor_copy(out=yst[y][:], in_=tpv[:, 0:512]), reads=[b_tpy], writes=[b_yst[y]])
                    trk.dma(SP, self.yT[h][:, qt * 512:(qt + 1) * 512], yst[y][:], reads=[b_yst[y]], owner=b_yst[y])
                return [s0, s1, s2, s3, s4]

            heads = list(CFG['p2_heads'])
            KBS = CFG['p2_kbs']
            steps = [(hi, h, qt, kb) for hi, h in enumerate(heads) for qt in CFG['p2_qts'] for kb in KBS]
            n = len(steps)
            load_head(heads[0])
            if len(heads) > 1:
                load_head(heads[1])
            pend = []
            p3s = {}
            sched = (1, 2, 3, 4, 5) if SMALL else (1, 3, 8, 12, 16)

            def do_qk(i):
                hi, h, qt, kb = steps[i]
                p3s[i] = qk(h, qt, kb)

            def do_av(i):
                hi, h, qt, kb = steps[i]
                av(h, qt, kb, p3s.pop(i))
                if (i + 1 == n or steps[i + 1][0] != hi) and hi + 2 < len(heads):
                    load_head(heads[hi + 2])
                ki = KBS.index(kb)
                if kb == KBS[-1]:
                    assert not pend
                    pend.extend(epilogue_stages(h, qt))
                    pend.pop(0)()
                elif pend and ki in sched:
                    pend.pop(0)()

            do_qk(0)
            for i in range(n):
                if i + 1 < n:
                    do_qk(i + 1)
                if i >= 1:
                    do_av(i - 1)
            do_av(n - 1)
            for f in pend:
                f()
            trk.barrier()
            trk.release(b_kTh + b_qTh + b_vah + b_yst + [b_lam])

    def phase_p3(self, l, x_src, x_dst, kindB):
        trk, nc = self.trk, self.nc
        PE, ACT, DVE, POOL, SP = self.PE, self.ACT, self.DVE, self.POOL, self.SP
        lam_init = 0.8 - 0.6 * math.exp(-0.3 * l)
        last = (l == self.NL - 1)
        with ExitStack() as st:
            wob = self.sb(st, "wob", [128, 8, 1024], BF16)
            b_w = Buf("wo")
            self.load_w_bf16(wob, self.w_out[l], 0, 1024, b_w, st)
            gb = self.sb(st, "lng", [128, D], F32)
            bb = self.sb(st, "lnb", [128, D], F32)
            b_gb = Buf("gb")
            b_bb = Buf("bb")
            trk.dma(SP, gb[:], self.ln_g[l].partition_broadcast(128), writes=[b_gb], owner=b_gb)
            trk.dma(SP, bb[:], self.ln_b[l].partition_broadcast(128), writes=[b_bb], owner=b_bb)
            epsb = self.sb(st, "epsb3", [128, 1], F32)
            b_eps = Buf("eps")
            trk.op(POOL, lambda: nc.gpsimd.memset(epsb[:], EPS), writes=[b_eps])
            if not kindB:
                sg = self.sb(st, "sg", [128, 1], F32)
                b_sg = Buf("sg")
                trk.dma(SP, sg[:], self.subg[l].rearrange("(p o) -> p o", o=1), writes=[b_sg], owner=b_sg)
                trk.op(DVE, lambda: nc.vector.tensor_scalar(out=sg[:], in0=sg[:], scalar1=(1.0 - lam_init), scalar2=None, op0=ALU.mult), reads=[b_sg], writes=[b_sg])
                trk.op(DVE, lambda: nc.vector.tensor_scalar(out=wob[:].rearrange("p h n -> p (h n)"), in0=wob[:].rearrange("p h n -> p (h n)"),
                                                            scalar1=sg[:, 0:1], scalar2=None, op0=ALU.mult), reads=[b_w, b_sg], writes=[b_w])
            gTs = [self.sb(st, f"gTs{i}", [128, 8, 512], BF16) for i in range(2)]
            b_gTs = [Buf(f"gTs{i}") for i in range(2)]
            ypT = [self.sb(st, f"ypT{i}", [128, 8, 512], BF16) for i in range(2)]
            b_ypT = [Buf(f"ypT{i}") for i in range(2)]
            if not kindB:
                yTs = [self.sb(st, f"yTs{i}", [128, 8, 512], BF16) for i in range(2)]
                b_yTs = [Buf(f"yTs{i}") for i in range(2)]
            else:
                Og = [[self.sb(st, f"Og{g}{i}", [128, 16 * 65], F32) for i in range(3)] for g in range(3)]
                b_Og = [[Buf(f"Og{g}{i}") for i in range(3)] for g in range(3)]
                rlb = self.sb(st, "rlb", [128, 16], F32)
                b_rlb = Buf("rlb")
                obf = [self.sb(st, f"obf{i}", [128, D], BF16) for i in range(2)]
                b_obf = [Buf(f"obf{i}") for i in range(2)]
            xin = [self.sb(st, f"xin{i}", [128, D], F32) for i in range(3)]
            b_xin = [Buf(f"xin{i}") for i in range(3)]
            z = [self.sb(st, f"z{i}", [128, D], F32) for i in range(2)]
            b_z = [Buf(f"z{i}") for i in range(2)]
            xo = [self.sb(st, f"xo{i}", [128, D], F32) for i in range(2)]
            b_xo = [Buf(f"xo{i}") for i in range(2)]
            stats = self.sb(st, "stats", [128, 16], F32)
            b_stats = Buf("stats")
            mv = self.sb(st, "mv", [128, 8], F32)
            b_mv = Buf("mv")
            xbf = [self.sb(st, f"xbf{i}", [128, D], BF16) for i in range(2)]
            b_xbf = [Buf(f"xbf{i}") for i in range(2)]
            xTst = [self.sb(st, f"xTst{i}", [128, 8, 512], BF16) for i in range(2)]
            b_xTst = [Buf(f"xTst{i}") for i in range(2)]
            b_F = [Buf("F0"), Buf("F1")]
            tp = self.tp_views([4, 5])
            b_tp = [Buf("tp0"), Buf("tp1")]
            tpy = self.tp_views([6, 7])
            b_tpy = [Buf("tpy0"), Buf("tpy1")]
            gTv = self.gT.rearrange("(c p) t -> p c t", p=128)
            pend_x = []
            def issue_sb(sbi):
                s = sbi % 2
                sl = slice(sbi * 512, (sbi + 1) * 512)
                trk.dma(SP, gTs[s][:], gTv[:, :, sl], writes=[b_gTs[s]], owner=b_gTs[s])
                if not kindB:
                    trk.dma(SP, yTs[s][:], self.yT.rearrange("h p t -> p h t")[:, :, sl], writes=[b_yTs[s]], owner=b_yTs[s])

            def issue_tb(tb):
                k = tb % 3
                trk.dma(SP, xin[k][:], x_src[tb * 128:(tb + 1) * 128, :], writes=[b_xin[k]], owner=b_xin[k])
                if kindB:
                    for g, (win, dil) in enumerate(GROUPS):
                        src = self.OB[g].rearrange("(r m) e -> m r e", r=dil)[tb * 128 // dil: tb * 128 // dil + 128 // dil, :, :]
                        trk.dma(SP, Og[g][k][:], src, writes=[b_Og[g][k]], owner=b_Og[g][k])

            self.xT_store_q = ACT
            sbs = list(CFG['p3_sbs'])
            tbs = [(si, sbi, tb4) for si, sbi in enumerate(sbs) for tb4 in range(4)]
            ntb = len(tbs)

            def front(i):
                si, sbi, tb4 = tbs[i]
                s = sbi % 2
                tb = sbi * 4 + tb4
                k = tb % 2
                if tb4 == 0:
                    if si + 1 < len(sbs):
                        issue_sb(sbs[si + 1])
                    if not kindB:
                        trk.op(POOL, lambda: nc.gpsimd.tensor_tensor(out=ypT[s][:], in0=yTs[s][:], in1=gTs[s][:], op=ALU.mult),
                               reads=[b_yTs[s], b_gTs[s]], writes=[b_ypT[s]])
                if kindB:
                    k3 = tb % 3
                    trk.op(POOL, lambda: nc.gpsimd.tensor_tensor(out=Og[0][k3][:], in0=Og[0][k3][:], in1=Og[1][k3][:], op=ALU.add),
                           reads=[b_Og[0][k3], b_Og[1][k3]], writes=[b_Og[0][k3]])
                    trk.op(POOL, lambda: nc.gpsimd.tensor_tensor(out=Og[0][k3][:], in0=Og[0][k3][:], in1=Og[2][k3][:], op=ALU.add),
                           reads=[b_Og[0][k3], b_Og[2][k3]], writes=[b_Og[0][k3]])
                    Uv = Og[0][k3][:].rearrange("p (h e) -> p h e", e=65)
                    trk.op(DVE, lambda: nc.vector.reciprocal(out=rlb[:], in_=Uv[:, :, 64]), reads=[b_Og[0][k3]], writes=[b_rlb])
                    rl3 = bass.AP(rlb[:].tensor, rlb[:].offset, [list(rlb[:].ap[0]), [1, 16], [0, 64]])
                    trk.op(DVE, lambda: nc.vector.tensor_tensor(out=obf[k][:].rearrange("p (h e) -> p h e", e=64), in0=Uv[:, :, 0:64], in1=rl3, op=ALU.mult),
                           reads=[b_Og[0][k3], b_rlb], writes=[b_obf[k]])
                    for c in range(8):
                        trk.op(PE, lambda c=c: nc.tensor.transpose(out=tpy[k][:, c * 128:(c + 1) * 128], in_=obf[k][:, c * 128:(c + 1) * 128], identity=self.ident[:]),
                               reads=[b_obf[k], self.b_const], writes=[b_tpy[k]], inc=(c == 7))
                    trk.op(DVE, lambda: nc.vector.tensor_tensor(out=ypT[s][:, :, tb4 * 128:(tb4 + 1) * 128], in0=tpy[k].rearrange("p (c t) -> p c t", c=8),
                                                                in1=gTs[s][:, :, tb4 * 128:(tb4 + 1) * 128], op=ALU.mult),
                           reads=[b_tpy[k], b_gTs[s]], writes=[b_ypT[s]])
                f = k
                for n in range(2):
                    for h in range(8):
                        trk.op(PE, lambda n=n, h=h: nc.tensor.matmul(self.ps[2 * f + n][:], lhsT=ypT[s][:, h, tb4 * 128:(tb4 + 1) * 128],
                                                                     rhs=wob[:, h, n * 512:(n + 1) * 512], start=(h == 0), stop=(h == 7)),
                               reads=[b_ypT[s], b_w], writes=[b_F[f]], inc=(n == 1 and h == 7))

            def back(i):
                si, sbi, tb4 = tbs[i]
                tb = sbi * 4 + tb4
                k = tb % 2
                f = k
                Fap = self.ps_all_ap(2 * f, 2)
                k3 = tb % 3
                trk.op(DVE, lambda: nc.vector.scalar_tensor_tensor(out=z[k][:], in0=xin[k3][:], scalar=ALPHA, in1=Fap, op0=ALU.mult, op1=ALU.add),
                       reads=[b_xin[k3], b_F[f]], writes=[b_z[k]])
                for n in range(2):
                    trk.op(DVE, lambda n=n: nc.vector.bn_stats(out=stats[:, n * 6:(n + 1) * 6], in_=z[k][:, n * 512:(n + 1) * 512]),
                           reads=[b_z[k]], writes=[b_stats])
                trk.op(DVE, lambda: nc.vector.bn_aggr(out=mv[:, 0:2], in_=stats[:, 0:12]), reads=[b_stats], writes=[b_mv])
                trk.op(ACT, lambda: nc.scalar.activation(out=mv[:, 2:3], in_=mv[:, 1:2], func=AF.Ln, bias=epsb[:, 0:1]), reads=[b_mv, b_eps], writes=[b_mv])
                trk.op(ACT, lambda: nc.scalar.activation(out=mv[:, 3:4], in_=mv[:, 2:3], func=AF.Exp, scale=-0.5), reads=[b_mv], writes=[b_mv])
                trk.op(DVE, lambda: nc.vector.tensor_scalar(out=z[k][:], in0=z[k][:], scalar1=mv[:, 0:1], scalar2=mv[:, 3:4], op0=ALU.subtract, op1=ALU.mult),
                       reads=[b_z[k], b_mv], writes=[b_z[k]])
                trk.op(DVE, lambda: nc.vector.tensor_tensor(out=z[k][:], in0=z[k][:], in1=gb[:], op=ALU.mult), reads=[b_z[k], b_gb], writes=[b_z[k]])
                trk.op(POOL, lambda: nc.gpsimd.tensor_tensor(out=xo[k][:], in0=z[k][:], in1=bb[:], op=ALU.add), reads=[b_z[k], b_bb], writes=[b_xo[k]])
                while pend_x:
                    pend_x.pop(0)()
                trk.dma(ACT, x_dst[tb * 128:(tb + 1) * 128, :], xo[k][:], reads=[b_xo[k]], owner=b_xo[k])
                if not last:
                    pend_x.append(lambda tb=tb, k=k: self.emit_xT_block(tb, xo[k][:], b_xo[k], xbf, b_xbf, tp, b_tp, xTst, b_xTst, self.ACT))

            issue_sb(sbs[0])
            issue_tb(tbs[0][1] * 4 + tbs[0][2])
            if ntb > 1:
                issue_tb(tbs[1][1] * 4 + tbs[1][2])
            front(0)
            for i in range(ntb):
                if i + 2 < ntb:
                    issue_tb(tbs[i + 2][1] * 4 + tbs[i + 2][2])
                if i + 1 < ntb:
                    front(i + 1)
                back(i)
            while pend_x:
                pend_x.pop(0)()
            self.xT_store_q = None
            trk.barrier()
            rel = self._wstg + [b_gb, b_bb] + b_gTs + b_xin + b_xo + b_xTst
            if kindB:
                rel += [b for g in b_Og for b in g]
            else:
                rel += b_yTs + [b_sg]
            trk.release(rel)

    def phase_p1b(self, l):
        trk, nc = self.trk, self.nc
        PE, ACT, DVE, POOL, SP = self.PE, self.ACT, self.DVE, self.POOL, self.SP
        with ExitStack() as st:
            wbf = self.sb(st, "wbfB", [128, 8, 3072], BF16)
            b_w = Buf("w")
            wg = self.sb(st, "wgB", [128, 8, 1024], BF16)
            b_wg = Buf("wg")
            rtab = self.sb(st, "rtabB", [128, NTB * 32], F32)
            b_rt = Buf("rt")
            xTw = [self.sb(st, f"xTw{i}", [128, 8, 2048], BF16) for i in range(2)]
            b_xTw = [Buf(f"xTw{i}") for i in range(2)]
            gst = self.sb(st, "gstB", [128, 8, 512], BF16)
            b_gst = Buf("gst")
            qst = self.sb(st, "qstB", [128, 8, 512], BF16)
            b_qst = Buf("qst")
            kst = self.sb(st, "kstB", [128, 8, 512], BF16)
            b_kst = Buf("kst")
            vst = self.sb(st, "vstB", [128, 8, 4, 130], BF16)
            b_vst = Buf("vst")
            qkb = [self.sb(st, f"qkbB{i}", [128, 1024], BF16) for i in range(2)]
            b_qkb = [Buf(f"qkb{i}") for i in range(2)]
            tmpA = self.sb(st, "rtAB", [128, 256], F32)
            tmpB = self.sb(st, "rtBB", [128, 256], F32)
            b_tmp = [Buf("rtA"), Buf("rtB")]
            trk.op(POOL, lambda: nc.gpsimd.memset(vst[:], 1.0), writes=[b_vst])
            Ub = [(0, 1), (2, 3)]
            b_U = [Buf("U0"), Buf("U1")]
            b_G = [Buf("G0"), Buf("G1")]
            tp = self.tp_views([6, 7])
            b_tp = [Buf("tp0"), Buf("tp1")]
            ucnt = 0
            qkcnt = 0
            gcnt = 0
            wcnt = 0
            pend_tp = []
            self.load_w_bf16(wg, self.w_in[l], 9216, 1024, b_wg, st, tag="g")
            wst_all = list(self._wstg)
            xTv = self.xT.rearrange("(c p) t -> p c t", p=128)
            gTv = self.gT.rearrange("(c p) t -> p c t", p=128)
            windows = range(1) if SMALL else range(4)
            for g, (win, dil) in enumerate(GROUPS):
                L = T_TOK // dil
                nkt = L // 128
                self.load_w_bf16(wbf, self.w_in[l], g * 3072, 3072, b_w, st, tag=f"w{g}")
                wst_all += list(self._wstg)
                trk.dma(SP, rtab[:], self.rope_d[g], writes=[b_rt], owner=b_rt)
                for w in windows:
                    xw = wcnt % 2
                    wcnt += 1
                    if wcnt == 1:
                        trk.dma(SP, xTw[xw][:], xTv[:, :, w * 2048:(w + 1) * 2048], writes=[b_xTw[xw]], owner=b_xTw[xw])
                    wl = list(windows)
                    nxt_w = wl[wl.index(w) + 1] if wl.index(w) + 1 < len(wl) else (wl[0] if g < 2 else None)
                    if nxt_w is not None:
                        nx = wcnt % 2
                        trk.dma(SP, xTw[nx][:], xTv[:, :, nxt_w * 2048:(nxt_w + 1) * 2048], writes=[b_xTw[nx]], owner=b_xTw[nx])
                    if g == 0:
                        for sb4 in range(4):
                            sbi = w * 4 + sb4
                            for cb in range(8):
                                gi = gcnt % 2
                                gcnt += 1
                                G = self.ps[4 + gi]
                                for c in range(8):
                                    trk.op(PE, lambda c=c, cb=cb, G=G: nc.tensor.matmul(G, lhsT=wg[:, c, cb * 128:(cb + 1) * 128],
                                                                                        rhs=xTw[xw][:, c, sb4 * 512:(sb4 + 1) * 512], start=(c == 0), stop=(c == 7)),
                                           reads=[b_wg, b_xTw[xw]], writes=[b_G[gi]], inc=(c == 7))
                                trk.op(ACT, lambda cb=cb, G=G: nc.scalar.activation(out=gst[:, cb, :], in_=G, func=AF.Silu),
                                       reads=[b_G[gi]], writes=[b_gst])
                            trk.dma(SP, gTv[:, :, sbi * 512:(sbi + 1) * 512], gst[:], reads=[b_gst], owner=b_gst)
                    nj = 16 // dil
                    for r in range(dil):
                        for j in range(nj):
                            rb = r * nkt + (w * 2048 // dil) // 128 + j
                            slot = j % 4
                            run_end = (slot == 3) or (j == nj - 1)
                            tok0 = r + dil * 128 * j
                            for ui in range(3):
                                u = ucnt % 2
                                ucnt += 1
                                b0, b1 = Ub[u]
                                for n in range(2):
                                    for c in range(8):
                                        lw = xTw[xw][:, c, tok0:tok0 + (127 * dil + 1):dil] if dil > 1 else xTw[xw][:, c, tok0:tok0 + 128]
                                        trk.op(PE, lambda n=n, c=c, lw=lw, b0=b0, b1=b1, ui=ui: nc.tensor.matmul(
                                            self.ps[(b0, b1)[n]], lhsT=lw,
                                            rhs=wbf[:, c, ui * 1024 + n * 512: ui * 1024 + (n + 1) * 512], start=(c == 0), stop=(c == 7)),
                                            reads=[b_w, b_xTw[xw]], writes=[b_U[u]], inc=(n == 1 and c == 7))
                                Uap = self.pair_ap(b0)
                                while pend_tp:
                                    pend_tp.pop(0)()
                                if ui < 2:
                                    k = qkcnt % 2
                                    qkcnt += 1
                                    trk.op(ACT, lambda k=k, Uap=Uap: nc.scalar.copy(out=qkb[k][:], in_=Uap), writes=[b_U[u], b_qkb[k]])
                                    self.emit_rope(Uap, b_U[u], qkb[k][:], b_qkb[k], rtab, b_rt, rb, tmpA, tmpB, b_tmp)
                                    def _tp(k=k, ui=ui, slot=slot):
                                        tpv = tp[k]
                                        for c in range(8):
                                            trk.op(PE, lambda c=c: nc.tensor.transpose(out=tpv[:, c * 128:(c + 1) * 128], in_=qkb[k][:, c * 128:(c + 1) * 128], identity=self.ident[:]),
                                                   reads=[b_qkb[k], self.b_const], writes=[b_tp[k]], inc=(c == 7))
                                        stg, b_stg = (qst, b_qst) if ui == 0 else (kst, b_kst)
                                        trk.op(DVE, lambda: nc.vector.tensor_copy(out=stg[:, :, slot * 128:(slot + 1) * 128],
                                                                                  in_=tpv.rearrange("p (c t) -> p c t", c=8)),
                                               reads=[b_tp[k]], writes=[b_stg])
                                    pend_tp.append(_tp)
                                else:
                                    vv = vst[:, :, slot, :].rearrange("p h (a e) -> p h a e", e=65)[:, :, :, 0:64]
                                    trk.op(ACT, lambda vv=vv, Uap=Uap: nc.scalar.copy(out=vv, in_=Uap.rearrange("p (h a e) -> p h a e", a=2, e=64)),
                                           reads=[b_U[u]], writes=[b_vst])
                            if run_end:
                                while pend_tp:
                                    pend_tp.pop(0)()
                                nb = slot + 1
                                rb0 = rb - slot
                                trk.dma(SP, self.qT[g].rearrange("h p t -> p h t")[:, :, rb0 * 128:(rb0 + nb) * 128], qst[:, :, 0:nb * 128], reads=[b_qst], owner=b_qst)
                                trk.dma(SP, self.kT[g].rearrange("h p t -> p h t")[:, :, rb0 * 128:(rb0 + nb) * 128], kst[:, :, 0:nb * 128], reads=[b_kst], owner=b_kst)
                                trk.dma(SP, self.vaB[g].rearrange("h p (b x) -> p h b x", x=130)[:, :, rb0:rb0 + nb, :], vst[:, :, 0:nb, :], reads=[b_vst], owner=b_vst)
            trk.barrier()
            trk.release(wst_all + [b_rt, b_gst, b_qst, b_kst, b_vst] + b_xTw)

    def phase_p2b(self, l):
        trk, nc = self.trk, self.nc
        PE, ACT, DVE, POOL, SP = self.PE, self.ACT, self.DVE, self.POOL, self.SP
        with ExitStack() as st:
            kTh = [self.sb(st, f"kThB{i}", [128, T_TOK], BF16) for i in range(3)]
            qTh = [self.sb(st, f"qThB{i}", [128, T_TOK], BF16) for i in range(3)]
            vah = [self.sb(st, f"vahB{i}", [128, NTB, 130], BF16) for i in range(3)]
            b_kTh = [Buf(f"kTh{i}") for i in range(3)]
            b_qTh = [Buf(f"qTh{i}") for i in range(3)]
            b_vah = [Buf(f"vah{i}") for i in range(3)]
            pt = [self.sb(st, f"ptB{i}", [128, 2, 256], BF16) for i in range(3)]
            b_pt = [Buf(f"pt{i}") for i in range(3)]
            ost = [self.sb(st, f"ostB{i}", [128, 130], F32) for i in range(4)]
            b_ost = [Buf(f"ost{i}") for i in range(4)]
            b_S = [Buf("S0"), Buf("S1")]
            b_O = [Buf("O0"), Buf("O1"), Buf("O2")]
            O_b = [4, 5, 6]
            tcount = [0]
            ocount = [0]
            lcount = [0]
            TL = 2048 if SMALL else T_TOK

            def load(g, hp):
                s = lcount[0] % 3
                lcount[0] += 1
                win, dil = GROUPS[g]
                Lf = T_TOK // dil
                Ll = TL // dil
                pieces = [(r * Lf, r * Lf + Ll) for r in range(dil)] if SMALL else [(0, T_TOK)]
                for (a, b) in pieces:
                    trk.dma(SP, kTh[s][:, a:b], self.kT[g, hp][:, a:b], writes=[b_kTh[s]], owner=b_kTh[s])
                    trk.dma(SP, qTh[s][:, a:b], self.qT[g, hp][:, a:b], writes=[b_qTh[s]], owner=b_qTh[s])
                    trk.dma(SP, vah[s][:, a // 128:b // 128, :], self.vaB[g, hp].rearrange("p (b x) -> p b x", x=130)[:, a // 128:b // 128, :],
                            writes=[b_vah[s]], owner=b_vah[s])
                return s

            def tile_geom(g, r, j):
                win, dil = GROUPS[g]
                Lf = T_TOK // dil
                nktf = Lf // 128
                nkt = (TL // dil) // 128
                c0 = 0 if j > 0 else 64
                c1 = 256 if j < nkt - 1 else 192
                Lh = Lf // 2
                var = 0
                if j == Lh // 128 - 1:
                    var = 1
                elif j == Lh // 128:
                    var = 2
                return Lf, nktf, nkt, c0, c1, var

            def qk(s, g, r, j):
                L, nktf, nkt, c0, c1, var = tile_geom(g, r, j)
                t = tcount[0]
                tcount[0] += 1
                sl = t % 2
                kt = r * nktf + j
                q0 = r * L + 128 * j - 64 + c0
                nco = c1 - c0
                for h2 in range(2):
                    Sb = self.ps[2 * sl + h2][:, 0:nco]
                    trk.op(PE, lambda h2=h2, Sb=Sb: nc.tensor.matmul(Sb, lhsT=kTh[s][h2 * 64:(h2 + 1) * 64, kt * 128:(kt + 1) * 128],
                                                                    rhs=qTh[s][h2 * 64:(h2 + 1) * 64, q0:q0 + nco], start=True, stop=True),
                           reads=[b_kTh[s], b_qTh[s]], writes=[b_S[sl]], inc=(h2 == 1))
                p3 = t % 3
                Sv = self.psall[:, 2 * sl * 512:(2 * sl + 2) * 512].rearrange("p (b n) -> p b n", b=2)[:, :, 0:nco]
                trk.op(ACT, lambda: nc.scalar.activation(out=pt[p3][:, :, 0:nco], in_=Sv, func=AF.Exp, scale=0.125),
                       reads=[b_S[sl]], writes=[b_pt[p3]])
                mk = self.mb[:, var * 256 + c0:var * 256 + c1]
                mk3 = bass.AP(mk.tensor, mk.offset, [list(mk.ap[0]), [0, 2], [1, nco]])
                trk.op(DVE, lambda: nc.vector.tensor_tensor(out=pt[p3][:, :, 0:nco], in0=pt[p3][:, :, 0:nco], in1=mk3, op=ALU.mult),
                       reads=[self.b_const], writes=[b_pt[p3]])
                return p3

            def av(s, g, hp, r, j, p3):
                L, nktf, nkt, c0, c1, var = tile_geom(g, r, j)
                kt = r * nktf + j
                parts = []
                parts.append((j - 1, c0, 128, j == 0, True))
                parts.append((j, 128, c1, True, j == nkt - 1))
                for (ch, a0, a1, first, lastc) in parts:
                    rows = a1 - a0
                    ob = (ch + 1) % 3
                    for h2 in range(2):
                        trk.op(PE, lambda h2=h2, a0=a0, a1=a1, rows=rows, ob=ob, first=first: nc.tensor.matmul(
                            self.ps[O_b[ob]][0:rows, h2 * 65:(h2 + 1) * 65], lhsT=pt[p3][:, h2, a0 - c0:a1 - c0],
                            rhs=vah[s][:, kt, h2 * 65:(h2 + 1) * 65], start=(first and h2 == 0), stop=lastc, skip_group_check=True),
                            reads=[b_pt[p3], b_vah[s]], writes=[b_O[ob]], inc=(h2 == 1))
                    if lastc:
                        o = ocount[0] % 4
                        ocount[0] += 1
                        trk.op(DVE, lambda rows=rows, ob=ob, o=o: nc.vector.tensor_copy(out=ost[o][0:rows, :], in_=self.ps[O_b[ob]][0:rows, 0:130]),
                               reads=[b_O[ob]], writes=[b_ost[o]])
                        ridx0 = r * L + (128 * ch + 64 if ch >= 0 else 0)
                        if ch == nkt - 1:
                            ridx0 = r * L + nkt * 128 - 64
                        trk.dma(SP, self.OB[g][ridx0:ridx0 + rows, hp * 130:(hp + 1) * 130], ost[o][0:rows, :], reads=[b_ost[o]], owner=b_ost[o])

            todo = [(g, hp) for g in range(3) for hp in range(8)]
            steps = []
            for ti, (g, hp) in enumerate(todo):
                win, dil = GROUPS[g]
                nkt = (TL // dil) // 128
                for r in range(dil):
                    for j in range(nkt):
                        steps.append((ti, g, hp, r, j))
            n = len(steps)
            bufs = {}
            for ti0 in range(min(3, len(todo))):
                bufs[ti0] = load(*todo[ti0])
            p3s = {}

            def do_qk(i):
                ti, g, hp, r, j = steps[i]
                p3s[i] = qk(bufs[ti], g, r, j)

            def do_av(i):
                ti, g, hp, r, j = steps[i]
                av(bufs[ti], g, hp, r, j, p3s.pop(i))
                if (i + 1 == n or steps[i + 1][0] != ti) and ti + 3 < len(todo):
                    bufs[ti + 3] = load(*todo[ti + 3])

            do_qk(0)
            for i in range(n):
                if i + 1 < n:
                    do_qk(i + 1)
                if i >= 1:
                    do_av(i - 1)
            do_av(n - 1)
            trk.barrier()
            trk.release(b_kTh + b_qTh + b_vah + b_ost)

    def ps_all_ap(self, b0, n):
        return self.psall[:, b0 * 512:(b0 + n) * 512]

    def dump(self, src_ap):
        b = Buf("dump")
        yv = self.y_out.rearrange("(a b) c -> a (b c)", b=8)
        for i in range(8):
            for j in range(4):
                self.trk.dma(self.POOL, yv[i * 128:(i + 1) * 128, j * 2048:(j + 1) * 2048], src_ap[i * 128:(i + 1) * 128, j * 2048:(j + 1) * 2048], owner=b)
        self.trk.barrier()

    def build(self):
        import os
        dbg = os.environ.get("KDEBUG", "")
        self.phase_p0()
        if dbg == "p0":
            self.es.close()
            return self.nc
        if dbg == "p1a":
            self.phase_p1a(0)
            self.es.close()
            return self.nc
        if dbg == "p2a":
            self.phase_p1a(0)
            self.phase_p2a(0)
            if _os.environ.get("KTWICE"):
                self.phase_p2a(0)
            self.es.close()
            return self.nc
        if dbg == "b":
            sub = _os.environ.get("KBSUB", "123")
            if "1" in sub:
                self.phase_p1b(1)
            if "2" in sub:
                self.phase_p2b(1)
            if "3" in sub:
                self.phase_p3(1, self.x_in, self.y_out, kindB=True)
            self.es.close()
            return self.nc
        if dbg == "p3a":
            self.phase_p1a(0)
            self.phase_p2a(0)
            self.phase_p3(0, self.x_in, self.y_out, kindB=False)
            self.es.close()
            return self.nc
        cur = self.x_in
        for l in range(self.NL):
            dst = self.y_out if l == self.NL - 1 else self.xs[l % 2]
            if l % 2 == 0:
                self.phase_p1a(l)
                self.phase_p2a(l)
                self.phase_p3(l, cur, dst, kindB=False)
            else:
                self.phase_p1b(l)
                self.phase_p2b(l)
                self.phase_p3(l, cur, dst, kindB=True)
            cur = dst
        self.es.close()
        return self.nc


def rope_table(pos):
    half = 8
    inv = (np.float32(THETA) ** (-np.arange(half, dtype=np.float32) / np.float32(half))).astype(np.float32)
    ang = pos.astype(np.float32)[:, None] * inv[None, :]
    c = np.cos(ang).astype(np.float32)
    s = np.sin(ang).astype(np.float32)
    return np.concatenate([c, c, -s, s], axis=1).astype(np.float32)


def host_consts(is_sample):
    t = np.arange(T_TOK)
    pos = (t % 4096) if is_sample else t
    ropes = []
    for (win, dil) in GROUPS:
        L = T_TOK // dil
        ridx = np.arange(T_TOK)
        r, m = ridx // L, ridx % L
        tok = m * dil + r
        tab = rope_table(pos[tok])
        ropes.append(tab.reshape(NTB, 128, 32).transpose(1, 0, 2).reshape(128, NTB * 32))
    rope = np.stack(ropes).astype(np.float32)
    cm = np.full((128, 1), 0.0 if is_sample else 1.0, np.float32)
    kk = np.arange(128)[:, None]
    cc = np.arange(256)[None, :]
    band = (cc >= kk) & (cc <= kk + 128)
    base = np.where(band, 1.0, 0.0).astype(np.float32)
    hi = base.copy()
    lo = base.copy()
    if is_sample:
        hi[:, 192:] = 0.0
        lo[:, :64] = 0.0
    mb = np.concatenate([base, hi, lo], axis=1).astype(ml_dtypes.bfloat16)
    ident = np.eye(128, dtype=np.float32).astype(ml_dtypes.bfloat16)
    return {"rope": rope, "cm": cm, "mb": mb, "ident": ident}


_PROG_CACHE = {}


def make_in_maps(inputs):
    xp = np.ascontiguousarray(inputs["x_prompt"], dtype=np.float32)
    xs = np.ascontiguousarray(inputs["x_sample"], dtype=np.float32)
    shared = {}
    for i in range(DEPTH):
        shared[f"w_in_{i}"] = np.ascontiguousarray(inputs[f"w_in_{i}"], dtype=np.float32)
        shared[f"w_out_{i}"] = np.ascontiguousarray(inputs[f"w_out_{i}"], dtype=np.float32)
        shared[f"ln_g_{i}"] = np.ascontiguousarray(inputs[f"ln_g_{i}"], dtype=np.float32)
        shared[f"ln_b_{i}"] = np.ascontiguousarray(inputs[f"ln_b_{i}"], dtype=np.float32)
        if i % 2 == 0:
            shared[f"lamv_{i}"] = np.stack([inputs[f"lam_q1_{i}"], inputs[f"lam_k1_{i}"], inputs[f"lam_q2_{i}"], inputs[f"lam_k2_{i}"]]).astype(np.float32)
            shared[f"subln_g_{i}"] = np.ascontiguousarray(inputs[f"subln_g_{i}"], dtype=np.float32)
    cp = host_consts(False)
    cs = host_consts(True)
    in_maps = []
    for c in range(8):
        m = dict(shared)
        if c < 4:
            m["x"] = xp[c]
            m.update(cp)
        else:
            j = c - 4
            m["x"] = np.ascontiguousarray(xs[2 * j:2 * j + 2].reshape(T_TOK, D))
            m.update(cs)
        in_maps.append(m)
    return in_maps


def filter_maps(in_maps):
    if _os.environ.get("KDEBUG"):
        keep = lambda k: not (len(k) > 2 and k[-2] == "_" and k[-1].isdigit() and int(k[-1]) >= {"b": 2}.get(_os.environ["KDEBUG"], 1))
        in_maps = [{k: v for k, v in m.items() if keep(k)} for m in in_maps]
    return in_maps


def run(inputs, NL=4):
    if NL not in _PROG_CACHE:
        _PROG_CACHE[NL] = Prog(NL).build()
    nc = _PROG_CACHE[NL]
    in_maps = filter_maps(make_in_maps(inputs))
    res = run_bass_kernel_spmd(nc, in_maps, core_ids=list(range(8)))
    outs = [r["y"] for r in res.results]
    y_prompt = np.stack(outs[:4]).astype(np.float32)
    y_sample = np.concatenate([o.reshape(2, 4096, D) for o in outs[4:]], axis=0).astype(np.float32)
    return y_prompt, y_sample


def kernel(**inputs):
    return run(inputs, NL=4)
```
